# Optimizing a Trainium2 kernel written in Bass

```python
import math
import jax, jax.numpy as jnp
from jax import lax
import numpy as np

D_MODEL = 1024
BATCH = 8
SEQ = 8192
DEPTH = 4

HEAD_DIM = 64
BRANCH_W = D_MODEL // 4
N_BRANCH = 4
N_HEADS = BRANCH_W // HEAD_DIM
SC_K = 3
SB_BLOCK = 128
SSM_GROUPS = 2
SSM_STATE = 64
SSM_CONV_K = 4
SSM_CHUNK = 256
SSM_CONV_DIM = BRANCH_W + 2 * SSM_GROUPS * SSM_STATE
MOBA_BLOCK = 256
MOBA_TOPK = 3
MOBA_QCHUNK = 128
SEQ_MULTIPLE = 256
FFN_HIDDEN = ((8 * D_MODEL + 3 * 256 - 1) // (3 * 256)) * 256
RMS_EPS = 1e-6

W_A_IN = 3 * BRANCH_W
W_B_IN = 3 * BRANCH_W
W_C_IN = BRANCH_W + SSM_CONV_DIM + N_HEADS
W_D_IN = 3 * BRANCH_W
W_G_IN = N_BRANCH * D_MODEL
IN_SPLITS = [W_A_IN, W_A_IN + W_B_IN, W_A_IN + W_B_IN + W_C_IN,
             W_A_IN + W_B_IN + W_C_IN + W_D_IN]
N_IN = W_A_IN + W_B_IN + W_C_IN + W_D_IN + W_G_IN

kernel_name = "hybrid_gated_conv_sb_ssd_moba_trunk"


def rmsnorm(x, g):
    xf = x.astype(jnp.float32)
    y = xf * lax.rsqrt(jnp.mean(xf * xf, axis=-1, keepdims=True) + RMS_EPS)
    return (y * g.astype(jnp.float32)).astype(x.dtype)


def causal_depthwise_conv(x, w):
    k = w.shape[0]
    return lax.conv_general_dilated(
        x, w[:, None, :].astype(x.dtype), window_strides=(1,), padding=[(k - 1, 0)],
        dimension_numbers=("NWC", "WIO", "NWC"), feature_group_count=x.shape[-1])


def short_conv_mixer(u, conv_w):
    xa, b_gate, c_gate = jnp.split(u, 3, axis=-1)
    return b_gate * causal_depthwise_conv(c_gate * xa, conv_w)


def stick_breaking_attention(q, k, v):
    bsz, seq, nh, dh = q.shape
    nqb = seq // SB_BLOCK
    f32 = jnp.float32
    qs = q * (dh ** -0.5)
    within_mat = jnp.tril(jnp.ones((SB_BLOCK, SB_BLOCK), f32), -1)
    outs = []
    for c in range(nqb):
        nkb = c + 1
        kc = nkb * SB_BLOCK
        qi = qs[:, c * SB_BLOCK:(c + 1) * SB_BLOCK]
        z = jnp.einsum("bqhd,bkhd->bhqk", qi, k[:, :kc], preferred_element_type=f32)
        q_pos = c * SB_BLOCK + jnp.arange(SB_BLOCK)
        past = jnp.arange(kc)[None, :] < q_pos[:, None]
        z = jnp.where(past, z, -jnp.inf)
        sp = jax.nn.softplus(z)
        lk = sp.reshape(bsz, nh, SB_BLOCK, nkb, SB_BLOCK)
        within = jnp.einsum("bhqnj,js->bhqns", lk, within_mat)
        after_mat = jnp.tril(jnp.ones((nkb, nkb), f32), -1)
        after = jnp.einsum("bhqm,mn->bhqn", jnp.sum(lk, axis=-1), after_mat)
        between = (within + after[..., None]).reshape(bsz, nh, SB_BLOCK, kc)
        w = jnp.exp(z - sp - between)
        outs.append(jnp.einsum("bhqk,bkhd->bqhd", w.astype(v.dtype), v[:, :kc]))
    return jnp.concatenate(outs, axis=1).reshape(bsz, seq, nh * dh)


def ssd_chunked(x, a, b_in, c_in):
    bsz, seq, nh, hp = x.shape
    ng, ns = b_in.shape[2], b_in.shape[3]
    ne = nh // ng
    t = SSM_CHUNK
    nc = seq // t
    f32 = jnp.float32
    x = x.astype(f32).reshape(bsz, nc, t, ng, ne, hp)
    a = a.astype(f32).reshape(bsz, nc, t, ng, ne)
    bc = b_in.astype(f32).reshape(bsz, nc, t, ng, ns)
    cc = c_in.astype(f32).reshape(bsz, nc, t, ng, ns)
    a_cs = jnp.cumsum(a, axis=2)
    causal = jnp.tril(jnp.ones((t, t), dtype=bool))[:, :, None, None]
    seg = a_cs[:, :, :, None] - a_cs[:, :, None, :]
    decay = jnp.exp(jnp.where(causal, seg, -jnp.inf))
    cb = jnp.einsum("bclgn,bcsgn->bclsg", cc, bc)
    y_diag = jnp.einsum("bclsge,bcsgep->bclgep", cb[..., None] * decay, x)
    decay_to_end = jnp.exp(a_cs[:, :, -1:] - a_cs)
    chunk_states = jnp.einsum("bclgn,bclge,bclgep->bcgepn", bc, decay_to_end, x)
    chunk_decay = jnp.exp(a_cs[:, :, -1])

    def step(h, inp):
        s_c, d_c = inp
        return h * d_c[..., None, None] + s_c, h

    h0 = jnp.zeros((bsz, ng, ne, hp, ns), f32)
    _, h_enter = lax.scan(step, h0, (jnp.moveaxis(chunk_states, 1, 0),
                                     jnp.moveaxis(chunk_decay, 1, 0)))
    h_enter = jnp.moveaxis(h_enter, 0, 1)
    y_off = jnp.einsum("bclgn,bcgepn,bclge->bclgep", cc, h_enter, jnp.exp(a_cs))
    return (y_diag + y_off).reshape(bsz, seq, nh, hp)


def mamba2_mixer(u, conv_w, conv_b, dt_bias, a_log, d_skip, norm_g):
    bsz, seq, _ = u.shape
    z, xbc, dt = jnp.split(u, [BRANCH_W, BRANCH_W + SSM_CONV_DIM], axis=-1)
    xbc = jax.nn.silu(causal_depthwise_conv(xbc, conv_w) + conv_b)
    xs, b_in, c_in = jnp.split(xbc, [BRANCH_W, BRANCH_W + SSM_GROUPS * SSM_STATE], axis=-1)
    dt = jax.nn.softplus(dt.astype(jnp.float32) + dt_bias.astype(jnp.float32))
    a = -jnp.exp(a_log.astype(jnp.float32))
    xh = xs.reshape(bsz, seq, N_HEADS, HEAD_DIM)
    y = ssd_chunked(xh.astype(jnp.float32) * dt[..., None], dt * a,
                    b_in.reshape(bsz, seq, SSM_GROUPS, SSM_STATE),
                    c_in.reshape(bsz, seq, SSM_GROUPS, SSM_STATE))
    y = y.astype(u.dtype) + xh * d_skip[:, None]
    gsz = BRANCH_W // SSM_GROUPS
    gated = y.reshape(bsz, seq, SSM_GROUPS, gsz) * jax.nn.silu(z.reshape(bsz, seq, SSM_GROUPS, gsz))
    return rmsnorm(gated, norm_g.reshape(SSM_GROUPS, gsz)).reshape(bsz, seq, BRANCH_W)


def moba_head(q, k, v):
    seq, dh = q.shape
    nb = seq // MOBA_BLOCK
    ke = max(1, min(MOBA_TOPK, nb - 1))
    cq = MOBA_QCHUNK
    n_pairs = seq * ke
    n_chunks = -(-(n_pairs + (nb + 1) * (cq - 1)) // cq)
    scale = dh ** -0.5
    f32 = jnp.float32
    kblk = k.reshape(nb, MOBA_BLOCK, dh)
    vblk = v.reshape(nb, MOBA_BLOCK, dh)
    k_mean = jnp.mean(kblk.astype(f32), axis=1)
    gate = q.astype(f32) @ k_mean.T
    own = jnp.arange(seq) // MOBA_BLOCK
    gate = jnp.where(jnp.arange(nb)[None, :] < own[:, None], gate, -jnp.inf)
    _, sel = lax.top_k(gate, ke)
    valid = jnp.arange(ke)[None, :] < own[:, None]
    grp = jnp.where(valid, sel, nb).reshape(n_pairs)
    cnt = jnp.zeros((nb + 1,), jnp.int32).at[grp].add(1)
    padded = (cnt + cq - 1) // cq * cq
    pad_end = jnp.cumsum(padded)
    pad_start = pad_end - padded
    order = jnp.argsort(grp, stable=True)
    g_sorted = grp[order]
    rank = jnp.arange(n_pairs, dtype=jnp.int32) - (jnp.cumsum(cnt) - cnt)[g_sorted]
    dest = jnp.zeros((n_pairs,), jnp.int32).at[order].set(pad_start[g_sorted] + rank)
    slot_pair = jnp.full((n_chunks * cq,), -1, jnp.int32).at[dest].set(
        jnp.arange(n_pairs, dtype=jnp.int32)).reshape(n_chunks, cq)
    chunk_grp = jnp.searchsorted(pad_end, jnp.arange(n_chunks, dtype=jnp.int32) * cq, side="right")
    slot_ok = (slot_pair >= 0) & (chunk_grp < nb)[:, None]
    blk = jnp.minimum(chunk_grp, nb - 1)
    q_c = q[jnp.maximum(slot_pair, 0) // ke]
    k_c = kblk[blk]
    v_c = vblk[blk]
    s = jnp.einsum("cqd,ckd->cqk", q_c, k_c, preferred_element_type=f32) * scale
    s = jnp.where(slot_ok[..., None], s, -jnp.inf)
    m = jnp.where(slot_ok, jnp.max(s, axis=-1), 0.0)
    p = jnp.exp(s - m[..., None])
    l = jnp.sum(p, axis=-1)
    o = jnp.einsum("cqk,ckd->cqd", p.astype(v.dtype), v_c, preferred_element_type=f32)
    m_p = m.reshape(-1)[dest].reshape(seq, ke)
    l_p = l.reshape(-1)[dest].reshape(seq, ke)
    o_p = o.reshape(-1, dh)[dest].reshape(seq, ke, dh)
    s_own = jnp.einsum("nqd,nkd->nqk", q.reshape(nb, MOBA_BLOCK, dh), kblk,
                       preferred_element_type=f32) * scale
    s_own = jnp.where(jnp.tril(jnp.ones((MOBA_BLOCK, MOBA_BLOCK), dtype=bool)), s_own, -jnp.inf)
    m_own = jnp.max(s_own, axis=-1)
    p_own = jnp.exp(s_own - m_own[..., None])
    l_own = jnp.sum(p_own, axis=-1).reshape(seq)
    o_own = jnp.einsum("nqk,nkd->nqd", p_own.astype(v.dtype), vblk,
                       preferred_element_type=f32).reshape(seq, dh)
    m_own = m_own.reshape(seq)
    m_all = jnp.maximum(m_own, jnp.max(jnp.where(valid, m_p, -jnp.inf), axis=-1))
    w_p = jnp.where(valid, jnp.exp(m_p - m_all[:, None]), 0.0)
    w_own = jnp.exp(m_own - m_all)
    num = w_own[:, None] * o_own + jnp.einsum("sr,srd->sd", w_p, o_p)
    den = w_own * l_own + jnp.sum(w_p * l_p, axis=-1)
    return (num / den[:, None]).astype(q.dtype)


def moba_attention(q, k, v):
    bsz, seq, nh, dh = q.shape
    qt, kt, vt = [jnp.moveaxis(t, 2, 1) for t in (q, k, v)]
    out = lax.map(lambda a: jax.vmap(moba_head)(*a), (qt, kt, vt))
    return jnp.moveaxis(out, 1, 2).reshape(bsz, seq, nh * dh)


def hybrid_layer(x, norm1_g, w_in, conv_a_w, ssm_conv_w, ssm_conv_b, ssm_dt_bias,
                 ssm_a_log, ssm_d, ssm_norm_g, w_branch, w_o, norm2_g, w_gate_up, w_down):
    bsz, seq, _ = x.shape
    h = rmsnorm(x, norm1_g)
    u = h @ w_in
    u_a, u_b, u_c, u_d, u_g = jnp.split(u, IN_SPLITS, axis=-1)
    y_a = short_conv_mixer(u_a, conv_a_w)
    q_b, k_b, v_b = [t.reshape(bsz, seq, N_HEADS, HEAD_DIM) for t in jnp.split(u_b, 3, axis=-1)]
    y_b = stick_breaking_attention(q_b, k_b, v_b)
    y_c = mamba2_mixer(u_c, ssm_conv_w, ssm_conv_b, ssm_dt_bias, ssm_a_log, ssm_d, ssm_norm_g)
    q_d, k_d, v_d = [t.reshape(bsz, seq, N_HEADS, HEAD_DIM) for t in jnp.split(u_d, 3, axis=-1)]
    y_d = moba_attention(q_d, k_d, v_d)
    gates = jax.nn.sigmoid(u_g.reshape(bsz, seq, N_BRANCH, D_MODEL))
    branches = [y_a, y_b, y_c, y_d]
    merged = gates[:, :, 0] * (branches[0] @ w_branch[0])
    for i in range(1, N_BRANCH):
        merged = merged + gates[:, :, i] * (branches[i] @ w_branch[i])
    x = x + merged @ w_o
    h2 = rmsnorm(x, norm2_g)
    g_ff, up_ff = jnp.split(h2 @ w_gate_up, 2, axis=-1)
    return x + (jax.nn.silu(g_ff) * up_ff) @ w_down


def setup_inputs(seed: int = 0) -> dict:
    key = jax.random.key(seed)
    ks = jax.random.split(key, 17)
    f32 = jnp.float32

    def nrm(k, shape, scale):
        return jax.random.normal(k, shape, f32) * scale

    dt0 = jnp.exp(jax.random.uniform(ks[6], (DEPTH, N_HEADS), f32, math.log(1e-3), math.log(1e-1)))
    return {
        "x": nrm(ks[0], (BATCH, SEQ, D_MODEL), 1.0),
        "norm1_g": 1.0 + nrm(ks[1], (DEPTH, D_MODEL), 0.02),
        "w_in": nrm(ks[2], (DEPTH, D_MODEL, N_IN), D_MODEL ** -0.5),
        "conv_a_w": nrm(ks[3], (DEPTH, SC_K, BRANCH_W), SC_K ** -0.5),
        "ssm_conv_w": nrm(ks[4], (DEPTH, SSM_CONV_K, SSM_CONV_DIM), SSM_CONV_K ** -0.5),
        "ssm_conv_b": nrm(ks[5], (DEPTH, SSM_CONV_DIM), 0.02),
        "ssm_dt_bias": dt0 + jnp.log(-jnp.expm1(-dt0)),
        "ssm_a_log": jnp.log(jax.random.uniform(ks[7], (DEPTH, N_HEADS), f32, 1.0, 16.0)),
        "ssm_d": 1.0 + nrm(ks[8], (DEPTH, N_HEADS), 0.1),
        "ssm_norm_g": 1.0 + nrm(ks[9], (DEPTH, BRANCH_W), 0.02),
        "w_branch": nrm(ks[10], (DEPTH, N_BRANCH, BRANCH_W, D_MODEL), BRANCH_W ** -0.5),
        "w_o": nrm(ks[11], (DEPTH, D_MODEL, D_MODEL), D_MODEL ** -0.5),
        "norm2_g": 1.0 + nrm(ks[12], (DEPTH, D_MODEL), 0.02),
        "w_gate_up": nrm(ks[13], (DEPTH, D_MODEL, 2 * FFN_HIDDEN), D_MODEL ** -0.5),
        "w_down": nrm(ks[14], (DEPTH, FFN_HIDDEN, D_MODEL), FFN_HIDDEN ** -0.5),
        "final_g": 1.0 + nrm(ks[15], (D_MODEL,), 0.02),
    }


def reference(x, norm1_g, w_in, conv_a_w, ssm_conv_w, ssm_conv_b, ssm_dt_bias, ssm_a_log,
              ssm_d, ssm_norm_g, w_branch, w_o, norm2_g, w_gate_up, w_down, final_g):
    seq = x.shape[1]
    seq_pad = -(-seq // SEQ_MULTIPLE) * SEQ_MULTIPLE
    h = jnp.pad(x, ((0, 0), (0, seq_pad - seq), (0, 0)))
    for l in range(DEPTH):
        h = hybrid_layer(h, norm1_g[l], w_in[l], conv_a_w[l], ssm_conv_w[l], ssm_conv_b[l],
                         ssm_dt_bias[l], ssm_a_log[l], ssm_d[l], ssm_norm_g[l], w_branch[l],
                         w_o[l], norm2_g[l], w_gate_up[l], w_down[l])
    return rmsnorm(h, final_g)[:, :seq]
```

```python
import numpy as np
import concourse.bass as bass
import concourse.mybir as mybir
from contextlib import ExitStack

F32 = mybir.dt.float32
BF16 = mybir.dt.bfloat16
AF = mybir.ActivationFunctionType
ALU = mybir.AluOpType
AX = mybir.AxisListType

ENGS = ("pe", "act", "dve", "pool", "sp")
NDSEM = 8


class Res:
    __slots__ = ("name", "w", "r")

    def __init__(self, name=""):
        self.name = name
        self.w = None
        self.r = {}


class T:
    def __init__(self, h, name=""):
        self.h = h
        self.res = Res(name)

    def __getitem__(self, idx):
        return self.h[idx]


class Op:
    __slots__ = ("eng", "fn", "deps", "needs_inc", "tok", "is_dma", "gi")


class SemPool:
    def __init__(self, nc):
        self.nc = nc
        self.es = ExitStack()
        self.sems = {}
        self.counts = {}
        self.prev = []

    def get(self, key):
        if key not in self.sems:
            self.sems[key] = self.es.enter_context(self.nc.semaphore("sem_" + key))
            self.counts[key] = 0
        return self.sems[key]


class Prog:
    _uid = [0]

    def __init__(self, nc, pool=None):
        self.nc = nc
        self.pool = pool if pool is not None else SemPool(nc)
        Prog._uid[0] += 1
        self.pfx = "p%d_" % Prog._uid[0]
        self.ops = []
        self.last = {e: None for e in ENGS}
        self.pending = {e: [] for e in ENGS}
        self.dma_ops = []
        self.es = ExitStack()
        self.sems = {}
        self.dsems = {}
        self.ndma = {e: 0 for e in ENGS}
        self.last_on_dsem = {}

    def sb(self, name, shape, dt, stack=None):
        h = (stack or self.es).enter_context(self.nc.sbuf_tensor(self.pfx + name, list(shape), dt))
        return T(h, name)

    def ps(self, name, shape, dt=F32, stack=None):
        h = (stack or self.es).enter_context(self.nc.psum_tensor(self.pfx + name, list(shape), dt))
        return T(h, name)

    def op(self, eng, fn, r=(), w=(), dma=False):
        o = Op()
        o.eng = eng
        o.fn = fn
        o.is_dma = dma
        o.needs_inc = dma
        o.tok = None
        o.gi = len(self.ops)
        deps = {}
        for t in r:
            res = t.res if isinstance(t, T) else t
            if res.w is not None:
                deps[id(res.w)] = res.w
        for t in w:
            res = t.res if isinstance(t, T) else t
            if res.w is not None:
                deps[id(res.w)] = res.w
            for d in res.r.values():
                deps[id(d)] = d
        for d in self.pending[eng]:
            deps[id(d)] = d
        self.pending[eng] = []
        if dma:
            k = (eng, self.ndma[eng] % NDSEM)
            self.ndma[eng] += 1
            prev = self.last_on_dsem.get(k)
            if prev is not None:
                deps[id(prev)] = prev
            self.last_on_dsem[k] = o
            o.tok = k
        dl = []
        for d in deps.values():
            if d is o:
                continue
            if eng == "pe" and d.eng == "pe" and not d.is_dma:
                continue
            d.needs_inc = True
            dl.append(d)
        o.deps = dl
        for t in r:
            res = t.res if isinstance(t, T) else t
            key = ("dma", o.gi) if dma else eng
            res.r[key] = o
        for t in w:
            res = t.res if isinstance(t, T) else t
            res.w = o
            res.r = {}
        self.ops.append(o)
        self.last[eng] = o
        if dma:
            self.dma_ops.append(o)
        return o

    def barrier(self):
        outs = [o for o in self.last.values() if o is not None]
        outs += list(self.last_on_dsem.values())
        for e in ENGS:
            self.pending[e] = list(outs)
        for o in outs:
            o.needs_inc = True

    def dma(self, out_ap, in_ap, r=(), w=(), eng="sp", **kw):
        return self.op(eng, lambda e: e.dma_start(out=out_ap, in_=in_ap, **kw), r=r, w=w, dma=True)

    def mm(self, out_ap, lhsT, rhs, start=True, stop=True, r=(), w=(), sgc=False):
        if sgc:
            return self.op("pe", lambda e: e.matmul(out_ap, lhsT, rhs, start=start, stop=stop, skip_group_check=True), r=r, w=w)
        return self.op("pe", lambda e: e.matmul(out_ap, lhsT, rhs, start=start, stop=stop), r=r, w=w)

    def tr(self, out_ap, in_ap, ident, r=(), w=()):
        return self.op("pe", lambda e: e.transpose(out_ap, in_ap, ident), r=r, w=w)

    def act(self, out_ap, in_ap, func, r=(), w=(), **kw):
        return self.op("act", lambda e: e.activation(out_ap, in_ap, func, **kw), r=r, w=w)


    def stt(self, out, in0, scalar, in1, op0, op1, r=(), w=(), accum_out=None):
        if accum_out is None:
            return self.op("dve", lambda e: e.scalar_tensor_tensor(out=out, in0=in0, scalar=scalar, in1=in1, op0=op0, op1=op1), r=r, w=w)
        return self.op("dve", lambda e: e.scalar_tensor_tensor(out=out, in0=in0, scalar=scalar, in1=in1, op0=op0, op1=op1, accum_out=accum_out), r=r, w=w)

    def ts(self, out, in0, s1, s2, op0, op1=None, r=(), w=(), eng="dve"):
        if op1 is None:
            return self.op(eng, lambda e: e.tensor_scalar(out=out, in0=in0, scalar1=s1, scalar2=None, op0=op0), r=r, w=w)
        return self.op(eng, lambda e: e.tensor_scalar(out=out, in0=in0, scalar1=s1, scalar2=s2, op0=op0, op1=op1), r=r, w=w)

    def tt(self, out, in0, in1, op, r=(), w=(), eng="dve"):
        return self.op(eng, lambda e: e.tensor_tensor(out=out, in0=in0, in1=in1, op=op), r=r, w=w)

    def copy(self, out, in_, r=(), w=(), eng="dve"):
        if eng == "act":
            return self.op("act", lambda e: e.copy(out=out, in_=in_), r=r, w=w)
        return self.op(eng, lambda e: e.tensor_copy(out=out, in_=in_), r=r, w=w)

    def memset(self, ap, val, w=(), eng="pool"):
        return self.op(eng, lambda e: e.memset(ap, val), w=w)

    def affine(self, t, pattern, base, cm, cmp, fill, val, eng="pool"):
        self.memset(t[:], val, w=[t])
        return self.op("pool", lambda e: e.affine_select(out=t[:], in_=t[:], pattern=pattern, compare_op=cmp, fill=fill, base=base, channel_multiplier=cm), r=[t], w=[t])

    def finalize(self):
        nc = self.nc
        es = self.es
        self.barrier()
        self.op("sp", None, r=(), w=())
        pool = self.pool
        for o in self.ops:
            if o.is_dma:
                key = "d_%s%d" % o.tok
                sem = pool.get(key)
                pool.counts[key] += 16
                o.tok = (sem, pool.counts[key], 16)
            elif o.needs_inc:
                key = "e_" + o.eng
                sem = pool.get(key)
                pool.counts[key] += 1
                o.tok = (sem, pool.counts[key], 1)
        streams = {e: [] for e in ENGS}
        seen = {e: {} for e in ENGS}
        first = {e: True for e in ENGS}
        for o in self.ops:
            waits = {}
            deptoks = [d.tok for d in o.deps]
            if first[o.eng]:
                first[o.eng] = False
                deptoks += [(sem, val, 0) for sem, val in pool.prev]
            for sem, val, _ in deptoks:
                sid = id(sem)
                if seen[o.eng].get(sid, 0) >= val:
                    continue
                if sid not in waits or waits[sid][1] < val:
                    waits[sid] = (sem, val)
            for sid, (sem, val) in waits.items():
                seen[o.eng][sid] = val
            streams[o.eng].append((list(waits.values()), o.fn, o.tok if (o.is_dma or o.needs_inc) else None))
        pool.prev = [(pool.sems[k], pool.counts[k]) for k in pool.sems if pool.counts[k] > 0]
        self.stats = {e: len(streams[e]) for e in ENGS}

        def run(stream):
            def f(e):
                for waits, fn, tok in stream:
                    for sem, val in waits:
                        e.wait_ge(sem, val)
                    if fn is None:
                        continue
                    ins = fn(e)
                    if tok is not None:
                        ins.then_inc(tok[0], tok[2])
            return f

        with nc.Block() as block:
            block.tensor(run(streams["pe"]))
            block.scalar(run(streams["act"]))
            block.vector(run(streams["dve"]))
            block.gpsimd(run(streams["pool"]))
            block.sync(run(streams["sp"]))
        es.close()


D = 1024
NIN = 7172
FF = 2816
BIG = 30000.0
EPS = 1e-6


class NS:
    pass


def rep(ap, pattern, **kw):
    return ap.rearrange(pattern, **kw)


def mk_ident(P, name="ident"):
    f = P.sb(name + "_f", [128, 128], F32)
    P.affine(f, [[-1, 128]], 0, 1, ALU.is_equal, 0.0, 1.0)
    b = P.sb(name, [128, 128], BF16)
    P.copy(b[:], f[:], r=[f], w=[b])
    return b, f


def norm_alloc(P, J):
    K = NS()
    K.J = J
    K.junk = P.sb("n_junk", [128, D], F32)
    K.ss = P.sb("n_ss", [128, 4], F32)
    K.ms = P.sb("n_ms", [128, 4], F32)
    K.rstd = P.sb("n_rstd", [128, 4], F32)
    K.mh = P.sb("n_mh", [128, 4], F32)
    P.memset(K.mh[:], -0.5, w=[K.mh])
    K.xn = P.sb("n_xn", [128, J, D], BF16)
    K.pT = [P.ps("n_pT%d" % i, [128, D], BF16) for i in range(2)]
    K.cnt = 0
    return K


def norm_stats(P, K, xt, J):
    for j in range(J):
        P.stt(K.junk[:], xt[:, j, :], 1.0, xt[:, j, :], ALU.mult, ALU.mult, r=[xt], w=[K.junk, K.ss],
              accum_out=K.ss[:, j:j + 1])
    P.ts(K.ms[:, 0:J], K.ss[:, 0:J], 1.0 / D, EPS, ALU.mult, ALU.add, r=[K.ss], w=[K.ms])
    P.tt(K.rstd[:, 0:J], K.ms[:, 0:J], K.mh[:, 0:J], ALU.pow, r=[K.ms, K.mh], w=[K.rstd], eng="pool")


def norm_p1(P, K, xt, gb=None):
    J = K.J
    norm_stats(P, K, xt, J)
    for j in range(J):
        if gb is None:
            P.act(K.xn[:, j, :], xt[:, j, :], AF.Copy, r=[xt, K.rstd], w=[K.xn], scale=K.rstd[:, j:j + 1])
        else:
            P.stt(K.xn[:, j, :], xt[:, j, :], K.rstd[:, j:j + 1], gb[:], ALU.mult, ALU.mult, r=[xt, K.rstd, gb], w=[K.xn])


def norm_p2(P, K, ident, hTt):
    J = K.J
    for j in range(J):
        pT = K.pT[K.cnt % 2]
        K.cnt += 1
        for kc in range(8):
            P.tr(pT[:, kc * 128:(kc + 1) * 128], K.xn[:, j, kc * 128:(kc + 1) * 128], ident[:], r=[K.xn, ident], w=[pT])
        P.copy(hTt[:, :, j * 128:(j + 1) * 128], rep(pT[:, :], "p (k t) -> p k t", k=8), r=[pT], w=[hTt])


def norm_tt(P, K, ident, xt, hTt, gb=None):
    norm_p1(P, K, xt, gb)
    norm_p2(P, K, ident, hTt)


def load_gain(P, name, g_ap):
    t = P.sb(name, [128, D], F32)
    P.dma(t[:], g_ap.partition_broadcast(128), w=[t])
    return t


def load_w(P, dst, dst_sl, src_ap):
    P.dma(dst_sl, src_ap, w=[dst], eng="pool")


def load_w_cast(P, dst, dst_sl, src_ap, ncols, stage, si, gcol=None, eng="dve", r_extra=()):
    st = stage[si % len(stage)]
    P.dma(st[:, 0:ncols], src_ap, w=[st])
    if gcol is None:
        P.copy(dst_sl, st[:, 0:ncols], r=[st], w=[dst], eng=eng)
    elif eng == "act":
        P.act(dst_sl, st[:, 0:ncols], AF.Copy, r=[st] + list(r_extra), w=[dst], scale=gcol)
    else:
        P.ts(dst_sl, st[:, 0:ncols], gcol, None, ALU.mult, r=[st] + list(r_extra), w=[dst], eng=eng)


def ph_norm0(nc, G):
    S = G.S
    P = Prog(nc, G.pool)
    ident, _ = mk_ident(P)
    J = 4
    K = norm_alloc(P, J)
    xts = [P.sb("xt%d" % i, [128, J, D], F32) for i in range(2)]
    hts = [P.sb("ht%d" % i, [128, 8, J * 128], BF16) for i in range(2)]
    hTv = rep(G.hT, "(k p) s -> p k s", p=128)
    gb = load_gain(P, "gb", G.norm1_g[0])
    for t in range(S // (128 * J)):
        xt = xts[t % 2]
        ht = hts[t % 2]
        t0 = t * 128 * J
        P.dma(xt[:], rep(G.x_in[t0:t0 + 128 * J, :], "(j p) d -> p j d", p=128), w=[xt])
        norm_tt(P, K, ident, xt, ht, gb)
        P.dma(hTv[:, :, t0:t0 + 128 * J], ht[:], r=[ht], eng="pool")
    P.finalize()


def ph_proj(nc, G, l):
    S = G.S
    P = Prog(nc, G.pool)
    NC_A = 3076
    WA = P.sb("WA", [128, 8, NC_A], BF16)
    for kc in range(8):
        load_w(P, WA, WA[:, kc, :], G.w_in[l, kc * 128:(kc + 1) * 128, 0:NC_A])
    hts = [P.sb("ht%d" % i, [128, 8, 512], BF16) for i in range(2)]
    groups = [
        (G.uaT, 0, 6, 1.0),
        (G.qbT, 768, 2, 0.125),
        (G.kbT, 1024, 2, 1.0),
        (G.zT, 1536, 2, 1.0),
        (G.xbcT, 1792, 4, 1.0),
        (G.qdT, 2308, 2, 0.125),
        (G.kdT, 2564, 2, 1.0),
    ]
    stg = {}
    for gi, (dst, c0, nch, sc) in enumerate(groups):
        stg[gi] = [P.sb("stg%d_%d" % (gi, i), [128, nch, 512], BF16) for i in range(2)]
    vst = [P.sb("vst%d" % i, [128, 4, 512], BF16) for i in range(2)]
    dst_ = [P.sb("dtst%d" % i, [128, 4, 4], F32) for i in range(2)]
    pf = [P.ps("pf%d" % i, [128, 512]) for i in range(6)]
    pv = P.ps("pv", [128, 512])
    pd = P.ps("pd", [128, 4, 4])
    hTv = rep(G.hT, "(k p) s -> p k s", p=128)
    cnt = 0
    for t in range(S // 512):
        t0 = t * 512
        ht = hts[t % 2]
        P.dma(ht[:], hTv[:, :, t0:t0 + 512], w=[ht])
        for gi, (dst, c0, nch, sc) in enumerate(groups):
            sg = stg[gi][t % 2]
            for ch in range(nch):
                ps = pf[cnt % 6]
                for kc in range(8):
                    P.mm(ps[:], WA[:, kc, c0 + ch * 128:c0 + (ch + 1) * 128], ht[:, kc, :], start=(kc == 0), stop=(kc == 7),
                         r=[WA, ht], w=[ps])
                if cnt % 2 == 0:
                    P.act(sg[:, ch, :], ps[:], AF.Copy, r=[ps], w=[sg], scale=sc)
                else:
                    P.ts(sg[:, ch, :], ps[:], sc, None, ALU.mult, r=[ps], w=[sg])
                cnt += 1
            P.dma(rep(dst, "(c p) s -> p c s", p=128)[:, :, t0:t0 + 512], sg[:], r=[sg], eng="pool")
        vs = vst[t % 2]
        ds = dst_[t % 2]
        for j in range(4):
            for half, c0 in enumerate((1280, 2820)):
                for kc in range(8):
                    P.mm(pv[:, half * 256:(half + 1) * 256], ht[:, kc, j * 128:(j + 1) * 128], WA[:, kc, c0:c0 + 256],
                         start=(kc == 0), stop=(kc == 7), r=[WA, ht], w=[pv])
            P.copy(vs[:, j, :], pv[:], r=[pv], w=[vs], eng=("dve" if j % 2 == 0 else "act"))
            for kc in range(8):
                P.mm(pd[:, j, :], ht[:, kc, j * 128:(j + 1) * 128], WA[:, kc, 2304:2308], start=(kc == 0), stop=(kc == 7),
                     r=[WA, ht], w=[pd])
        P.copy(ds[:], pd[:], r=[pd], w=[ds])
        P.dma(rep(G.vb[t0:t0 + 512, :], "(j p) c -> p j c", p=128), vs[:, :, 0:256], r=[vs], eng="pool")
        P.dma(rep(G.vd[t0:t0 + 512, :], "(j p) c -> p j c", p=128), vs[:, :, 256:512], r=[vs], eng="pool")
        P.dma(rep(G.dt[t0:t0 + 512, :], "(j p) h -> p j h", p=128), ds[:], r=[ds], eng="pool")
    P.finalize()


def ph_conva(nc, G, l):
    S = G.S
    P = Prog(nc, G.pool)
    cw = P.sb("cw", [128, 3, 2], F32)
    for k in range(3):
        P.dma(cw[:, k, :], rep(G.conv_a_w[l, k], "(c p) -> p c", p=128), w=[cw], allow_slow_non_contiguous=True)
    uv = rep(G.uaT, "(c p) s -> p c s", p=128)
    TT = 512
    ins = [P.sb("cin%d" % i, [128, 6, TT + 2], BF16) for i in range(2)]
    for i in range(2):
        P.memset(ins[i][:, :, 0:2], 0.0, w=[ins[i]])
    pt = P.sb("cp", [128, TT + 2], F32)
    acc = P.sb("cacc", [128, TT], F32)
    ys = [P.sb("cy%d" % i, [128, 2, TT], BF16) for i in range(2)]
    for t in range(S // TT):
        t0 = t * TT
        it = ins[t % 2]
        if t == 0:
            P.dma(it[:, :, 2:TT + 2], uv[:, :, 0:TT], w=[it])
        else:
            P.dma(it[:, :, :], uv[:, :, t0 - 2:t0 + TT], w=[it])
        y = ys[t % 2]
        for fc in range(2):
            P.tt(pt[:], it[:, fc, :], it[:, 4 + fc, :], ALU.mult, r=[it], w=[pt])
            P.ts(acc[:], pt[:, 2:TT + 2], cw[:, 2, fc:fc + 1], None, ALU.mult, r=[pt, cw], w=[acc])
            P.stt(acc[:], pt[:, 1:TT + 1], cw[:, 1, fc:fc + 1], acc[:], ALU.mult, ALU.add, r=[pt, cw, acc], w=[acc])
            P.stt(acc[:], pt[:, 0:TT], cw[:, 0, fc:fc + 1], acc[:], ALU.mult, ALU.add, r=[pt, cw, acc], w=[acc])
            P.tt(y[:, fc, :], acc[:], it[:, 2 + fc, 2:TT + 2], ALU.mult, r=[acc, it], w=[y])
        P.dma(rep(G.yT[0], "(c p) s -> p c s", p=128)[:, :, t0:t0 + TT], y[:], r=[y], eng="pool")
    P.finalize()


def ph_ssd(nc, G, l):
    S = G.S
    P = Prog(nc, G.pool)
    T_ = 256
    ident, identf = mk_ident(P)
    U32 = P.sb("U32", [128, 128], F32)
    P.affine(U32, [[-1, 128]], 0, 1, ALU.is_gt, 0.0, 1.0)
    ones32 = P.sb("ones32", [128, 128], F32)
    P.memset(ones32[:], 1.0, w=[ones32])
    onesb = P.sb("onesb", [128, 128], BF16)
    P.memset(onesb[:], 1.0, w=[onesb])
    L0 = P.sb("L0", [128, 256], F32)
    P.affine(L0, [[1, 256]], 0, -1, ALU.is_ge, 0.0, 1.0)
    NEG0 = P.sb("NEG0", [128, 256], F32)
    P.affine(NEG0, [[1, 256]], 0, -1, ALU.is_ge, -BIG, 0.0)
    mh = P.sb("mh", [128, 256], F32)
    P.memset(mh[:], -0.5, w=[mh])
    cw = P.sb("cw", [128, 4, 4], F32)
    for k in range(4):
        P.dma(cw[:, k, :], rep(G.ssm_conv_w[l, k], "(c p) -> p c", p=128), w=[cw], allow_slow_non_contiguous=True)
    cb = P.sb("cb", [128, 4], F32)
    P.dma(cb[:], rep(G.ssm_conv_b[l], "(c p) -> p c", p=128), w=[cb], allow_slow_non_contiguous=True)
    dtb = P.sb("dtb", [128, 4], F32)
    P.dma(dtb[:], G.ssm_dt_bias[l].partition_broadcast(128), w=[dtb])
    Arow = P.sb("Arow", [128, 4], F32)
    P.dma(Arow[:], G.ssm_a_log[l].partition_broadcast(128), w=[Arow])
    P.act(Arow[:], Arow[:], AF.Exp, r=[Arow], w=[Arow])
    P.ts(Arow[:], Arow[:], -1.0, None, ALU.mult, r=[Arow], w=[Arow])
    Dcol = P.sb("Dcol", [128, 2], F32)
    for g in range(2):
        for e_ in range(2):
            P.dma(Dcol[e_ * 64:(e_ + 1) * 64, g:g + 1], G.ssm_d[l, 2 * g + e_:2 * g + e_ + 1].partition_broadcast(64), w=[Dcol])
    ng = P.sb("ng", [128, 2], F32)
    P.dma(ng[:], rep(G.ssm_norm_g[l], "(g p) -> p g", p=128), w=[ng], allow_slow_non_contiguous=True)

    XIN = [P.sb("XIN%d" % i, [128, 4, T_ + 3], BF16) for i in range(2)]
    ZIN = [P.sb("ZIN%d" % i, [128, 2, T_], BF16) for i in range(2)]
    DTIN = [P.sb("DTIN%d" % i, [128, 2, 4], F32) for i in range(2)]
    P.memset(XIN[0][:, :, 0:3], 0.0, w=[XIN[0]])
    acc = [P.sb("acc%d" % i, [128, T_], F32) for i in range(4)]
    xcs = [[P.sb("xc%d_%d" % (j, i), [128, T_], BF16) for i in range(4)] for j in range(2)]
    dts = P.sb("dts", [128, 2, 4], F32)
    dte_ = P.sb("dte_", [128, 2, 4], F32)
    dsps = [P.sb("dsp%d" % j, [128, 2, 4], F32) for j in range(2)]
    a_ts = [P.sb("a_t%d" % j, [128, 2, 4], F32) for j in range(2)]
    Xpads = [[P.sb("Xpad%d_%d" % (j, i), [128, 2, 2, 128], BF16) for i in range(2)] for j in range(2)]
    for j in range(2):
        for i in range(2):
            P.memset(Xpads[j][i][:], 0.0, w=[Xpads[j][i]], eng=("dve" if i == 0 else "pool"))
    Btoks = [[P.sb("Btok%d_%d" % (j, i), [128, 128], BF16) for i in range(2)] for j in range(2)]
    Xdte = [P.sb("Xdte%d" % i, [128, 256], BF16) for i in range(2)]
    aL0 = [P.sb("aL0_%d" % i, [128, 256], F32) for i in range(2)]
    aL1 = [P.sb("aL1_%d" % i, [128, 128], F32) for i in range(2)]
    Dt = [P.sb("Dt%d" % i, [128, 384], F32) for i in range(2)]
    Et = [P.sb("Et%d" % i, [128, 256], F32) for i in range(4)]
    Mt = [P.sb("Mt%d" % i, [128, 384], BF16) for i in range(2)]
    Cpt = [P.sb("Cpt%d" % i, [128, 256], BF16) for i in range(2)]
    H32 = P.sb("H32", [128, 2, 64], F32)
    Hpad = P.sb("Hpad", [128, 2, 128], BF16)
    P.memset(H32[:], 0.0, w=[H32])
    P.memset(Hpad[:], 0.0, w=[Hpad])
    yf = P.sb("yf", [128, 256], F32)
    sz = P.sb("sz", [128, 256], F32)
    gt = P.sb("gt", [128, 256], F32)
    sq = P.sb("sq", [128, 256], BF16)
    rs = P.sb("rs", [128, 256], F32)
    yst = [P.sb("yst%d" % i, [128, 2, 256], BF16) for i in range(2)]
    segp = [P.ps("segp%d" % i, [128, 384]) for i in range(2)]
    csbp = P.ps("csbp", [128, 256])
    Gp = [P.ps("Gp%d" % i, [128, 384]) for i in range(2)]
    Yg = [P.ps("Yg%d" % i, [128, 256]) for i in range(2)]
    misc = P.ps("misc", [128, 512])
    ptb = T(misc.h[:, 0:256].bitcast(BF16), "ptb")
    HSb = T(misc.h[:, 256:512], "HSb")
    ptb.res = misc.res
    HSb.res = misc.res

    xv = rep(G.xbcT, "(c p) s -> p c s", p=128)
    zv = rep(G.zT, "(g p) s -> p g s", p=128)
    yv = rep(G.yT[2], "(g p) s -> p g s", p=128)
    nchunks = S // T_
    hc = 0

    def front_parts(c):
        t0 = c * T_
        xin = XIN[c % 2]
        zin = ZIN[c % 2]
        dtin = DTIN[c % 2]
        xc = xcs[c % 2]
        dsp = dsps[c % 2]
        a_t = a_ts[c % 2]
        Xpad = Xpads[c % 2]
        Btok = Btoks[c % 2]

        def conv(ct):
            P.ts(acc[ct][:], xin[:, ct, 0:T_], cw[:, 0, ct:ct + 1], None, ALU.mult, r=[xin, cw], w=[acc[ct]])
            for k in range(1, 4):
                P.stt(acc[ct][:], xin[:, ct, k:k + T_], cw[:, k, ct:ct + 1], acc[ct][:], ALU.mult, ALU.add,
                      r=[xin, cw, acc[ct]], w=[acc[ct]])
            P.act(xc[ct][:], acc[ct][:], AF.Silu, r=[acc[ct], cb], w=[xc[ct]], bias=cb[:, ct:ct + 1])

        def xtr(g):
            for t in range(2):
                sl = ptb[:, (2 * t + g) * 128:(2 * t + g + 1) * 128]
                P.tr(sl, xc[g][:, t * 128:(t + 1) * 128], ident[:], r=[xc[g], ident], w=[ptb])
                for e_ in range(2):
                    P.ts(Xpad[t][:, g, e_, e_ * 64:(e_ + 1) * 64],
                         ptb[:, (2 * t + g) * 128 + e_ * 64:(2 * t + g) * 128 + (e_ + 1) * 64],
                         dsp[:, t, 2 * g + e_:2 * g + e_ + 1], None, ALU.mult, r=[ptb, dsp], w=[Xpad[t]])

        def p0():
            if c == 0:
                P.dma(xin[:, :, 3:T_ + 3], xv[:, :, 0:T_], w=[xin])
            else:
                P.dma(xin[:, :, :], xv[:, :, t0 - 3:t0 + T_], w=[xin])
            P.dma(zin[:], zv[:, :, t0:t0 + T_], w=[zin])
            P.dma(dtin[:], rep(G.dt[t0:t0 + T_, :], "(t p) h -> p t h", p=128), w=[dtin])
            for t in range(2):
                P.tt(dts[:, t, :], dtin[:, t, :], dtb[:], ALU.add, r=[dtin, dtb], w=[dts])
            P.act(dte_[:], dts[:], AF.Exp, r=[dts], w=[dte_])
            P.act(dsp[:], dte_[:], AF.Ln, r=[dte_], w=[dsp], bias=1.0)
            for t in range(2):
                P.tt(a_t[:, t, :], dsp[:, t, :], Arow[:], ALU.mult, r=[dsp, Arow], w=[a_t])
            conv(0)

        def p1():
            conv(1)
            xtr(0)

        def p2():
            conv(2)
            xtr(1)

        def p3():
            conv(3)
            for t in range(2):
                sl = ptb[:, t * 128:(t + 1) * 128]
                P.tr(sl, xc[2][:, t * 128:(t + 1) * 128], ident[:], r=[xc[2], ident], w=[ptb])
                P.copy(Btok[t][:], sl, r=[ptb], w=[Btok[t]], eng="act")

        return [p0, p1, p2, p3]

    for f_ in front_parts(0):
        f_()
    for c in range(nchunks):
        t0 = c * T_
        zin = ZIN[c % 2]
        xc = xcs[c % 2]
        dsp = dsps[c % 2]
        a_t = a_ts[c % 2]
        Xpad = Xpads[c % 2]
        Btok = Btoks[c % 2]
        nxt = front_parts(c + 1) if c + 1 < nchunks else [None] * 4
        for g in range(2):
            gr = slice(g * 64, (g + 1) * 64)
            P.mm(Gp[g][:, 0:256], xc[2][gr, 0:128], xc[3][gr, 0:256], r=[xc[2], xc[3]], w=[Gp[g]])
            P.mm(Gp[g][:, 256:384], xc[2][gr, 128:256], xc[3][gr, 128:256], r=[xc[2], xc[3]], w=[Gp[g]])
        def h_pre(h):
            b = (hc + h) % 2
            P.ts(aL0[b][:], L0[:], a_t[:, 0, h:h + 1], None, ALU.mult, r=[L0, a_t], w=[aL0[b]])
            P.ts(aL1[b][:], L0[:, 0:128], a_t[:, 1, h:h + 1], None, ALU.mult, r=[L0, a_t], w=[aL1[b]])

        def h_seg(h):
            b = (hc + h) % 2
            sp_ = segp[b]
            P.mm(sp_[:, 0:256], U32[:], aL0[b][:], start=True, stop=False, r=[U32, aL0[b]], w=[sp_])
            P.mm(sp_[:, 128:256], ones32[:], aL1[b][:], start=False, stop=False, r=[ones32, aL1[b]], w=[sp_])
            P.mm(sp_[:, 0:256], identf[:], NEG0[:], start=False, stop=True, r=[identf, NEG0], w=[sp_])
            P.mm(sp_[:, 256:384], U32[:], aL1[b][:], start=True, stop=False, r=[U32, aL1[b]], w=[sp_])
            P.mm(sp_[:, 256:384], identf[:], NEG0[:, 0:128], start=False, stop=True, r=[identf, NEG0], w=[sp_])
            P.mm(csbp[:, 0:256], ones32[:], aL0[b][:], start=True, stop=False, r=[ones32, aL0[b]], w=[csbp])
            P.mm(csbp[:, 128:256], ones32[:], aL1[b][:], start=False, stop=True, r=[ones32, aL1[b]], w=[csbp])

        def h_act(h):
            b = (hc + h) % 2
            P.act(Dt[b][:], segp[b][:], AF.Exp, r=[segp[b]], w=[Dt[b]])
            P.act(Et[h][:], csbp[:], AF.Exp, r=[csbp], w=[Et[h]])

        def h_post(h):
            b = (hc + h) % 2
            g, e_ = h // 2, h % 2
            gr = slice(g * 64, (g + 1) * 64)
            P.tt(Mt[b][:], Dt[b][:], Gp[g][:], ALU.mult, r=[Dt[b], Gp[g]], w=[Mt[b]])
            P.tt(Cpt[b][gr, :], xc[3][gr, :], Et[h][gr, :], ALU.mult, r=[xc[3], Et[h]], w=[Cpt[b]])
            P.mm(Yg[g][:, 0:256], Xpad[0][:, g, e_, :], Mt[b][:, 0:256], start=(e_ == 0), stop=False,
                 r=[Xpad[0], Mt[b]], w=[Yg[g]])
            P.mm(Yg[g][:, 128:256], Xpad[1][:, g, e_, :], Mt[b][:, 256:384], start=False, stop=False,
                 r=[Xpad[1], Mt[b]], w=[Yg[g]])
            P.mm(Yg[g][:, 0:256], Hpad[gr, e_, :], Cpt[b][gr, :], start=False, stop=(e_ == 1),
                 r=[Hpad, Cpt[b]], w=[Yg[g]])
            P.ts(Xdte[0][:, h * 64:(h + 1) * 64], Xpad[0][:, g, e_, e_ * 64:(e_ + 1) * 64], Dt[b][:, 255:256], None,
                 ALU.mult, r=[Xpad[0], Dt[b]], w=[Xdte[0]])
            P.ts(Xdte[1][:, h * 64:(h + 1) * 64], Xpad[1][:, g, e_, e_ * 64:(e_ + 1) * 64], Dt[b][:, 383:384], None,
                 ALU.mult, r=[Xpad[1], Dt[b]], w=[Xdte[1]])

        h_pre(0)
        h_seg(0)
        for h in range(4):
            if h + 1 < 4:
                h_pre(h + 1)
            h_act(h)
            if h + 1 < 4:
                h_seg(h + 1)
            h_post(h)
            if nxt[h] is not None:
                nxt[h]()
        P.mm(HSb[:], Btok[0][:], Xdte[0][:], start=True, stop=False, r=[Btok[0], Xdte[0]], w=[HSb])
        P.mm(HSb[:], Btok[1][:], Xdte[1][:], start=False, stop=True, r=[Btok[1], Xdte[1]], w=[HSb])
        for g in range(2):
            gr = slice(g * 64, (g + 1) * 64)
            for e_ in range(2):
                h = 2 * g + e_
                P.stt(H32[gr, e_, :], H32[gr, e_, :], Et[h][gr, 255:256], HSb[gr, h * 64:(h + 1) * 64], ALU.mult, ALU.add,
                      r=[H32, Et[h], HSb], w=[H32])
                P.copy(Hpad[gr, e_, e_ * 64:(e_ + 1) * 64], H32[gr, e_, :], r=[H32], w=[Hpad])
        ys = yst[c % 2]
        for g in range(2):
            P.stt(yf[:], xc[g][:], Dcol[:, g:g + 1], Yg[g][:], ALU.mult, ALU.add, r=[xc[g], Dcol, Yg[g]], w=[yf])
            P.act(sz[:], zin[:, g, :], AF.Silu, r=[zin], w=[sz])
            P.tt(gt[:], yf[:], sz[:], ALU.mult, r=[yf, sz], w=[gt])
            P.tt(sq[:], gt[:], gt[:], ALU.mult, r=[gt], w=[sq])
            P.mm(csbp[:], onesb[:], sq[:], r=[onesb, sq], w=[csbp])
            P.ts(rs[:], csbp[:], 1.0 / 128, EPS, ALU.mult, ALU.add, r=[csbp], w=[rs])
            P.act(rs[:], rs[:], AF.Ln, r=[rs], w=[rs])
            P.act(rs[:], rs[:], AF.Exp, r=[rs], w=[rs], scale=-0.5)
            P.stt(ys[:, g, :], gt[:], ng[:, g:g + 1], rs[:], ALU.mult, ALU.mult, r=[gt, ng, rs], w=[ys])
        P.dma(yv[:, :, t0:t0 + T_], ys[:], r=[ys], eng="pool")
    P.finalize()


def ph_sb(nc, G, l):
    S = G.S
    P = Prog(nc, G.pool)
    NT = S // 512
    NK = S // 128
    ident, identf = mk_ident(P)
    IUf = P.sb("IUf", [128, 128], F32)
    P.affine(IUf, [[-1, 128]], 0, 1, ALU.is_ge, 0.0, -1.0)
    IUn = P.sb("IUn", [128, 128], BF16)
    P.copy(IUn[:], IUf[:], r=[IUf], w=[IUn])
    onesb = P.sb("onesb", [128, 2], BF16)
    P.memset(onesb[:], 1.0, w=[onesb])
    CM = []
    f = P.sb("CMf", [128, 512], F32)
    for d in range(4):
        P.affine(f, [[1, 512]], -128 * d, -1, ALU.is_gt, -BIG, 0.0)
        b = P.sb("CM%d" % d, [128, 512], BF16)
        P.copy(b[:], f[:], r=[f], w=[b])
        CM.append(b)
    qz = [[P.sb("qz%d_%d" % (j, i), [128, S], BF16) for i in range(2)] for j in range(2)]
    kz = [[P.sb("kz%d_%d" % (j, i), [128, S], BF16) for i in range(2)] for j in range(2)]
    for j in range(2):
        for i in range(2):
            P.memset(qz[j][i][64:128, :], 0.0, w=[qz[j][i]], eng=("dve" if i == 0 else "pool"))
            P.memset(kz[j][i][64:128, :], 0.0, w=[kz[j][i]], eng=("dve" if i == 0 else "pool"))
    v = P.sb("v", [128, NK, 256], BF16)
    vv = rep(G.vb, "(n p) c -> p n c", p=128)
    step = 2048
    for s0 in range(0, S, step):
        s1 = min(S, s0 + step)
        P.dma(v[:, s0 // 128:s1 // 128, :], vv[:, s0 // 128:s1 // 128, :], w=[v])

    def load_qk(hp):
        for i in range(2):
            h = 2 * hp + i
            for s0 in range(0, S, 4096):
                s1 = min(S, s0 + 4096)
                P.dma(qz[hp % 2][i][0:64, s0:s1], G.qbT[h * 64:(h + 1) * 64, s0:s1], w=[qz[hp % 2][i]])
                P.dma(kz[hp % 2][i][0:64, s0:s1], G.kbT[h * 64:(h + 1) * 64, s0:s1], w=[kz[hp % 2][i]])

    load_qk(0)
    load_qk(1)
    Zb = [[P.ps("Zb%d_%d" % (i, j), [128, 512]) for j in range(3)] for i in range(2)]
    OC = [P.ps("OC%d" % i, [128, 512]) for i in range(2)]
    Op = [T(rep(OC[i].h[:, 0:256], "p (q d) -> p q d", d=64), "Op%d" % i) for i in range(2)]
    csp = [T(OC[0].h[:, 256 + 4 * i:260 + 4 * i], "csp%d" % i) for i in range(2)]
    csp2 = T(OC[0].h[:, 256:264], "csp2")
    csp2.res = OC[0].res
    pTv = [T(OC[i].h[:, 384:512].bitcast(BF16), "pTv%d" % i) for i in range(2)]
    for i in range(2):
        Op[i].res = OC[i].res
        csp[i].res = OC[0].res
        pTv[i].res = OC[i].res
    Et = [[P.sb("Et%d_%d" % (i, j), [128, 512], F32) for j in range(2)] for i in range(2)]
    SPt = [[P.sb("SPt%d_%d" % (i, j), [128, 512], BF16) for j in range(2)] for i in range(2)]
    Pt = [[P.sb("Pt%d_%d" % (i, j), [128, 512], BF16) for j in range(2)] for i in range(2)]
    acc = [[P.sb("acc%d_%d" % (i, j), [128, 4, 64], F32) for j in range(2)] for i in range(2)]
    tmpa = [P.sb("tmpa%d" % i, [128, 4, 64], F32) for i in range(2)]
    accb = [P.sb("accb%d" % i, [128, 4, 64], BF16) for i in range(2)]
    dd2 = [P.sb("dd2_%d" % j, [128, 8], F32) for j in range(2)]
    dd = [[T(dd2[j].h[:, 4 * i:4 * i + 4], "dd%d_%d" % (i, j)) for j in range(2)] for i in range(2)]
    for i in range(2):
        for j in range(2):
            dd[i][j].res = dd2[j].res
    yst = [P.sb("yst%d" % i, [64, 512], BF16) for i in range(4)]
    yc = [0]

    for hp in range(2):
        its = [(c, n) for c in range(NT) for n in range(0, 4 * c + 4)]
        N = len(its)

        def stA(k):
            c, n = its[k]
            qs = slice(c * 512, (c + 1) * 512)
            ks = slice(n * 128, (n + 1) * 128)
            diag = n >= 4 * c
            for i in range(2):
                Z = Zb[i][k % 3]
                P.mm(Z[:], kz[hp % 2][i][:, ks], qz[hp % 2][i][:, qs], start=True, stop=(not diag),
                     r=[kz[hp % 2][i], qz[hp % 2][i]], w=[Z])
                if diag:
                    P.mm(Z[:], ident[:], CM[n - 4 * c][:], start=False, stop=True, r=[ident, CM[n - 4 * c]], w=[Z])

        def stB(k):
            for i in range(2):
                P.act(Et[i][k % 2][:], Zb[i][k % 3][:], AF.Exp, r=[Zb[i][k % 3]], w=[Et[i][k % 2]])
            for i in range(2):
                P.act(SPt[i][k % 2][:], Et[i][k % 2][:], AF.Ln, r=[Et[i][k % 2]], w=[SPt[i][k % 2]], bias=1.0)

        def stC(k):
            for i in range(2):
                Z = Zb[i][k % 3]
                P.mm(Z[:], IUn[:], SPt[i][k % 2][:], start=False, stop=True, r=[IUn, SPt[i][k % 2]], w=[Z], sgc=True)

        def stD(k):
            for i in range(2):
                P.act(Pt[i][k % 2][:], Zb[i][k % 3][:], AF.Exp, r=[Zb[i][k % 3]], w=[Pt[i][k % 2]])

        def stE(k):
            c, n = its[k]
            for i in range(2):
                h = 2 * hp + i
                for qi in range(4):
                    P.mm(Op[i][:, qi, :], Pt[i][k % 2][:, qi * 128:(qi + 1) * 128], v[:, n, h * 64:(h + 1) * 64],
                         r=[Pt[i][k % 2], v], w=[Op[i]])
                for qi in range(4):
                    P.mm(csp[i][:, qi:qi + 1], SPt[i][k % 2][:, qi * 128:(qi + 1) * 128], onesb[:, 0:1],
                         r=[SPt[i][k % 2], onesb], w=[csp[i]])

        def stFd(k):
            c, n = its[k]
            if n > 0:
                P.act(dd2[k % 2][:], csp2[:], AF.Exp, r=[csp2], w=[dd2[k % 2]], scale=-1.0)

        def stF(k):
            c, n = its[k]
            cb = c % 2
            for i in range(2):
                if n == 0:
                    P.copy(acc[i][cb][:], Op[i][:], r=[Op[i]], w=[acc[i][cb]])
                else:
                    P.tt(tmpa[i][:], acc[i][cb][:], dd[i][k % 2][:].unsqueeze(2).to_broadcast([128, 4, 64]), ALU.mult,
                         r=[acc[i][cb], dd[i][k % 2]], w=[tmpa[i]])
                    P.tt(acc[i][cb][:], tmpa[i][:], Op[i][:], ALU.add, r=[tmpa[i], Op[i]], w=[acc[i][cb]])
            if n == 4 * c + 3:
                qs = slice(c * 512, (c + 1) * 512)
                for i in range(2):
                    h = 2 * hp + i
                    P.copy(accb[i][:], acc[i][cb][:], r=[acc[i][cb]], w=[accb[i]])
                    ys = yst[yc[0] % 4]
                    yc[0] += 1
                    for half in range(2):
                        for q2 in range(2):
                            qi = 2 * half + q2
                            P.tr(pTv[i][0:64, q2 * 128:(q2 + 1) * 128], accb[i][:, qi, :], ident[:], r=[accb[i], ident], w=[pTv[i]])
                        P.copy(ys[:, half * 256:(half + 1) * 256], pTv[i][0:64, :], r=[pTv[i]], w=[ys], eng="act")
                    P.dma(G.yT[1, h * 64:(h + 1) * 64, qs], ys[:], r=[ys], eng="pool")

        stA(0)
        for k in range(N + 2):
            if k + 1 < N:
                stA(k + 1)
            if k < N:
                stB(k)
            if 0 <= k - 2 < N:
                stFd(k - 2)
            if 0 <= k - 1 < N:
                stD(k - 1)
            if k < N:
                stC(k)
            if 0 <= k - 2 < N:
                stF(k - 2)
            if 0 <= k - 1 < N:
                stE(k - 1)
    P.finalize()


def ph_moba(nc, G, l, stabilize=True):
    S = G.S
    P = Prog(nc, G.pool)
    NT = S // 512
    NK = S // 128
    NB = S // 256
    assert NB <= 32
    ident, identf = mk_ident(P)
    ones32 = P.sb("ones32", [128, 64], F32)
    P.memset(ones32[:], 1.0, w=[ones32])
    onesb = P.sb("onesb", [128, 128], BF16)
    P.memset(onesb[:], 1.0, w=[onesb])
    CM = []
    f = P.sb("CMf", [128, 512], F32)
    for d in range(4):
        P.affine(f, [[1, 512]], -128 * d, -1, ALU.is_ge, -BIG, 0.0)
        b = P.sb("CM%d" % d, [128, 512], BF16)
        P.copy(b[:], f[:], r=[f], w=[b])
        CM.append(b)
    KE = [P.sb("KE%d" % i, [128, S], BF16) for i in range(2)]
    QN = [P.sb("QN%d" % i, [128, S], BF16) for i in range(2)]
    for i in range(2):
        P.memset(KE[i][64:128, :], 0.0, w=[KE[i]], eng=("dve" if i == 0 else "pool"))
        P.memset(QN[i][64:128, :], 0.0, w=[QN[i]], eng=("dve" if i == 0 else "pool"))
    ohf = P.sb("ohf", [128, 2048], F32)
    for s0 in range(0, S, 2048):
        w_ = min(2048, S - s0)
        ohv = T(rep(ohf.h[64:96, 0:w_], "p (b k) -> p b k", k=256), "ohv")
        ohv.res = ohf.res
        P.affine(ohv, [[-1, w_ // 256], [0, 256]], -(s0 // 256), 1, ALU.is_equal, 0.0, 1.0)
        for i in range(2):
            P.copy(KE[i][64:96, s0:s0 + w_], ohf[64:96, 0:w_], r=[ohf], w=[KE[i]], eng=("dve" if i == 0 else "act"))
    Vaug = P.sb("Vaug", [128, NK, 4, 65], BF16)
    vtmp = [P.sb("vtmp%d" % i, [128, 8, 256], BF16) for i in range(2)]
    vv = rep(G.vd, "(n p) c -> p n c", p=128)
    P.memset(Vaug[:, :, :, 64:65], 1.0, w=[Vaug])
    step = 1024
    for si, s0 in enumerate(range(0, S, step)):
        s1 = min(S, s0 + step)
        n0, n1 = s0 // 128, s1 // 128
        vt = vtmp[si % 2]
        P.dma(vt[:, 0:n1 - n0, :], vv[:, n0:n1, :], w=[vt])
        for h in range(4):
            P.copy(Vaug[:, n0:n1, h, 0:64], vt[:, 0:n1 - n0, h * 64:(h + 1) * 64], r=[vt], w=[Vaug],
                   eng=("act" if h % 2 == 0 else "dve"))
    km32 = [P.sb("km32_%d" % i, [64, 32], F32) for i in range(2)]
    kmhi = [P.sb("kmhi%d" % i, [64, 32], BF16) for i in range(2)]
    kmhf = [P.sb("kmhf%d" % i, [64, 32], F32) for i in range(2)]
    kmlo = [P.sb("kmlo%d" % i, [64, 32], BF16) for i in range(2)]
    for i in range(2):
        P.memset(km32[i][:], 0.0, w=[km32[i]])

    Zp = [[P.ps("Zp%d_%d" % (i, j), [128, 512]) for j in range(2)] for i in range(2)]
    OT = [[P.ps("OT%d_%d" % (i, j), [128, 512]) for j in range(2)] for i in range(2)]
    KM = P.sb("KM", [128, 2], F32)
    ksqa = P.sb("ksqa", [64, S], BF16)
    kmx = P.sb("kmx", [128, 16], F32)
    kqs = [OT[1][0], OT[1][1]]
    NEGVB = P.sb("NEGVB", [128, 32, 2, 32], F32)
    P.affine(NEGVB, [[1, 32], [0, 2], [-1, 32]], 0, 0, ALU.is_gt, -BIG, 0.0)
    OWNB = P.sb("OWNB", [128, 32, 2, 32], F32)
    P.affine(OWNB, [[1, 32], [0, 2], [-1, 32]], 0, 0, ALU.is_equal, 0.0, 1.0)
    NEGV3 = rep(NEGVB[:], "p a b n -> p (a b) n")
    OWN3 = rep(OWNB[:], "p a b n -> p (a b) n")
    QB = 16
    qsq = [P.sb("qsq%d" % i, [64, QB * 128], BF16) for i in range(2)]
    gm = [P.sb("gm%d" % i, [128, QB, 32], F32) for i in range(2)]
    g2 = [P.sb("g2_%d" % i, [128, QB, 32], F32) for i in range(2)]
    eq = [P.sb("eq%d" % i, [128, QB, 32], F32) for i in range(2)]
    mx = [P.sb("mx%d" % i, [128, QB], F32) for i in range(2)]
    mq = [P.sb("mq%d" % i, [128, QB], F32) for i in range(2)]
    nb = [P.sb("nb%d" % i, [128, QB, 32], BF16) for i in range(2)]
    Pt = [[P.sb("Pt%d_%d" % (i, j), [128, 512], BF16) for j in range(2)] for i in range(2)]
    RL = [P.sb("RL%d" % i, [128, 512], F32) for i in range(2)]
    bcs = [P.sb("bcs%d" % i, [64, 512], F32) for i in range(2)]
    yo = [P.sb("yo%d" % i, [64, 512], BF16) for i in range(4)]
    gpb = [T(rep(Zp[i][0].h[:, :], "p (q n) -> p q n", n=32), "gpb%d" % i) for i in range(2)]
    pTb = [T(Zp[i][1].h[:, :].bitcast(BF16), "pTb%d" % i) for i in range(2)]
    qnp = [T(OT[i][0].h[:, 0:QB], "qnp%d" % i) for i in range(2)]
    for i in range(2):
        gpb[i].res = Zp[i][0].res
        pTb[i].res = Zp[i][1].res
        qnp[i].res = OT[i][0].res
    yc = [0]
    zc = [0]
    kc_ = 0
    for hp in range(2):
        for i in range(2):
            h = 2 * hp + i
            for s0 in range(0, S, 4096):
                s1 = min(S, s0 + 4096)
                P.dma(QN[i][0:64, s0:s1], G.qdT[h * 64:(h + 1) * 64, s0:s1], w=[QN[i]])
                P.dma(KE[i][0:64, s0:s1], G.kdT[h * 64:(h + 1) * 64, s0:s1], w=[KE[i]])
        for i in range(2):
            P.op("dve", (lambda i_: (lambda e: e.tensor_reduce(out=km32[i_][:, 0:NB], in_=rep(KE[i_][0:64, :], "p (b k) -> p b k", k=256),
                                                               axis=AX.X, op=ALU.add)))(i), r=[KE[i]], w=[km32[i]])
            P.copy(kmhi[i][:], km32[i][:], r=[km32[i]], w=[kmhi[i]])
            P.copy(kmhf[i][:], kmhi[i][:], r=[kmhi[i]], w=[kmhf[i]])
            P.tt(kmlo[i][:], km32[i][:], kmhf[i][:], ALU.subtract, r=[km32[i], kmhf[i]], w=[kmlo[i]])
            if stabilize:
                nch = S // 512
                for ci in range(nch):
                    s0 = ci * 512
                    P.tt(ksqa[:, s0:s0 + 512], KE[i][0:64, s0:s0 + 512], KE[i][0:64, s0:s0 + 512], ALU.mult, r=[KE[i]], w=[ksqa])
                for ci in range(nch):
                    s0 = ci * 512
                    kqb = kqs[ci % 2]
                    P.mm(kqb[:], onesb[0:64, :], ksqa[:, s0:s0 + 512], r=[onesb, ksqa], w=[kqb])
                    P.op("dve", (lambda o_, i_: (lambda e: e.tensor_reduce(out=o_, in_=i_, axis=AX.X, op=ALU.max)))(
                        kmx[:, ci:ci + 1], kqb[:]), r=[kqb], w=[kmx])
                P.op("dve", (lambda o_, i_: (lambda e: e.tensor_reduce(out=o_, in_=i_, axis=AX.X, op=ALU.max)))(
                    KM[:, i:i + 1], kmx[:, 0:nch]), r=[kmx], w=[KM])
        for q0 in range(0, NK, QB):
            nq = min(QB, NK - q0)
            cs_ = slice(q0 * 128, (q0 + nq) * 128)
            for i in range(2):
                if stabilize:
                    P.tt(qsq[i][:, 0:nq * 128], QN[i][0:64, cs_], QN[i][0:64, cs_], ALU.mult, r=[QN[i]], w=[qsq[i]])
                for j in range(nq):
                    cj = slice((q0 + j) * 128, (q0 + j + 1) * 128)
                    P.mm(gpb[i][:, j, :], QN[i][0:64, cj], kmhi[i][:], start=True, stop=False, r=[QN[i], kmhi[i]], w=[gpb[i]])
                    P.mm(gpb[i][:, j, :], QN[i][0:64, cj], kmlo[i][:], start=False, stop=True, r=[QN[i], kmlo[i]], w=[gpb[i]])
                    if stabilize:
                        P.mm(qnp[i][:, j:j + 1], qsq[i][:, j * 128:(j + 1) * 128], onesb[0:64, 0:1],
                             r=[qsq[i], onesb], w=[qnp[i]])
            for i in range(2):
                G_ = gm[i]
                P.tt(G_[:, 0:nq, :], gpb[i][:, 0:nq, :], NEGV3[:, q0:q0 + nq, :], ALU.add, r=[gpb[i], NEGVB], w=[G_])
                src = G_
                for it in range(3):
                    P.op("dve", (lambda o_, s_: (lambda e: e.tensor_reduce(out=o_, in_=s_, axis=AX.X, op=ALU.max)))(
                        mx[i][:, 0:nq], src[:, 0:nq, :]), r=[src], w=[mx[i]])
                    if it < 2:
                        P.tt(eq[i][:, 0:nq, :], src[:, 0:nq, :], mx[i][:, 0:nq].unsqueeze(2).to_broadcast([128, nq, 32]),
                             ALU.is_equal, r=[src, mx[i]], w=[eq[i]])
                        P.stt(g2[i][:, 0:nq, :], eq[i][:, 0:nq, :], -1e6, src[:, 0:nq, :], ALU.mult, ALU.add,
                              r=[eq[i], src], w=[g2[i]])
                        src = g2[i]
                P.ts(mx[i][:, 0:nq], mx[i][:, 0:nq], -BIG / 2, None, ALU.max, r=[mx[i]], w=[mx[i]])
                P.tt(eq[i][:, 0:nq, :], G_[:, 0:nq, :], mx[i][:, 0:nq].unsqueeze(2).to_broadcast([128, nq, 32]),
                     ALU.is_ge, r=[G_, mx[i]], w=[eq[i]])
                P.tt(eq[i][:, 0:nq, :], eq[i][:, 0:nq, :], OWN3[:, q0:q0 + nq, :], ALU.max, r=[eq[i], OWNB], w=[eq[i]])
                if stabilize:
                    P.act(mq[i][:, 0:nq], qnp[i][:, 0:nq], AF.Sqrt, r=[qnp[i], KM], w=[mq[i]], scale=KM[:, i:i + 1])
                    P.ts(mq[i][:, 0:nq], mq[i][:, 0:nq], -1.0, -BIG, ALU.mult, ALU.add, r=[mq[i]], w=[mq[i]])
                else:
                    P.memset(mq[i][:], -BIG, w=[mq[i]], eng="dve")
                P.stt(nb[i][:, 0:nq, :], eq[i][:, 0:nq, :], BIG, mq[i][:, 0:nq].unsqueeze(2).to_broadcast([128, nq, 32]),
                      ALU.mult, ALU.add, r=[eq[i], mq[i]], w=[nb[i]])
                for j0 in range(0, nq, 8):
                    nj = min(8, nq - j0)
                    for j in range(j0, j0 + nj):
                        P.tr(pTb[i][64:96, (j - j0) * 128:(j - j0 + 1) * 128], nb[i][:, j, :], ident[:], r=[nb[i], ident], w=[pTb[i]])
                    P.copy(QN[i][64:96, (q0 + j0) * 128:(q0 + j0 + nj) * 128], pTb[i][64:96, 0:nj * 128], r=[pTb[i]], w=[QN[i]],
                           eng="act")
        its = [(c, n) for c in range(NT) for n in range(0, 4 * c + 4)]
        N = len(its)
        zslot = {}

        def mA(k):
            c, n = its[k]
            qs = slice(c * 512, (c + 1) * 512)
            ks = slice(n * 128, (n + 1) * 128)
            diag = n >= 4 * c
            zslot[k] = zc[0] % 2
            zc[0] += 1
            for i in range(2):
                Z = Zp[i][zslot[k]]
                P.mm(Z[:], KE[i][:, ks], QN[i][:, qs], start=True, stop=(not diag), r=[KE[i], QN[i]], w=[Z])
                if diag:
                    P.mm(Z[:], ident[:], CM[n - 4 * c][:], start=False, stop=True, r=[ident, CM[n - 4 * c]], w=[Z])

        def mB(k):
            for i in range(2):
                P.act(Pt[i][k % 2][:], Zp[i][zslot[k]][:], AF.Exp, r=[Zp[i][zslot[k]]], w=[Pt[i][k % 2]])

        def mC(k):
            c, n = its[k]
            for i in range(2):
                h = 2 * hp + i
                P.mm(OT[i][c % 2][0:65, :], Vaug[:, n, h, :], Pt[i][k % 2][:], start=(n == 0), stop=(n == 4 * c + 3),
                     r=[Vaug, Pt[i][k % 2]], w=[OT[i][c % 2]])

        def mFin(c):
            qs = slice(c * 512, (c + 1) * 512)
            for i in range(2):
                h = 2 * hp + i
                O_ = OT[i][c % 2]
                P.op("dve", (lambda o_, i_: (lambda e: e.reciprocal(out=o_, in_=i_)))(RL[i][64:65, :], O_[64:65, :]),
                     r=[O_], w=[RL[i]])
                Zf = Zp[i][zfree[0]]
                P.mm(Zf[0:64, :], ones32[64:65, 0:64], RL[i][64:65, :], r=[ones32, RL[i]], w=[Zf])
                P.copy(bcs[i][:], Zf[0:64, :], r=[Zf], w=[bcs[i]], eng="act")
                y = yo[yc[0] % 4]
                yc[0] += 1
                P.tt(y[:], O_[0:64, :], bcs[i][:], ALU.mult, r=[O_, bcs[i]], w=[y])
                P.dma(G.yT[3, h * 64:(h + 1) * 64, qs], y[:], r=[y], eng="pool")

        zfree = [0]
        mA(0)
        pend = None
        for k in range(N):
            if k + 1 < N:
                mA(k + 1)
            mB(k)
            zfree[0] = zslot[k]
            if pend is not None:
                mFin(pend)
                pend = None
            mC(k)
            c, n = its[k]
            if n == 4 * c + 3:
                pend = c
        mFin(pend)
    P.finalize()


def ph_c1(nc, G, l):
    S = G.S
    P = Prog(nc, G.pool)
    TT = 256
    J = 2
    ident, _ = mk_ident(P)
    K = norm_alloc(P, J)
    Wg = P.sb("Wg", [128, 8, 4096], BF16)
    Wb = P.sb("Wb", [128, 4, 2, D], BF16)
    Wo = P.sb("Wo", [128, 8, D], BF16)
    for kc in range(8):
        load_w(P, Wg, Wg[:, kc, :], G.w_in[l, kc * 128:(kc + 1) * 128, 3076:3076 + 4096])
    for br in range(4):
        for k2 in range(2):
            load_w(P, Wb, Wb[:, br, k2, :], G.w_branch[l, br, k2 * 128:(k2 + 1) * 128, :])
    for kc in range(8):
        load_w(P, Wo, Wo[:, kc, :], G.w_o[l, kc * 128:(kc + 1) * 128, :])
    gb = load_gain(P, "gb", G.norm2_g[l])
    hts = [P.sb("ht%d" % i, [128, 8, TT], BF16) for i in range(2)]
    yts = [P.sb("yt%d" % i, [128, 4, 2, TT], BF16) for i in range(2)]
    xos = [P.sb("xo%d" % i, [128, J, D], F32) for i in range(2)]
    h2s = [P.sb("h2s%d" % i, [128, 8, TT], BF16) for i in range(2)]
    mT = P.sb("mT", [128, 8, TT], BF16)
    sg = [P.sb("sg%d" % i, [128, TT], F32) for i in range(2)]
    tmp = [P.sb("tmp%d" % i, [128, TT], F32) for i in range(2)]
    mg = [P.sb("mg%d" % i, [128, TT], F32) for i in range(2)]
    Gp = [P.ps("Gp%d" % i, [128, TT]) for i in range(2)]
    Pj = [P.ps("Pj%d" % i, [128, TT]) for i in range(2)]
    Op = [P.ps("Op%d" % i, [128, 512]) for i in range(2)]
    xsrc = G.x_in if l == 0 else G.xa
    hTv = rep(G.hT, "(k p) s -> p k s", p=128)
    h2v = rep(G.h2T, "(k p) s -> p k s", p=128)
    cnt = 0
    oc = 0
    pend = None
    for t in range(S // TT):
        t0 = t * TT
        ht = hts[t % 2]
        yt = yts[t % 2]
        xo = xos[t % 2]
        P.dma(ht[:], hTv[:, :, t0:t0 + TT], w=[ht])
        for br in range(4):
            P.dma(yt[:, br, :, :], rep(G.yT[br], "(k p) s -> p k s", p=128)[:, :, t0:t0 + TT], w=[yt])
        P.dma(xo[:], rep(xsrc[t0:t0 + TT, :], "(j p) d -> p j d", p=128), w=[xo])
        for dmc in range(8):
            m = mg[dmc % 2]
            for br in range(4):
                gp = Gp[cnt % 2]
                pj = Pj[cnt % 2]
                s_ = sg[cnt % 2]
                tm = tmp[cnt % 2]
                cnt += 1
                cg = br * 1024 + dmc * 128
                for kc in range(8):
                    P.mm(gp[:], Wg[:, kc, cg:cg + 128], ht[:, kc, :], start=(kc == 0), stop=(kc == 7), r=[Wg, ht], w=[gp])
                for k2 in range(2):
                    P.mm(pj[:], Wb[:, br, k2, dmc * 128:(dmc + 1) * 128], yt[:, br, k2, :], start=(k2 == 0), stop=(k2 == 1),
                         r=[Wb, yt], w=[pj])
                P.act(s_[:], gp[:], AF.Sigmoid, r=[gp], w=[s_])
                if br == 0:
                    P.tt(m[:], s_[:], pj[:], ALU.mult, r=[s_, pj], w=[m])
                else:
                    P.tt(tm[:], s_[:], pj[:], ALU.mult, r=[s_, pj], w=[tm])
                    if br < 3:
                        P.tt(m[:], m[:], tm[:], ALU.add, r=[m, tm], w=[m])
                    else:
                        P.tt(mT[:, dmc, :], m[:], tm[:], ALU.add, r=[m, tm], w=[mT])
        if pend is not None:
            h2 = h2s[pend[0] % 2]
            norm_p2(P, K, ident, h2)
            P.dma(h2v[:, :, pend[1]:pend[1] + TT], h2[:], r=[h2], eng="pool")
            pend = None
        for j in range(J):
            for hf in range(2):
                op_ = Op[oc % 2]
                oc += 1
                for dmc in range(8):
                    P.mm(op_[:], mT[:, dmc, j * 128:(j + 1) * 128], Wo[:, dmc, hf * 512:(hf + 1) * 512], start=(dmc == 0),
                         stop=(dmc == 7), r=[mT, Wo], w=[op_])
                P.tt(xo[:, j, hf * 512:(hf + 1) * 512], xo[:, j, hf * 512:(hf + 1) * 512], op_[:], ALU.add, r=[xo, op_], w=[xo])
        P.dma(rep(G.xm[t0:t0 + TT, :], "(j p) d -> p j d", p=128), xo[:], r=[xo], eng="pool")
        norm_p1(P, K, xo, gb)
        pend = (t, t0)
    if pend is not None:
        h2 = h2s[pend[0] % 2]
        norm_p2(P, K, ident, h2)
        P.dma(h2v[:, :, pend[1]:pend[1] + TT], h2[:], r=[h2], eng="pool")
    P.finalize()


def ph_c2(nc, G, l):
    S = G.S
    P = Prog(nc, G.pool)
    TT = 256
    J = 2
    NF = FF // 128
    last = (l == G.L - 1)
    ident, _ = mk_ident(P)
    K = norm_alloc(P, J)
    SW = 1408
    Wgu = [P.sb("Wgu%d" % q, [128, 8, SW], BF16) for q in range(4)]
    Wd = P.sb("Wd", [128, NF, D], BF16)
    for q in (0, 2, 1, 3):
        for kc in range(8):
            c0 = q * SW
            load_w(P, Wgu[q], Wgu[q][:, kc, :], G.w_gate_up[l, kc * 128:(kc + 1) * 128, c0:c0 + SW])
    for fc in range(NF):
        load_w(P, Wd, Wd[:, fc, :], G.w_down[l, fc * 128:(fc + 1) * 128, :])
    if not last:
        gb = load_gain(P, "gb", G.norm1_g[l + 1])
    if last:
        fgb = P.sb("fgb", [128, D], F32)
        P.dma(fgb[:], G.final_g.partition_broadcast(128), w=[fgb])
    h2s = [P.sb("h2s%d" % i, [128, 8, TT], BF16) for i in range(2)]
    xos = [P.sb("xo%d" % i, [128, J, D], F32) for i in range(2)]
    aT = P.sb("aT", [128, NF, TT], BF16)
    hto = P.sb("hto", [128, 8, TT], BF16)
    sg = [P.sb("sg%d" % i, [128, TT], F32) for i in range(2)]
    Gp = [P.ps("Gp%d" % i, [128, TT]) for i in range(2)]
    Up = [P.ps("Up%d" % i, [128, TT]) for i in range(2)]
    Op = [P.ps("Op%d" % i, [128, 512]) for i in range(2)]
    hTv = rep(G.hT, "(k p) s -> p k s", p=128)
    h2v = rep(G.h2T, "(k p) s -> p k s", p=128)
    cnt = 0
    oc = 0
    pend = None
    for t in range(S // TT):
        t0 = t * TT
        h2 = h2s[t % 2]
        xo = xos[t % 2]
        P.dma(h2[:], h2v[:, :, t0:t0 + TT], w=[h2])
        P.dma(xo[:], rep(G.xm[t0:t0 + TT, :], "(j p) d -> p j d", p=128), w=[xo])
        for fc in range(NF):
            gp = Gp[cnt % 2]
            up = Up[cnt % 2]
            s_ = sg[cnt % 2]
            cnt += 1
            wq = Wgu[fc // 11]
            wu = Wgu[2 + fc // 11]
            fo = (fc % 11) * 128
            for kc in range(8):
                P.mm(gp[:], wq[:, kc, fo:fo + 128], h2[:, kc, :], start=(kc == 0), stop=(kc == 7), r=[wq, h2], w=[gp])
            for kc in range(8):
                P.mm(up[:], wu[:, kc, fo:fo + 128], h2[:, kc, :], start=(kc == 0), stop=(kc == 7), r=[wu, h2], w=[up])
            P.act(s_[:], gp[:], AF.Silu, r=[gp], w=[s_])
            P.tt(aT[:, fc, :], s_[:], up[:], ALU.mult, r=[s_, up], w=[aT])
        if pend is not None:
            norm_p2(P, K, ident, hto)
            P.dma(hTv[:, :, pend:pend + TT], hto[:], r=[hto], eng="pool")
            pend = None
        for j in range(J):
            for hf in range(2):
                op_ = Op[oc % 2]
                oc += 1
                for fc in range(NF):
                    P.mm(op_[:], aT[:, fc, j * 128:(j + 1) * 128], Wd[:, fc, hf * 512:(hf + 1) * 512], start=(fc == 0),
                         stop=(fc == NF - 1), r=[aT, Wd], w=[op_])
                P.tt(xo[:, j, hf * 512:(hf + 1) * 512], xo[:, j, hf * 512:(hf + 1) * 512], op_[:], ALU.add, r=[xo, op_], w=[xo])
        if not last:
            P.dma(rep(G.xa[t0:t0 + TT, :], "(j p) d -> p j d", p=128), xo[:], r=[xo], eng="pool")
            norm_p1(P, K, xo, gb)
            pend = t0
        else:
            norm_stats(P, K, xo, J)
            for j in range(J):
                P.stt(xo[:, j, :], xo[:, j, :], K.rstd[:, j:j + 1], fgb[:], ALU.mult, ALU.mult, r=[xo, K.rstd, fgb], w=[xo])
            P.dma(rep(G.out[t0:t0 + TT, :], "(j p) d -> p j d", p=128), xo[:], r=[xo], eng="pool")
    if pend is not None:
        norm_p2(P, K, ident, hto)
        P.dma(hTv[:, :, pend:pend + TT], hto[:], r=[hto], eng="pool")
    P.finalize()


W_SPECS = [
    ("norm1_g", lambda L: [L, D]), ("w_in", lambda L: [L, D, NIN]), ("conv_a_w", lambda L: [L, 3, 256]),
    ("ssm_conv_w", lambda L: [L, 4, 512]), ("ssm_conv_b", lambda L: [L, 512]), ("ssm_dt_bias", lambda L: [L, 4]),
    ("ssm_a_log", lambda L: [L, 4]), ("ssm_d", lambda L: [L, 4]), ("ssm_norm_g", lambda L: [L, 256]),
    ("w_branch", lambda L: [L, 4, 256, D]), ("w_o", lambda L: [L, D, D]), ("norm2_g", lambda L: [L, D]),
    ("w_gate_up", lambda L: [L, D, 2 * FF]), ("w_down", lambda L: [L, FF, D]), ("final_g", lambda L: [D]),
]


def build(S, L, dbg=(), phases=None):
    nc = bass.Bass("TRN2", target_bir_lowering=False)
    G = NS()
    G.S = S
    G.L = L
    G.pool = SemPool(nc)
    G.x_in = nc.dram_tensor("x", [S, D], F32, kind="ExternalInput").ap()
    for name, shp in W_SPECS:
        setattr(G, name, nc.dram_tensor(name, shp(L), F32, kind="ExternalInput").ap())
    G.out = nc.dram_tensor("out", [S, D], F32, kind="ExternalOutput").ap()

    def scr(name, shape, dt):
        kind = "ExternalOutput" if name in dbg else "Internal"
        t = nc.dram_tensor(name, list(shape), dt, kind=kind).ap()
        setattr(G, name, t)
        return t

    scr("hT", [D, S], BF16)
    scr("h2T", [D, S], BF16)
    scr("xa", [S, D], F32)
    scr("xm", [S, D], F32)
    scr("uaT", [768, S], BF16)
    scr("qbT", [256, S], BF16)
    scr("kbT", [256, S], BF16)
    scr("vb", [S, 256], BF16)
    scr("zT", [256, S], BF16)
    scr("xbcT", [512, S], BF16)
    scr("dt", [S, 4], F32)
    scr("qdT", [256, S], BF16)
    scr("kdT", [256, S], BF16)
    scr("vd", [S, 256], BF16)
    scr("yT", [4, 256, S], BF16)
    run = (lambda p: True) if phases is None else (lambda p: p in phases)
    if run("norm0"):
        ph_norm0(nc, G)
    for l in range(L):
        if run("proj"):
            ph_proj(nc, G, l)
        if run("conva"):
            ph_conva(nc, G, l)
        if run("ssd"):
            ph_ssd(nc, G, l)
        if run("sb"):
            ph_sb(nc, G, l)
        if run("moba"):
            ph_moba(nc, G, l)
        if run("c1"):
            ph_c1(nc, G, l)
        if run("c2"):
            ph_c2(nc, G, l)
    G.pool.es.close()
    return nc


from concourse.bass_utils import run_bass_kernel_spmd

_W_NAMES = [n for n, _ in W_SPECS]


def kernel(**inputs):
    x = np.ascontiguousarray(np.asarray(inputs["x"], dtype=np.float32))
    B, S, _ = x.shape
    L = int(np.asarray(inputs["w_in"]).shape[0])
    assert B == 8
    nc = build(S, L)
    w = {n: np.ascontiguousarray(np.asarray(inputs[n], dtype=np.float32)) for n in _W_NAMES}
    in_maps = []
    for b in range(B):
        m = {"x": x[b]}
        m.update(w)
        in_maps.append(m)
    res = run_bass_kernel_spmd(nc, in_maps, core_ids=list(range(B)))
    return np.stack([np.asarray(res.results[b]["out"], dtype=np.float32) for b in range(B)], axis=0)
```

```python
import numpy as np
import concourse.bass as bass
import concourse.mybir as mybir
from contextlib import ExitStack

F32 = mybir.dt.float32
BF16 = mybir.dt.bfloat16
AF = mybir.ActivationFunctionType
ALU = mybir.AluOpType
AX = mybir.AxisListType

ENGS = ("pe", "act", "dve", "pool", "sp")
NDSEM = 8


class Res:
    __slots__ = ("name", "w", "r")

    def __init__(self, name=""):
        self.name = name
        self.w = None
        self.r = {}


class T:
    def __init__(self, h, name=""):
        self.h = h
        self.res = Res(name)

    def __getitem__(self, idx):
        return self.h[idx]


class Op:
    __slots__ = ("eng", "fn", "deps", "needs_inc", "tok", "is_dma", "gi")


class SemPool:
    def __init__(self, nc):
        self.nc = nc
        self.es = ExitStack()
        self.sems = {}
        self.counts = {}
        self.prev = []

    def get(self, key):
        if key not in self.sems:
            self.sems[key] = self.es.enter_context(self.nc.semaphore("sem_" + key))
            self.counts[key] = 0
        return self.sems[key]


class Prog:
    _uid = [0]

    def __init__(self, nc, pool=None):
        self.nc = nc
        self.pool = pool if pool is not None else SemPool(nc)
        Prog._uid[0] += 1
        self.pfx = "p%d_" % Prog._uid[0]
        self.ops = []
        self.last = {e: None for e in ENGS}
        self.pending = {e: [] for e in ENGS}
        self.dma_ops = []
        self.es = ExitStack()
        self.sems = {}
        self.dsems = {}
        self.ndma = {e: 0 for e in ENGS}
        self.last_on_dsem = {}

    def sb(self, name, shape, dt, stack=None):
        h = (stack or self.es).enter_context(self.nc.sbuf_tensor(self.pfx + name, list(shape), dt))
        return T(h, name)

    def ps(self, name, shape, dt=F32, stack=None):
        h = (stack or self.es).enter_context(self.nc.psum_tensor(self.pfx + name, list(shape), dt))
        return T(h, name)

    def op(self, eng, fn, r=(), w=(), dma=False):
        o = Op()
        o.eng = eng
        o.fn = fn
        o.is_dma = dma
        o.needs_inc = dma
        o.tok = None
        o.gi = len(self.ops)
        deps = {}
        for t in r:
            res = t.res if isinstance(t, T) else t
            if res.w is not None:
                deps[id(res.w)] = res.w
        for t in w:
            res = t.res if isinstance(t, T) else t
            if res.w is not None:
                deps[id(res.w)] = res.w
            for d in res.r.values():
                deps[id(d)] = d
        for d in self.pending[eng]:
            deps[id(d)] = d
        self.pending[eng] = []
        if dma:
            k = (eng, self.ndma[eng] % NDSEM)
            self.ndma[eng] += 1
            prev = self.last_on_dsem.get(k)
            if prev is not None:
                deps[id(prev)] = prev
            self.last_on_dsem[k] = o
            o.tok = k
        dl = []
        for d in deps.values():
            if d is o:
                continue
            if eng == "pe" and d.eng == "pe" and not d.is_dma:
                continue
            d.needs_inc = True
            dl.append(d)
        o.deps = dl
        for t in r:
            res = t.res if isinstance(t, T) else t
            key = ("dma", o.gi) if dma else eng
            res.r[key] = o
        for t in w:
            res = t.res if isinstance(t, T) else t
            res.w = o
            res.r = {}
        self.ops.append(o)
        self.last[eng] = o
        if dma:
            self.dma_ops.append(o)
        return o

    def barrier(self):
        outs = [o for o in self.last.values() if o is not None]
        outs += list(self.last_on_dsem.values())
        for e in ENGS:
            self.pending[e] = list(outs)
        for o in outs:
            o.needs_inc = True

    def dma(self, out_ap, in_ap, r=(), w=(), eng="sp", **kw):
        return self.op(eng, lambda e: e.dma_start(out=out_ap, in_=in_ap, **kw), r=r, w=w, dma=True)

    def mm(self, out_ap, lhsT, rhs, start=True, stop=True, r=(), w=(), sgc=False):
        if sgc:
            return self.op("pe", lambda e: e.matmul(out_ap, lhsT, rhs, start=start, stop=stop, skip_group_check=True), r=r, w=w)
        return self.op("pe", lambda e: e.matmul(out_ap, lhsT, rhs, start=start, stop=stop), r=r, w=w)

    def tr(self, out_ap, in_ap, ident, r=(), w=()):
        return self.op("pe", lambda e: e.transpose(out_ap, in_ap, ident), r=r, w=w)

    def act(self, out_ap, in_ap, func, r=(), w=(), **kw):
        return self.op("act", lambda e: e.activation(out_ap, in_ap, func, **kw), r=r, w=w)


    def stt(self, out, in0, scalar, in1, op0, op1, r=(), w=(), accum_out=None):
        if accum_out is None:
            return self.op("dve", lambda e: e.scalar_tensor_tensor(out=out, in0=in0, scalar=scalar, in1=in1, op0=op0, op1=op1), r=r, w=w)
        return self.op("dve", lambda e: e.scalar_tensor_tensor(out=out, in0=in0, scalar=scalar, in1=in1, op0=op0, op1=op1, accum_out=accum_out), r=r, w=w)

    def ts(self, out, in0, s1, s2, op0, op1=None, r=(), w=(), eng="dve"):
        if op1 is None:
            return self.op(eng, lambda e: e.tensor_scalar(out=out, in0=in0, scalar1=s1, scalar2=None, op0=op0), r=r, w=w)
        return self.op(eng, lambda e: e.tensor_scalar(out=out, in0=in0, scalar1=s1, scalar2=s2, op0=op0, op1=op1), r=r, w=w)

    def tt(self, out, in0, in1, op, r=(), w=(), eng="dve"):
        return self.op(eng, lambda e: e.tensor_tensor(out=out, in0=in0, in1=in1, op=op), r=r, w=w)

    def copy(self, out, in_, r=(), w=(), eng="dve"):
        if eng == "act":
            return self.op("act", lambda e: e.copy(out=out, in_=in_), r=r, w=w)
        return self.op(eng, lambda e: e.tensor_copy(out=out, in_=in_), r=r, w=w)

    def memset(self, ap, val, w=(), eng="pool"):
        return self.op(eng, lambda e: e.memset(ap, val), w=w)

    def affine(self, t, pattern, base, cm, cmp, fill, val, eng="pool"):
        self.memset(t[:], val, w=[t])
        return self.op("pool", lambda e: e.affine_select(out=t[:], in_=t[:], pattern=pattern, compare_op=cmp, fill=fill, base=base, channel_multiplier=cm), r=[t], w=[t])

    def finalize(self):
        nc = self.nc
        es = self.es
        self.barrier()
        self.op("sp", None, r=(), w=())
        pool = self.pool
        for o in self.ops:
            if o.is_dma:
                key = "d_%s%d" % o.tok
                sem = pool.get(key)
                pool.counts[key] += 16
                o.tok = (sem, pool.counts[key], 16)
            elif o.needs_inc:
                key = "e_" + o.eng
                sem = pool.get(key)
                pool.counts[key] += 1
                o.tok = (sem, pool.counts[key], 1)
        streams = {e: [] for e in ENGS}
        seen = {e: {} for e in ENGS}
        first = {e: True for e in ENGS}
        for o in self.ops:
            waits = {}
            deptoks = [d.tok for d in o.deps]
            if first[o.eng]:
                first[o.eng] = False
                deptoks += [(sem, val, 0) for sem, val in pool.prev]
            for sem, val, _ in deptoks:
                sid = id(sem)
                if seen[o.eng].get(sid, 0) >= val:
                    continue
                if sid not in waits or waits[sid][1] < val:
                    waits[sid] = (sem, val)
            for sid, (sem, val) in waits.items():
                seen[o.eng][sid] = val
            streams[o.eng].append((list(waits.values()), o.fn, o.tok if (o.is_dma or o.needs_inc) else None))
        pool.prev = [(pool.sems[k], pool.counts[k]) for k in pool.sems if pool.counts[k] > 0]
        self.stats = {e: len(streams[e]) for e in ENGS}

        def run(stream):
            def f(e):
                for waits, fn, tok in stream:
                    for sem, val in waits:
                        e.wait_ge(sem, val)
                    if fn is None:
                        continue
                    ins = fn(e)
                    if tok is not None:
                        ins.then_inc(tok[0], tok[2])
            return f

        with nc.Block() as block:
            block.tensor(run(streams["pe"]))
            block.scalar(run(streams["act"]))
            block.vector(run(streams["dve"]))
            block.gpsimd(run(streams["pool"]))
            block.sync(run(streams["sp"]))
        es.close()


D = 1024
NIN = 7172
FF = 2816
BIG = 30000.0
EPS = 1e-6


class NS:
    pass


def rep(ap, pattern, **kw):
    return ap.rearrange(pattern, **kw)


def mk_ident(P, name="ident"):
    f = P.sb(name + "_f", [128, 128], F32)
    P.affine(f, [[-1, 128]], 0, 1, ALU.is_equal, 0.0, 1.0)
    b = P.sb(name, [128, 128], BF16)
    P.copy(b[:], f[:], r=[f], w=[b])
    return b, f


def norm_alloc(P, J):
    K = NS()
    K.J = J
    K.junk = P.sb("n_junk", [128, D], F32)
    K.ss = P.sb("n_ss", [128, 4], F32)
    K.ms = P.sb("n_ms", [128, 4], F32)
    K.rstd = P.sb("n_rstd", [128, 4], F32)
    K.mh = P.sb("n_mh", [128, 4], F32)
    P.memset(K.mh[:], -0.5, w=[K.mh])
    K.xn = P.sb("n_xn", [128, J, D], BF16)
    K.pT = [P.ps("n_pT%d" % i, [128, D], BF16) for i in range(2)]
    K.cnt = 0
    return K


def norm_stats(P, K, xt, J):
    for j in range(J):
        P.stt(K.junk[:], xt[:, j, :], 1.0, xt[:, j, :], ALU.mult, ALU.mult, r=[xt], w=[K.junk, K.ss],
              accum_out=K.ss[:, j:j + 1])
    P.ts(K.ms[:, 0:J], K.ss[:, 0:J], 1.0 / D, EPS, ALU.mult, ALU.add, r=[K.ss], w=[K.ms])
    P.tt(K.rstd[:, 0:J], K.ms[:, 0:J], K.mh[:, 0:J], ALU.pow, r=[K.ms, K.mh], w=[K.rstd], eng="pool")


def norm_p1(P, K, xt, gb=None):
    J = K.J
    norm_stats(P, K, xt, J)
    for j in range(J):
        if gb is None:
            P.act(K.xn[:, j, :], xt[:, j, :], AF.Copy, r=[xt, K.rstd], w=[K.xn], scale=K.rstd[:, j:j + 1])
        else:
            P.stt(K.xn[:, j, :], xt[:, j, :], K.rstd[:, j:j + 1], gb[:], ALU.mult, ALU.mult, r=[xt, K.rstd, gb], w=[K.xn])


def norm_p2(P, K, ident, hTt):
    J = K.J
    for j in range(J):
        pT = K.pT[K.cnt % 2]
        K.cnt += 1
        for kc in range(8):
            P.tr(pT[:, kc * 128:(kc + 1) * 128], K.xn[:, j, kc * 128:(kc + 1) * 128], ident[:], r=[K.xn, ident], w=[pT])
        P.copy(hTt[:, :, j * 128:(j + 1) * 128], rep(pT[:, :], "p (k t) -> p k t", k=8), r=[pT], w=[hTt])


def norm_tt(P, K, ident, xt, hTt, gb=None):
    norm_p1(P, K, xt, gb)
    norm_p2(P, K, ident, hTt)


def load_gain(P, name, g_ap):
    t = P.sb(name, [128, D], F32)
    P.dma(t[:], g_ap.partition_broadcast(128), w=[t])
    return t


def load_w(P, dst, dst_sl, src_ap):
    P.dma(dst_sl, src_ap, w=[dst], eng="pool")


def load_w_cast(P, dst, dst_sl, src_ap, ncols, stage, si, gcol=None, eng="dve", r_extra=()):
    st = stage[si % len(stage)]
    P.dma(st[:, 0:ncols], src_ap, w=[st])
    if gcol is None:
        P.copy(dst_sl, st[:, 0:ncols], r=[st], w=[dst], eng=eng)
    elif eng == "act":
        P.act(dst_sl, st[:, 0:ncols], AF.Copy, r=[st] + list(r_extra), w=[dst], scale=gcol)
    else:
        P.ts(dst_sl, st[:, 0:ncols], gcol, None, ALU.mult, r=[st] + list(r_extra), w=[dst], eng=eng)


def ph_norm0(nc, G):
    S = G.S
    P = Prog(nc, G.pool)
    ident, _ = mk_ident(P)
    J = 4
    K = norm_alloc(P, J)
    xts = [P.sb("xt%d" % i, [128, J, D], F32) for i in range(2)]
    hts = [P.sb("ht%d" % i, [128, 8, J * 128], BF16) for i in range(2)]
    hTv = rep(G.hT, "(k p) s -> p k s", p=128)
    gb = load_gain(P, "gb", G.norm1_g[0])
    for t in range(S // (128 * J)):
        xt = xts[t % 2]
        ht = hts[t % 2]
        t0 = t * 128 * J
        P.dma(xt[:], rep(G.x_in[t0:t0 + 128 * J, :], "(j p) d -> p j d", p=128), w=[xt])
        norm_tt(P, K, ident, xt, ht, gb)
        P.dma(hTv[:, :, t0:t0 + 128 * J], ht[:], r=[ht], eng="pool")
    P.finalize()


def ph_proj(nc, G, l):
    S = G.S
    P = Prog(nc, G.pool)
    NC_A = 3076
    WA = P.sb("WA", [128, 8, NC_A], BF16)
    for kc in range(8):
        load_w(P, WA, WA[:, kc, :], G.w_in[l, kc * 128:(kc + 1) * 128, 0:NC_A])
    hts = [P.sb("ht%d" % i, [128, 8, 512], BF16) for i in range(2)]
    groups = [
        (G.uaT, 0, 6, 1.0),
        (G.qbT, 768, 2, 0.125),
        (G.kbT, 1024, 2, 1.0),
        (G.zT, 1536, 2, 1.0),
        (G.xbcT, 1792, 4, 1.0),
        (G.qdT, 2308, 2, 0.125),
        (G.kdT, 2564, 2, 1.0),
    ]
    stg = {}
    for gi, (dst, c0, nch, sc) in enumerate(groups):
        stg[gi] = [P.sb("stg%d_%d" % (gi, i), [128, nch, 512], BF16) for i in range(2)]
    vst = [P.sb("vst%d" % i, [128, 4, 512], BF16) for i in range(2)]
    dst_ = [P.sb("dtst%d" % i, [128, 4, 4], F32) for i in range(2)]
    pf = [P.ps("pf%d" % i, [128, 512]) for i in range(6)]
    pv = P.ps("pv", [128, 512])
    pd = P.ps("pd", [128, 4, 4])
    hTv = rep(G.hT, "(k p) s -> p k s", p=128)
    cnt = 0
    for t in range(S // 512):
        t0 = t * 512
        ht = hts[t % 2]
        P.dma(ht[:], hTv[:, :, t0:t0 + 512], w=[ht])
        for gi, (dst, c0, nch, sc) in enumerate(groups):
            sg = stg[gi][t % 2]
            for ch in range(nch):
                ps = pf[cnt % 6]
                for kc in range(8):
                    P.mm(ps[:], WA[:, kc, c0 + ch * 128:c0 + (ch + 1) * 128], ht[:, kc, :], start=(kc == 0), stop=(kc == 7),
                         r=[WA, ht], w=[ps])
                if cnt % 2 == 0:
                    P.act(sg[:, ch, :], ps[:], AF.Copy, r=[ps], w=[sg], scale=sc)
                else:
                    P.ts(sg[:, ch, :], ps[:], sc, None, ALU.mult, r=[ps], w=[sg])
                cnt += 1
            P.dma(rep(dst, "(c p) s -> p c s", p=128)[:, :, t0:t0 + 512], sg[:], r=[sg], eng="pool")
        vs = vst[t % 2]
        ds = dst_[t % 2]
        for j in range(4):
            for half, c0 in enumerate((1280, 2820)):
                for kc in range(8):
                    P.mm(pv[:, half * 256:(half + 1) * 256], ht[:, kc, j * 128:(j + 1) * 128], WA[:, kc, c0:c0 + 256],
                         start=(kc == 0), stop=(kc == 7), r=[WA, ht], w=[pv])
            P.copy(vs[:, j, :], pv[:], r=[pv], w=[vs], eng=("dve" if j % 2 == 0 else "act"))
            for kc in range(8):
                P.mm(pd[:, j, :], ht[:, kc, j * 128:(j + 1) * 128], WA[:, kc, 2304:2308], start=(kc == 0), stop=(kc == 7),
                     r=[WA, ht], w=[pd])
        P.copy(ds[:], pd[:], r=[pd], w=[ds])
        P.dma(rep(G.vb[t0:t0 + 512, :], "(j p) c -> p j c", p=128), vs[:, :, 0:256], r=[vs], eng="pool")
        P.dma(rep(G.vd[t0:t0 + 512, :], "(j p) c -> p j c", p=128), vs[:, :, 256:512], r=[vs], eng="pool")
        P.dma(rep(G.dt[t0:t0 + 512, :], "(j p) h -> p j h", p=128), ds[:], r=[ds], eng="pool")
    P.finalize()


def ph_conva(nc, G, l):
    S = G.S
    P = Prog(nc, G.pool)
    cw = P.sb("cw", [128, 3, 2], F32)
    for k in range(3):
        P.dma(cw[:, k, :], rep(G.conv_a_w[l, k], "(c p) -> p c", p=128), w=[cw], allow_slow_non_contiguous=True)
    uv = rep(G.uaT, "(c p) s -> p c s", p=128)
    TT = 512
    ins = [P.sb("cin%d" % i, [128, 6, TT + 2], BF16) for i in range(2)]
    for i in range(2):
        P.memset(ins[i][:, :, 0:2], 0.0, w=[ins[i]])
    pt = P.sb("cp", [128, TT + 2], F32)
    acc = P.sb("cacc", [128, TT], F32)
    ys = [P.sb("cy%d" % i, [128, 2, TT], BF16) for i in range(2)]
    for t in range(S // TT):
        t0 = t * TT
        it = ins[t % 2]
        if t == 0:
            P.dma(it[:, :, 2:TT + 2], uv[:, :, 0:TT], w=[it])
        else:
            P.dma(it[:, :, :], uv[:, :, t0 - 2:t0 + TT], w=[it])
        y = ys[t % 2]
        for fc in range(2):
            P.tt(pt[:], it[:, fc, :], it[:, 4 + fc, :], ALU.mult, r=[it], w=[pt])
            P.ts(acc[:], pt[:, 2:TT + 2], cw[:, 2, fc:fc + 1], None, ALU.mult, r=[pt, cw], w=[acc])
            P.stt(acc[:], pt[:, 1:TT + 1], cw[:, 1, fc:fc + 1], acc[:], ALU.mult, ALU.add, r=[pt, cw, acc], w=[acc])
            P.stt(acc[:], pt[:, 0:TT], cw[:, 0, fc:fc + 1], acc[:], ALU.mult, ALU.add, r=[pt, cw, acc], w=[acc])
            P.tt(y[:, fc, :], acc[:], it[:, 2 + fc, 2:TT + 2], ALU.mult, r=[acc, it], w=[y])
        P.dma(rep(G.yT[0], "(c p) s -> p c s", p=128)[:, :, t0:t0 + TT], y[:], r=[y], eng="pool")
    P.finalize()


def ph_ssd(nc, G, l):
    S = G.S
    P = Prog(nc, G.pool)
    T_ = 256
    ident, identf = mk_ident(P)
    U32 = P.sb("U32", [128, 128], F32)
    P.affine(U32, [[-1, 128]], 0, 1, ALU.is_gt, 0.0, 1.0)
    ones32 = P.sb("ones32", [128, 128], F32)
    P.memset(ones32[:], 1.0, w=[ones32])
    onesb = P.sb("onesb", [128, 128], BF16)
    P.memset(onesb[:], 1.0, w=[onesb])
    L0 = P.sb("L0", [128, 256], F32)
    P.affine(L0, [[1, 256]], 0, -1, ALU.is_ge, 0.0, 1.0)
    NEG0 = P.sb("NEG0", [128, 256], F32)
    P.affine(NEG0, [[1, 256]], 0, -1, ALU.is_ge, -BIG, 0.0)
    mh = P.sb("mh", [128, 256], F32)
    P.memset(mh[:], -0.5, w=[mh])
    cw = P.sb("cw", [128, 4, 4], F32)
    for k in range(4):
        P.dma(cw[:, k, :], rep(G.ssm_conv_w[l, k], "(c p) -> p c", p=128), w=[cw], allow_slow_non_contiguous=True)
    cb = P.sb("cb", [128, 4], F32)
    P.dma(cb[:], rep(G.ssm_conv_b[l], "(c p) -> p c", p=128), w=[cb], allow_slow_non_contiguous=True)
    dtb = P.sb("dtb", [128, 4], F32)
    P.dma(dtb[:], G.ssm_dt_bias[l].partition_broadcast(128), w=[dtb])
    Arow = P.sb("Arow", [128, 4], F32)
    P.dma(Arow[:], G.ssm_a_log[l].partition_broadcast(128), w=[Arow])
    P.act(Arow[:], Arow[:], AF.Exp, r=[Arow], w=[Arow])
    P.ts(Arow[:], Arow[:], -1.0, None, ALU.mult, r=[Arow], w=[Arow])
    Dcol = P.sb("Dcol", [128, 2], F32)
    for g in range(2):
        for e_ in range(2):
            P.dma(Dcol[e_ * 64:(e_ + 1) * 64, g:g + 1], G.ssm_d[l, 2 * g + e_:2 * g + e_ + 1].partition_broadcast(64), w=[Dcol])
    ng = P.sb("ng", [128, 2], F32)
    P.dma(ng[:], rep(G.ssm_norm_g[l], "(g p) -> p g", p=128), w=[ng], allow_slow_non_contiguous=True)

    XIN = [P.sb("XIN%d" % i, [128, 4, T_ + 3], BF16) for i in range(2)]
    ZIN = [P.sb("ZIN%d" % i, [128, 2, T_], BF16) for i in range(2)]
    DTIN = [P.sb("DTIN%d" % i, [128, 2, 4], F32) for i in range(2)]
    P.memset(XIN[0][:, :, 0:3], 0.0, w=[XIN[0]])
    acc = [P.sb("acc%d" % i, [128, T_], F32) for i in range(4)]
    xcs = [[P.sb("xc%d_%d" % (j, i), [128, T_], BF16) for i in range(4)] for j in range(2)]
    dts = P.sb("dts", [128, 2, 4], F32)
    dte_ = P.sb("dte_", [128, 2, 4], F32)
    dsps = [P.sb("dsp%d" % j, [128, 2, 4], F32) for j in range(2)]
    a_ts = [P.sb("a_t%d" % j, [128, 2, 4], F32) for j in range(2)]
    Xpads = [[P.sb("Xpad%d_%d" % (j, i), [128, 2, 2, 128], BF16) for i in range(2)] for j in range(2)]
    for j in range(2):
        for i in range(2):
            P.memset(Xpads[j][i][:], 0.0, w=[Xpads[j][i]], eng=("dve" if i == 0 else "pool"))
    Btoks = [[P.sb("Btok%d_%d" % (j, i), [128, 128], BF16) for i in range(2)] for j in range(2)]
    Xdte = [P.sb("Xdte%d" % i, [128, 256], BF16) for i in range(2)]
    aL0 = [P.sb("aL0_%d" % i, [128, 256], F32) for i in range(2)]
    aL1 = [P.sb("aL1_%d" % i, [128, 128], F32) for i in range(2)]
    Dt = [P.sb("Dt%d" % i, [128, 384], F32) for i in range(2)]
    Et = [P.sb("Et%d" % i, [128, 256], F32) for i in range(4)]
    Mt = [P.sb("Mt%d" % i, [128, 384], BF16) for i in range(2)]
    Cpt = [P.sb("Cpt%d" % i, [128, 256], BF16) for i in range(2)]
    H32 = P.sb("H32", [128, 2, 64], F32)
    Hpad = P.sb("Hpad", [128, 2, 128], BF16)
    P.memset(H32[:], 0.0, w=[H32])
    P.memset(Hpad[:], 0.0, w=[Hpad])
    yf = P.sb("yf", [128, 256], F32)
    sz = P.sb("sz", [128, 256], F32)
    gt = P.sb("gt", [128, 256], F32)
    sq = P.sb("sq", [128, 256], BF16)
    rs = P.sb("rs", [128, 256], F32)
    yst = [P.sb("yst%d" % i, [128, 2, 256], BF16) for i in range(2)]
    segp = [P.ps("segp%d" % i, [128, 384]) for i in range(2)]
    csbp = P.ps("csbp", [128, 256])
    Gp = [P.ps("Gp%d" % i, [128, 384]) for i in range(2)]
    Yg = [P.ps("Yg%d" % i, [128, 256]) for i in range(2)]
    misc = P.ps("misc", [128, 512])
    ptb = T(misc.h[:, 0:256].bitcast(BF16), "ptb")
    HSb = T(misc.h[:, 256:512], "HSb")
    ptb.res = misc.res
    HSb.res = misc.res

    xv = rep(G.xbcT, "(c p) s -> p c s", p=128)
    zv = rep(G.zT, "(g p) s -> p g s", p=128)
    yv = rep(G.yT[2], "(g p) s -> p g s", p=128)
    nchunks = S // T_
    hc = 0

    def front_parts(c):
        t0 = c * T_
        xin = XIN[c % 2]
        zin = ZIN[c % 2]
        dtin = DTIN[c % 2]
        xc = xcs[c % 2]
        dsp = dsps[c % 2]
        a_t = a_ts[c % 2]
        Xpad = Xpads[c % 2]
        Btok = Btoks[c % 2]

        def conv(ct):
            P.ts(acc[ct][:], xin[:, ct, 0:T_], cw[:, 0, ct:ct + 1], None, ALU.mult, r=[xin, cw], w=[acc[ct]])
            for k in range(1, 4):
                P.stt(acc[ct][:], xin[:, ct, k:k + T_], cw[:, k, ct:ct + 1], acc[ct][:], ALU.mult, ALU.add,
                      r=[xin, cw, acc[ct]], w=[acc[ct]])
            P.act(xc[ct][:], acc[ct][:], AF.Silu, r=[acc[ct], cb], w=[xc[ct]], bias=cb[:, ct:ct + 1])

        def xtr(g):
            for t in range(2):
                sl = ptb[:, (2 * t + g) * 128:(2 * t + g + 1) * 128]
                P.tr(sl, xc[g][:, t * 128:(t + 1) * 128], ident[:], r=[xc[g], ident], w=[ptb])
                for e_ in range(2):
                    P.ts(Xpad[t][:, g, e_, e_ * 64:(e_ + 1) * 64],
                         ptb[:, (2 * t + g) * 128 + e_ * 64:(2 * t + g) * 128 + (e_ + 1) * 64],
                         dsp[:, t, 2 * g + e_:2 * g + e_ + 1], None, ALU.mult, r=[ptb, dsp], w=[Xpad[t]])

        def p0():
            if c == 0:
                P.dma(xin[:, :, 3:T_ + 3], xv[:, :, 0:T_], w=[xin])
            else:
                P.dma(xin[:, :, :], xv[:, :, t0 - 3:t0 + T_], w=[xin])
            P.dma(zin[:], zv[:, :, t0:t0 + T_], w=[zin])
            P.dma(dtin[:], rep(G.dt[t0:t0 + T_, :], "(t p) h -> p t h", p=128), w=[dtin])
            for t in range(2):
                P.tt(dts[:, t, :], dtin[:, t, :], dtb[:], ALU.add, r=[dtin, dtb], w=[dts])
            P.act(dte_[:], dts[:], AF.Exp, r=[dts], w=[dte_])
            P.act(dsp[:], dte_[:], AF.Ln, r=[dte_], w=[dsp], bias=1.0)
            for t in range(2):
                P.tt(a_t[:, t, :], dsp[:, t, :], Arow[:], ALU.mult, r=[dsp, Arow], w=[a_t])
            conv(0)

        def p1():
            conv(1)
            xtr(0)

        def p2():
            conv(2)
            xtr(1)

        def p3():
            conv(3)
            for t in range(2):
                sl = ptb[:, t * 128:(t + 1) * 128]
                P.tr(sl, xc[2][:, t * 128:(t + 1) * 128], ident[:], r=[xc[2], ident], w=[ptb])
                P.copy(Btok[t][:], sl, r=[ptb], w=[Btok[t]], eng="act")

        return [p0, p1, p2, p3]

    for f_ in front_parts(0):
        f_()
    for c in range(nchunks):
        t0 = c * T_
        zin = ZIN[c % 2]
        xc = xcs[c % 2]
        dsp = dsps[c % 2]
        a_t = a_ts[c % 2]
        Xpad = Xpads[c % 2]
        Btok = Btoks[c % 2]
        if c > 0:
            for f_ in front_parts(c):
                f_()
        nxt = [None] * 4
        for g in range(2):
            gr = slice(g * 64, (g + 1) * 64)
            P.mm(Gp[g][:, 0:256], xc[2][gr, 0:128], xc[3][gr, 0:256], r=[xc[2], xc[3]], w=[Gp[g]])
            P.mm(Gp[g][:, 256:384], xc[2][gr, 128:256], xc[3][gr, 128:256], r=[xc[2], xc[3]], w=[Gp[g]])
        def h_pre(h):
            b = (hc + h) % 2
            P.ts(aL0[b][:], L0[:], a_t[:, 0, h:h + 1], None, ALU.mult, r=[L0, a_t], w=[aL0[b]])
            P.ts(aL1[b][:], L0[:, 0:128], a_t[:, 1, h:h + 1], None, ALU.mult, r=[L0, a_t], w=[aL1[b]])

        def h_seg(h):
            b = (hc + h) % 2
            sp_ = segp[b]
            P.mm(sp_[:, 0:256], U32[:], aL0[b][:], start=True, stop=False, r=[U32, aL0[b]], w=[sp_])
            P.mm(sp_[:, 128:256], ones32[:], aL1[b][:], start=False, stop=False, r=[ones32, aL1[b]], w=[sp_])
            P.mm(sp_[:, 0:256], identf[:], NEG0[:], start=False, stop=True, r=[identf, NEG0], w=[sp_])
            P.mm(sp_[:, 256:384], U32[:], aL1[b][:], start=True, stop=False, r=[U32, aL1[b]], w=[sp_])
            P.mm(sp_[:, 256:384], identf[:], NEG0[:, 0:128], start=False, stop=True, r=[identf, NEG0], w=[sp_])
            P.mm(csbp[:, 0:256], ones32[:], aL0[b][:], start=True, stop=False, r=[ones32, aL0[b]], w=[csbp])
            P.mm(csbp[:, 128:256], ones32[:], aL1[b][:], start=False, stop=True, r=[ones32, aL1[b]], w=[csbp])

        def h_act(h):
            b = (hc + h) % 2
            P.act(Dt[b][:], segp[b][:], AF.Exp, r=[segp[b]], w=[Dt[b]])
            P.act(Et[h][:], csbp[:], AF.Exp, r=[csbp], w=[Et[h]])

        def h_post(h):
            b = (hc + h) % 2
            g, e_ = h // 2, h % 2
            gr = slice(g * 64, (g + 1) * 64)
            P.tt(Mt[b][:], Dt[b][:], Gp[g][:], ALU.mult, r=[Dt[b], Gp[g]], w=[Mt[b]])
            P.tt(Cpt[b][gr, :], xc[3][gr, :], Et[h][gr, :], ALU.mult, r=[xc[3], Et[h]], w=[Cpt[b]])
            P.mm(Yg[g][:, 0:256], Xpad[0][:, g, e_, :], Mt[b][:, 0:256], start=(e_ == 0), stop=False,
                 r=[Xpad[0], Mt[b]], w=[Yg[g]])
            P.mm(Yg[g][:, 128:256], Xpad[1][:, g, e_, :], Mt[b][:, 256:384], start=False, stop=False,
                 r=[Xpad[1], Mt[b]], w=[Yg[g]])
            P.mm(Yg[g][:, 0:256], Hpad[gr, e_, :], Cpt[b][gr, :], start=False, stop=(e_ == 1),
                 r=[Hpad, Cpt[b]], w=[Yg[g]])
            P.ts(Xdte[0][:, h * 64:(h + 1) * 64], Xpad[0][:, g, e_, e_ * 64:(e_ + 1) * 64], Dt[b][:, 255:256], None,
                 ALU.mult, r=[Xpad[0], Dt[b]], w=[Xdte[0]])
            P.ts(Xdte[1][:, h * 64:(h + 1) * 64], Xpad[1][:, g, e_, e_ * 64:(e_ + 1) * 64], Dt[b][:, 383:384], None,
                 ALU.mult, r=[Xpad[1], Dt[b]], w=[Xdte[1]])

        h_pre(0)
        h_seg(0)
        for h in range(4):
            if h + 1 < 4:
                h_pre(h + 1)
            h_act(h)
            if h + 1 < 4:
                h_seg(h + 1)
            h_post(h)
            if nxt[h] is not None:
                nxt[h]()
        P.mm(HSb[:], Btok[0][:], Xdte[0][:], start=True, stop=False, r=[Btok[0], Xdte[0]], w=[HSb])
        P.mm(HSb[:], Btok[1][:], Xdte[1][:], start=False, stop=True, r=[Btok[1], Xdte[1]], w=[HSb])
        for g in range(2):
            gr = slice(g * 64, (g + 1) * 64)
            for e_ in range(2):
                h = 2 * g + e_
                P.stt(H32[gr, e_, :], H32[gr, e_, :], Et[h][gr, 255:256], HSb[gr, h * 64:(h + 1) * 64], ALU.mult, ALU.add,
                      r=[H32, Et[h], HSb], w=[H32])
                P.copy(Hpad[gr, e_, e_ * 64:(e_ + 1) * 64], H32[gr, e_, :], r=[H32], w=[Hpad])
        ys = yst[c % 2]
        for g in range(2):
            P.stt(yf[:], xc[g][:], Dcol[:, g:g + 1], Yg[g][:], ALU.mult, ALU.add, r=[xc[g], Dcol, Yg[g]], w=[yf])
            P.act(sz[:], zin[:, g, :], AF.Silu, r=[zin], w=[sz])
            P.tt(gt[:], yf[:], sz[:], ALU.mult, r=[yf, sz], w=[gt])
            P.tt(sq[:], gt[:], gt[:], ALU.mult, r=[gt], w=[sq])
            P.mm(csbp[:], onesb[:], sq[:], r=[onesb, sq], w=[csbp])
            P.ts(rs[:], csbp[:], 1.0 / 128, EPS, ALU.mult, ALU.add, r=[csbp], w=[rs])
            P.act(rs[:], rs[:], AF.Ln, r=[rs], w=[rs])
            P.act(rs[:], rs[:], AF.Exp, r=[rs], w=[rs], scale=-0.5)
            P.stt(ys[:, g, :], gt[:], ng[:, g:g + 1], rs[:], ALU.mult, ALU.mult, r=[gt, ng, rs], w=[ys])
        P.dma(yv[:, :, t0:t0 + T_], ys[:], r=[ys], eng="pool")
    P.finalize()


def ph_sb(nc, G, l):
    S = G.S
    P = Prog(nc, G.pool)
    NT = S // 512
    NK = S // 128
    ident, identf = mk_ident(P)
    IUf = P.sb("IUf", [128, 128], F32)
    P.affine(IUf, [[-1, 128]], 0, 1, ALU.is_ge, 0.0, -1.0)
    IUn = P.sb("IUn", [128, 128], BF16)
    P.copy(IUn[:], IUf[:], r=[IUf], w=[IUn])
    onesb = P.sb("onesb", [128, 2], BF16)
    P.memset(onesb[:], 1.0, w=[onesb])
    CM = []
    f = P.sb("CMf", [128, 512], F32)
    for d in range(4):
        P.affine(f, [[1, 512]], -128 * d, -1, ALU.is_gt, -BIG, 0.0)
        b = P.sb("CM%d" % d, [128, 512], BF16)
        P.copy(b[:], f[:], r=[f], w=[b])
        CM.append(b)
    qz = [[P.sb("qz%d_%d" % (j, i), [128, S], BF16) for i in range(2)] for j in range(2)]
    kz = [[P.sb("kz%d_%d" % (j, i), [128, S], BF16) for i in range(2)] for j in range(2)]
    for j in range(2):
        for i in range(2):
            P.memset(qz[j][i][64:128, :], 0.0, w=[qz[j][i]], eng=("dve" if i == 0 else "pool"))
            P.memset(kz[j][i][64:128, :], 0.0, w=[kz[j][i]], eng=("dve" if i == 0 else "pool"))
    v = P.sb("v", [128, NK, 256], BF16)
    vv = rep(G.vb, "(n p) c -> p n c", p=128)
    step = 2048
    for s0 in range(0, S, step):
        s1 = min(S, s0 + step)
        P.dma(v[:, s0 // 128:s1 // 128, :], vv[:, s0 // 128:s1 // 128, :], w=[v])

    def load_qk(hp):
        for i in range(2):
            h = 2 * hp + i
            for s0 in range(0, S, 4096):
                s1 = min(S, s0 + 4096)
                P.dma(qz[hp % 2][i][0:64, s0:s1], G.qbT[h * 64:(h + 1) * 64, s0:s1], w=[qz[hp % 2][i]])
                P.dma(kz[hp % 2][i][0:64, s0:s1], G.kbT[h * 64:(h + 1) * 64, s0:s1], w=[kz[hp % 2][i]])

    load_qk(0)
    load_qk(1)
    Zb = [[P.ps("Zb%d_%d" % (i, j), [128, 512]) for j in range(3)] for i in range(2)]
    OC = [P.ps("OC%d" % i, [128, 512]) for i in range(2)]
    Op = [T(rep(OC[i].h[:, 0:256], "p (q d) -> p q d", d=64), "Op%d" % i) for i in range(2)]
    csp = [T(OC[0].h[:, 256 + 4 * i:260 + 4 * i], "csp%d" % i) for i in range(2)]
    csp2 = T(OC[0].h[:, 256:264], "csp2")
    csp2.res = OC[0].res
    pTv = [T(OC[i].h[:, 384:512].bitcast(BF16), "pTv%d" % i) for i in range(2)]
    for i in range(2):
        Op[i].res = OC[i].res
        csp[i].res = OC[0].res
        pTv[i].res = OC[i].res
    Et = [[P.sb("Et%d_%d" % (i, j), [128, 512], F32) for j in range(2)] for i in range(2)]
    SPt = [[P.sb("SPt%d_%d" % (i, j), [128, 512], BF16) for j in range(2)] for i in range(2)]
    Pt = [[P.sb("Pt%d_%d" % (i, j), [128, 512], BF16) for j in range(2)] for i in range(2)]
    acc = [[P.sb("acc%d_%d" % (i, j), [128, 4, 64], F32) for j in range(2)] for i in range(2)]
    tmpa = [P.sb("tmpa%d" % i, [128, 4, 64], F32) for i in range(2)]
    accb = [P.sb("accb%d" % i, [128, 4, 64], BF16) for i in range(2)]
    dd2 = [P.sb("dd2_%d" % j, [128, 8], F32) for j in range(2)]
    dd = [[T(dd2[j].h[:, 4 * i:4 * i + 4], "dd%d_%d" % (i, j)) for j in range(2)] for i in range(2)]
    for i in range(2):
        for j in range(2):
            dd[i][j].res = dd2[j].res
    yst = [P.sb("yst%d" % i, [64, 512], BF16) for i in range(4)]
    yc = [0]

    for hp in range(2):
        its = [(c, n) for c in range(NT) for n in range(0, 4 * c + 4)]
        N = len(its)

        def dof(k):
            c, n = its[k]
            return max(0, n - 4 * c)

        def stA(k):
            c, n = its[k]
            co = 128 * dof(k)
            qs = slice(c * 512 + co, (c + 1) * 512)
            ks = slice(n * 128, (n + 1) * 128)
            diag = n >= 4 * c
            for i in range(2):
                Z = Zb[i][k % 3]
                P.mm(Z[:, co:512], kz[hp % 2][i][:, ks], qz[hp % 2][i][:, qs], start=True, stop=(not diag),
                     r=[kz[hp % 2][i], qz[hp % 2][i]], w=[Z])
                if diag:
                    P.mm(Z[:, co:512], ident[:], CM[n - 4 * c][:, co:512], start=False, stop=True,
                         r=[ident, CM[n - 4 * c]], w=[Z])

        def stB(k):
            co = 128 * dof(k)
            for i in range(2):
                P.act(Et[i][k % 2][:, co:512], Zb[i][k % 3][:, co:512], AF.Exp, r=[Zb[i][k % 3]], w=[Et[i][k % 2]])
            for i in range(2):
                P.act(SPt[i][k % 2][:, co:512], Et[i][k % 2][:, co:512], AF.Ln, r=[Et[i][k % 2]], w=[SPt[i][k % 2]], bias=1.0)

        def stC(k):
            co = 128 * dof(k)
            for i in range(2):
                Z = Zb[i][k % 3]
                P.mm(Z[:, co:512], IUn[:], SPt[i][k % 2][:, co:512], start=False, stop=True, r=[IUn, SPt[i][k % 2]], w=[Z], sgc=True)

        def stD(k):
            co = 128 * dof(k)
            for i in range(2):
                P.act(Pt[i][k % 2][:, co:512], Zb[i][k % 3][:, co:512], AF.Exp, r=[Zb[i][k % 3]], w=[Pt[i][k % 2]])

        def stE(k):
            c, n = its[k]
            d0 = dof(k)
            for i in range(2):
                h = 2 * hp + i
                for qi in range(d0, 4):
                    P.mm(Op[i][:, qi, :], Pt[i][k % 2][:, qi * 128:(qi + 1) * 128], v[:, n, h * 64:(h + 1) * 64],
                         r=[Pt[i][k % 2], v], w=[Op[i]])
                for qi in range(d0, 4):
                    P.mm(csp[i][:, qi:qi + 1], SPt[i][k % 2][:, qi * 128:(qi + 1) * 128], onesb[:, 0:1],
                         r=[SPt[i][k % 2], onesb], w=[csp[i]])

        def stFd(k):
            c, n = its[k]
            if n > 0:
                P.act(dd2[k % 2][:], csp2[:], AF.Exp, r=[csp2], w=[dd2[k % 2]], scale=-1.0)

        def stF(k):
            c, n = its[k]
            cb = c % 2
            for i in range(2):
                if n == 0:
                    P.copy(acc[i][cb][:], Op[i][:], r=[Op[i]], w=[acc[i][cb]])
                else:
                    d0 = dof(k)
                    P.tt(tmpa[i][:, d0:4, :], acc[i][cb][:, d0:4, :],
                         dd[i][k % 2][:, d0:4].unsqueeze(2).to_broadcast([128, 4 - d0, 64]), ALU.mult,
                         r=[acc[i][cb], dd[i][k % 2]], w=[tmpa[i]])
                    P.tt(acc[i][cb][:, d0:4, :], tmpa[i][:, d0:4, :], Op[i][:, d0:4, :], ALU.add, r=[tmpa[i], Op[i]], w=[acc[i][cb]])
            if n == 4 * c + 3:
                qs = slice(c * 512, (c + 1) * 512)
                for i in range(2):
                    h = 2 * hp + i
                    P.copy(accb[i][:], acc[i][cb][:], r=[acc[i][cb]], w=[accb[i]])
                    ys = yst[yc[0] % 4]
                    yc[0] += 1
                    for half in range(2):
                        for q2 in range(2):
                            qi = 2 * half + q2
                            P.tr(pTv[i][0:64, q2 * 128:(q2 + 1) * 128], accb[i][:, qi, :], ident[:], r=[accb[i], ident], w=[pTv[i]])
                        P.copy(ys[:, half * 256:(half + 1) * 256], pTv[i][0:64, :], r=[pTv[i]], w=[ys], eng="act")
                    P.dma(G.yT[1, h * 64:(h + 1) * 64, qs], ys[:], r=[ys], eng="pool")

        stA(0)
        for k in range(N + 2):
            if k + 1 < N:
                stA(k + 1)
            if k < N:
                stB(k)
            if 0 <= k - 2 < N:
                stFd(k - 2)
            if 0 <= k - 1 < N:
                stD(k - 1)
            if k < N:
                stC(k)
            if 0 <= k - 2 < N:
                stF(k - 2)
            if 0 <= k - 1 < N:
                stE(k - 1)
    P.finalize()


def ph_moba(nc, G, l, stabilize=True):
    S = G.S
    P = Prog(nc, G.pool)
    NT = S // 512
    NK = S // 128
    NB = S // 256
    assert NB <= 32
    ident, identf = mk_ident(P)
    ones32 = P.sb("ones32", [128, 64], F32)
    P.memset(ones32[:], 1.0, w=[ones32])
    onesb = P.sb("onesb", [128, 128], BF16)
    P.memset(onesb[:], 1.0, w=[onesb])
    CM = []
    f = P.sb("CMf", [128, 512], F32)
    for d in range(4):
        P.affine(f, [[1, 512]], -128 * d, -1, ALU.is_ge, -BIG, 0.0)
        b = P.sb("CM%d" % d, [128, 512], BF16)
        P.copy(b[:], f[:], r=[f], w=[b])
        CM.append(b)
    KE = [P.sb("KE%d" % i, [128, S], BF16) for i in range(2)]
    QN = [P.sb("QN%d" % i, [128, S], BF16) for i in range(2)]
    for i in range(2):
        P.memset(KE[i][64:128, :], 0.0, w=[KE[i]], eng=("dve" if i == 0 else "pool"))
        P.memset(QN[i][64:128, :], 0.0, w=[QN[i]], eng=("dve" if i == 0 else "pool"))
    ohf = P.sb("ohf", [128, 2048], F32)
    for s0 in range(0, S, 2048):
        w_ = min(2048, S - s0)
        ohv = T(rep(ohf.h[64:96, 0:w_], "p (b k) -> p b k", k=256), "ohv")
        ohv.res = ohf.res
        P.affine(ohv, [[-1, w_ // 256], [0, 256]], -(s0 // 256), 1, ALU.is_equal, 0.0, 1.0)
        for i in range(2):
            P.copy(KE[i][64:96, s0:s0 + w_], ohf[64:96, 0:w_], r=[ohf], w=[KE[i]], eng=("dve" if i == 0 else "act"))
    Vaug = P.sb("Vaug", [128, NK, 4, 65], BF16)
    vtmp = [P.sb("vtmp%d" % i, [128, 8, 256], BF16) for i in range(2)]
    vv = rep(G.vd, "(n p) c -> p n c", p=128)
    P.memset(Vaug[:, :, :, 64:65], 1.0, w=[Vaug])
    step = 1024
    for si, s0 in enumerate(range(0, S, step)):
        s1 = min(S, s0 + step)
        n0, n1 = s0 // 128, s1 // 128
        vt = vtmp[si % 2]
        P.dma(vt[:, 0:n1 - n0, :], vv[:, n0:n1, :], w=[vt])
        for h in range(4):
            P.copy(Vaug[:, n0:n1, h, 0:64], vt[:, 0:n1 - n0, h * 64:(h + 1) * 64], r=[vt], w=[Vaug],
                   eng=("act" if h % 2 == 0 else "dve"))
    km32 = [P.sb("km32_%d" % i, [64, 32], F32) for i in range(2)]
    kmhi = [P.sb("kmhi%d" % i, [64, 32], BF16) for i in range(2)]
    kmhf = [P.sb("kmhf%d" % i, [64, 32], F32) for i in range(2)]
    kmlo = [P.sb("kmlo%d" % i, [64, 32], BF16) for i in range(2)]
    for i in range(2):
        P.memset(km32[i][:], 0.0, w=[km32[i]])

    Zp = [[P.ps("Zp%d_%d" % (i, j), [128, 512]) for j in range(2)] for i in range(2)]
    OT = [[P.ps("OT%d_%d" % (i, j), [128, 512]) for j in range(2)] for i in range(2)]
    KM = P.sb("KM", [128, 2], F32)
    ksqa = P.sb("ksqa", [64, S], BF16)
    kmx = P.sb("kmx", [128, 16], F32)
    kqs = [OT[1][0], OT[1][1]]
    NEGVB = P.sb("NEGVB", [128, 32, 2, 32], F32)
    P.affine(NEGVB, [[1, 32], [0, 2], [-1, 32]], 0, 0, ALU.is_gt, -BIG, 0.0)
    OWNB = P.sb("OWNB", [128, 32, 2, 32], F32)
    P.affine(OWNB, [[1, 32], [0, 2], [-1, 32]], 0, 0, ALU.is_equal, 0.0, 1.0)
    NEGV3 = rep(NEGVB[:], "p a b n -> p (a b) n")
    OWN3 = rep(OWNB[:], "p a b n -> p (a b) n")
    QB = 16
    qsq = [P.sb("qsq%d" % i, [64, QB * 128], BF16) for i in range(2)]
    gm = [P.sb("gm%d" % i, [128, QB, 32], F32) for i in range(2)]
    g2 = [P.sb("g2_%d" % i, [128, QB, 32], F32) for i in range(2)]
    eq = [P.sb("eq%d" % i, [128, QB, 32], F32) for i in range(2)]
    mx = [P.sb("mx%d" % i, [128, QB], F32) for i in range(2)]
    mq = [P.sb("mq%d" % i, [128, QB], F32) for i in range(2)]
    nb = [P.sb("nb%d" % i, [128, QB, 32], BF16) for i in range(2)]
    Pt = [[P.sb("Pt%d_%d" % (i, j), [128, 512], BF16) for j in range(2)] for i in range(2)]
    RL = [P.sb("RL%d" % i, [128, 512], F32) for i in range(2)]
    bcs = [P.sb("bcs%d" % i, [64, 512], F32) for i in range(2)]
    yo = [P.sb("yo%d" % i, [64, 512], BF16) for i in range(4)]
    gpb = [T(rep(Zp[i][0].h[:, :], "p (q n) -> p q n", n=32), "gpb%d" % i) for i in range(2)]
    pTb = [T(Zp[i][1].h[:, :].bitcast(BF16), "pTb%d" % i) for i in range(2)]
    qnp = [T(OT[i][0].h[:, 0:QB], "qnp%d" % i) for i in range(2)]
    for i in range(2):
        gpb[i].res = Zp[i][0].res
        pTb[i].res = Zp[i][1].res
        qnp[i].res = OT[i][0].res
    yc = [0]
    zc = [0]
    kc_ = 0
    for hp in range(2):
        for i in range(2):
            h = 2 * hp + i
            for s0 in range(0, S, 4096):
                s1 = min(S, s0 + 4096)
                P.dma(QN[i][0:64, s0:s1], G.qdT[h * 64:(h + 1) * 64, s0:s1], w=[QN[i]])
                P.dma(KE[i][0:64, s0:s1], G.kdT[h * 64:(h + 1) * 64, s0:s1], w=[KE[i]])
        for i in range(2):
            P.op("dve", (lambda i_: (lambda e: e.tensor_reduce(out=km32[i_][:, 0:NB], in_=rep(KE[i_][0:64, :], "p (b k) -> p b k", k=256),
                                                               axis=AX.X, op=ALU.add)))(i), r=[KE[i]], w=[km32[i]])
            P.copy(kmhi[i][:], km32[i][:], r=[km32[i]], w=[kmhi[i]])
            P.copy(kmhf[i][:], kmhi[i][:], r=[kmhi[i]], w=[kmhf[i]])
            P.tt(kmlo[i][:], km32[i][:], kmhf[i][:], ALU.subtract, r=[km32[i], kmhf[i]], w=[kmlo[i]])
            if stabilize:
                nch = S // 512
                for ci in range(nch):
                    s0 = ci * 512
                    P.tt(ksqa[:, s0:s0 + 512], KE[i][0:64, s0:s0 + 512], KE[i][0:64, s0:s0 + 512], ALU.mult, r=[KE[i]], w=[ksqa])
                for ci in range(nch):
                    s0 = ci * 512
                    kqb = kqs[ci % 2]
                    P.mm(kqb[:], onesb[0:64, :], ksqa[:, s0:s0 + 512], r=[onesb, ksqa], w=[kqb])
                    P.op("dve", (lambda o_, i_: (lambda e: e.tensor_reduce(out=o_, in_=i_, axis=AX.X, op=ALU.max)))(
                        kmx[:, ci:ci + 1], kqb[:]), r=[kqb], w=[kmx])
                P.op("dve", (lambda o_, i_: (lambda e: e.tensor_reduce(out=o_, in_=i_, axis=AX.X, op=ALU.max)))(
                    KM[:, i:i + 1], kmx[:, 0:nch]), r=[kmx], w=[KM])
        for q0 in range(0, NK, QB):
            nq = min(QB, NK - q0)
            cs_ = slice(q0 * 128, (q0 + nq) * 128)
            for i in range(2):
                if stabilize:
                    P.tt(qsq[i][:, 0:nq * 128], QN[i][0:64, cs_], QN[i][0:64, cs_], ALU.mult, r=[QN[i]], w=[qsq[i]])
                for j in range(nq):
                    cj = slice((q0 + j) * 128, (q0 + j + 1) * 128)
                    P.mm(gpb[i][:, j, :], QN[i][0:64, cj], kmhi[i][:], start=True, stop=False, r=[QN[i], kmhi[i]], w=[gpb[i]])
                    P.mm(gpb[i][:, j, :], QN[i][0:64, cj], kmlo[i][:], start=False, stop=True, r=[QN[i], kmlo[i]], w=[gpb[i]])
                    if stabilize:
                        P.mm(qnp[i][:, j:j + 1], qsq[i][:, j * 128:(j + 1) * 128], onesb[0:64, 0:1],
                             r=[qsq[i], onesb], w=[qnp[i]])
            for i in range(2):
                G_ = gm[i]
                P.tt(G_[:, 0:nq, :], gpb[i][:, 0:nq, :], NEGV3[:, q0:q0 + nq, :], ALU.add, r=[gpb[i], NEGVB], w=[G_])
                src = G_
                for it in range(3):
                    P.op("dve", (lambda o_, s_: (lambda e: e.tensor_reduce(out=o_, in_=s_, axis=AX.X, op=ALU.max)))(
                        mx[i][:, 0:nq], src[:, 0:nq, :]), r=[src], w=[mx[i]])
                    if it < 2:
                        P.tt(eq[i][:, 0:nq, :], src[:, 0:nq, :], mx[i][:, 0:nq].unsqueeze(2).to_broadcast([128, nq, 32]),
                             ALU.is_equal, r=[src, mx[i]], w=[eq[i]])
                        P.stt(g2[i][:, 0:nq, :], eq[i][:, 0:nq, :], -1e6, src[:, 0:nq, :], ALU.mult, ALU.add,
                              r=[eq[i], src], w=[g2[i]])
                        src = g2[i]
                P.ts(mx[i][:, 0:nq], mx[i][:, 0:nq], -BIG / 2, None, ALU.max, r=[mx[i]], w=[mx[i]])
                P.tt(eq[i][:, 0:nq, :], G_[:, 0:nq, :], mx[i][:, 0:nq].unsqueeze(2).to_broadcast([128, nq, 32]),
                     ALU.is_ge, r=[G_, mx[i]], w=[eq[i]])
                P.tt(eq[i][:, 0:nq, :], eq[i][:, 0:nq, :], OWN3[:, q0:q0 + nq, :], ALU.max, r=[eq[i], OWNB], w=[eq[i]])
                if stabilize:
                    P.act(mq[i][:, 0:nq], qnp[i][:, 0:nq], AF.Sqrt, r=[qnp[i], KM], w=[mq[i]], scale=KM[:, i:i + 1])
                    P.ts(mq[i][:, 0:nq], mq[i][:, 0:nq], -1.0, -BIG, ALU.mult, ALU.add, r=[mq[i]], w=[mq[i]])
                else:
                    P.memset(mq[i][:], -BIG, w=[mq[i]], eng="dve")
                P.stt(nb[i][:, 0:nq, :], eq[i][:, 0:nq, :], BIG, mq[i][:, 0:nq].unsqueeze(2).to_broadcast([128, nq, 32]),
                      ALU.mult, ALU.add, r=[eq[i], mq[i]], w=[nb[i]])
                for j0 in range(0, nq, 8):
                    nj = min(8, nq - j0)
                    for j in range(j0, j0 + nj):
                        P.tr(pTb[i][64:96, (j - j0) * 128:(j - j0 + 1) * 128], nb[i][:, j, :], ident[:], r=[nb[i], ident], w=[pTb[i]])
                    P.copy(QN[i][64:96, (q0 + j0) * 128:(q0 + j0 + nj) * 128], pTb[i][64:96, 0:nj * 128], r=[pTb[i]], w=[QN[i]],
                           eng="act")
        its = [(c, n) for c in range(NT) for n in range(0, 4 * c + 4)]
        N = len(its)
        zslot = {}

        def mA(k):
            c, n = its[k]
            qs = slice(c * 512, (c + 1) * 512)
            ks = slice(n * 128, (n + 1) * 128)
            diag = n >= 4 * c
            zslot[k] = zc[0] % 2
            zc[0] += 1
            for i in range(2):
                Z = Zp[i][zslot[k]]
                P.mm(Z[:], KE[i][:, ks], QN[i][:, qs], start=True, stop=(not diag), r=[KE[i], QN[i]], w=[Z])
                if diag:
                    P.mm(Z[:], ident[:], CM[n - 4 * c][:], start=False, stop=True, r=[ident, CM[n - 4 * c]], w=[Z])

        def mB(k):
            for i in range(2):
                P.act(Pt[i][k % 2][:], Zp[i][zslot[k]][:], AF.Exp, r=[Zp[i][zslot[k]]], w=[Pt[i][k % 2]])

        def mC(k):
            c, n = its[k]
            for i in range(2):
                h = 2 * hp + i
                P.mm(OT[i][c % 2][0:65, :], Vaug[:, n, h, :], Pt[i][k % 2][:], start=(n == 0), stop=(n == 4 * c + 3),
                     r=[Vaug, Pt[i][k % 2]], w=[OT[i][c % 2]])

        def mFin(c):
            qs = slice(c * 512, (c + 1) * 512)
            for i in range(2):
                h = 2 * hp + i
                O_ = OT[i][c % 2]
                P.op("dve", (lambda o_, i_: (lambda e: e.reciprocal(out=o_, in_=i_)))(RL[i][64:65, :], O_[64:65, :]),
                     r=[O_], w=[RL[i]])
                Zf = Zp[i][zfree[0]]
                P.mm(Zf[0:64, :], ones32[64:65, 0:64], RL[i][64:65, :], r=[ones32, RL[i]], w=[Zf])
                P.copy(bcs[i][:], Zf[0:64, :], r=[Zf], w=[bcs[i]], eng="act")
                y = yo[yc[0] % 4]
                yc[0] += 1
                P.tt(y[:], O_[0:64, :], bcs[i][:], ALU.mult, r=[O_, bcs[i]], w=[y])
                P.dma(G.yT[3, h * 64:(h + 1) * 64, qs], y[:], r=[y], eng="pool")

        zfree = [0]
        mA(0)
        pend = None
        for k in range(N):
            if k + 1 < N:
                mA(k + 1)
            mB(k)
            zfree[0] = zslot[k]
            if pend is not None:
                mFin(pend)
                pend = None
            mC(k)
            c, n = its[k]
            if n == 4 * c + 3:
                pend = c
        mFin(pend)
    P.finalize()


def ph_c1(nc, G, l):
    S = G.S
    P = Prog(nc, G.pool)
    TT = 256
    J = 2
    ident, _ = mk_ident(P)
    K = norm_alloc(P, J)
    Wg = P.sb("Wg", [128, 8, 4096], BF16)
    Wb = P.sb("Wb", [128, 4, 2, D], BF16)
    Wo = P.sb("Wo", [128, 8, D], BF16)
    for kc in range(8):
        load_w(P, Wg, Wg[:, kc, :], G.w_in[l, kc * 128:(kc + 1) * 128, 3076:3076 + 4096])
    for br in range(4):
        for k2 in range(2):
            load_w(P, Wb, Wb[:, br, k2, :], G.w_branch[l, br, k2 * 128:(k2 + 1) * 128, :])
    for kc in range(8):
        load_w(P, Wo, Wo[:, kc, :], G.w_o[l, kc * 128:(kc + 1) * 128, :])
    gb = load_gain(P, "gb", G.norm2_g[l])
    hts = [P.sb("ht%d" % i, [128, 8, TT], BF16) for i in range(2)]
    yts = [P.sb("yt%d" % i, [128, 4, 2, TT], BF16) for i in range(2)]
    xos = [P.sb("xo%d" % i, [128, J, D], F32) for i in range(2)]
    h2s = [P.sb("h2s%d" % i, [128, 8, TT], BF16) for i in range(2)]
    mT = P.sb("mT", [128, 8, TT], BF16)
    sg = [P.sb("sg%d" % i, [128, TT], F32) for i in range(2)]
    tmp = [P.sb("tmp%d" % i, [128, TT], F32) for i in range(2)]
    mg = [P.sb("mg%d" % i, [128, TT], F32) for i in range(2)]
    Gp = [P.ps("Gp%d" % i, [128, TT]) for i in range(2)]
    Pj = [P.ps("Pj%d" % i, [128, TT]) for i in range(2)]
    Op = [P.ps("Op%d" % i, [128, 512]) for i in range(2)]
    xsrc = G.x_in if l == 0 else G.xa
    hTv = rep(G.hT, "(k p) s -> p k s", p=128)
    h2v = rep(G.h2T, "(k p) s -> p k s", p=128)
    cnt = 0
    oc = 0
    pend = None
    for t in range(S // TT):
        t0 = t * TT
        ht = hts[t % 2]
        yt = yts[t % 2]
        xo = xos[t % 2]
        P.dma(ht[:], hTv[:, :, t0:t0 + TT], w=[ht])
        for br in range(4):
            P.dma(yt[:, br, :, :], rep(G.yT[br], "(k p) s -> p k s", p=128)[:, :, t0:t0 + TT], w=[yt])
        P.dma(xo[:], rep(xsrc[t0:t0 + TT, :], "(j p) d -> p j d", p=128), w=[xo])
        for dmc in range(8):
            m = mg[dmc % 2]
            for br in range(4):
                gp = Gp[cnt % 2]
                pj = Pj[cnt % 2]
                s_ = sg[cnt % 2]
                tm = tmp[cnt % 2]
                cnt += 1
                cg = br * 1024 + dmc * 128
                for kc in range(8):
                    P.mm(gp[:], Wg[:, kc, cg:cg + 128], ht[:, kc, :], start=(kc == 0), stop=(kc == 7), r=[Wg, ht], w=[gp])
                for k2 in range(2):
                    P.mm(pj[:], Wb[:, br, k2, dmc * 128:(dmc + 1) * 128], yt[:, br, k2, :], start=(k2 == 0), stop=(k2 == 1),
                         r=[Wb, yt], w=[pj])
                P.act(s_[:], gp[:], AF.Sigmoid, r=[gp], w=[s_])
                if br == 0:
                    P.tt(m[:], s_[:], pj[:], ALU.mult, r=[s_, pj], w=[m])
                else:
                    P.tt(tm[:], s_[:], pj[:], ALU.mult, r=[s_, pj], w=[tm])
                    if br < 3:
                        P.tt(m[:], m[:], tm[:], ALU.add, r=[m, tm], w=[m])
                    else:
                        P.tt(mT[:, dmc, :], m[:], tm[:], ALU.add, r=[m, tm], w=[mT])
        if pend is not None:
            h2 = h2s[pend[0] % 2]
            norm_p2(P, K, ident, h2)
            P.dma(h2v[:, :, pend[1]:pend[1] + TT], h2[:], r=[h2], eng="pool")
            pend = None
        for j in range(J):
            for hf in range(2):
                op_ = Op[oc % 2]
                oc += 1
                for dmc in range(8):
                    P.mm(op_[:], mT[:, dmc, j * 128:(j + 1) * 128], Wo[:, dmc, hf * 512:(hf + 1) * 512], start=(dmc == 0),
                         stop=(dmc == 7), r=[mT, Wo], w=[op_])
                P.tt(xo[:, j, hf * 512:(hf + 1) * 512], xo[:, j, hf * 512:(hf + 1) * 512], op_[:], ALU.add, r=[xo, op_], w=[xo])
        P.dma(rep(G.xm[t0:t0 + TT, :], "(j p) d -> p j d", p=128), xo[:], r=[xo], eng="pool")
        norm_p1(P, K, xo, gb)
        pend = (t, t0)
    if pend is not None:
        h2 = h2s[pend[0] % 2]
        norm_p2(P, K, ident, h2)
        P.dma(h2v[:, :, pend[1]:pend[1] + TT], h2[:], r=[h2], eng="pool")
    P.finalize()


def ph_c2(nc, G, l):
    S = G.S
    P = Prog(nc, G.pool)
    TT = 256
    J = 2
    NF = FF // 128
    last = (l == G.L - 1)
    ident, _ = mk_ident(P)
    K = norm_alloc(P, J)
    SW = 1408
    Wgu = [P.sb("Wgu%d" % q, [128, 8, SW], BF16) for q in range(4)]
    Wd = P.sb("Wd", [128, NF, D], BF16)
    for q in (0, 2, 1, 3):
        for kc in range(8):
            c0 = q * SW
            load_w(P, Wgu[q], Wgu[q][:, kc, :], G.w_gate_up[l, kc * 128:(kc + 1) * 128, c0:c0 + SW])
    for fc in range(NF):
        load_w(P, Wd, Wd[:, fc, :], G.w_down[l, fc * 128:(fc + 1) * 128, :])
    if not last:
        gb = load_gain(P, "gb", G.norm1_g[l + 1])
    if last:
        fgb = P.sb("fgb", [128, D], F32)
        P.dma(fgb[:], G.final_g.partition_broadcast(128), w=[fgb])
    h2s = [P.sb("h2s%d" % i, [128, 8, TT], BF16) for i in range(2)]
    xos = [P.sb("xo%d" % i, [128, J, D], F32) for i in range(2)]
    aT = P.sb("aT", [128, NF, TT], BF16)
    hto = P.sb("hto", [128, 8, TT], BF16)
    sg = [P.sb("sg%d" % i, [128, TT], F32) for i in range(2)]
    Gp = [P.ps("Gp%d" % i, [128, TT]) for i in range(2)]
    Up = [P.ps("Up%d" % i, [128, TT]) for i in range(2)]
    Op = [P.ps("Op%d" % i, [128, 512]) for i in range(2)]
    hTv = rep(G.hT, "(k p) s -> p k s", p=128)
    h2v = rep(G.h2T, "(k p) s -> p k s", p=128)
    cnt = 0
    oc = 0
    pend = None
    for t in range(S // TT):
        t0 = t * TT
        h2 = h2s[t % 2]
        xo = xos[t % 2]
        P.dma(h2[:], h2v[:, :, t0:t0 + TT], w=[h2])
        P.dma(xo[:], rep(G.xm[t0:t0 + TT, :], "(j p) d -> p j d", p=128), w=[xo])
        for fc in range(NF):
            gp = Gp[cnt % 2]
            up = Up[cnt % 2]
            s_ = sg[cnt % 2]
            cnt += 1
            wq = Wgu[fc // 11]
            wu = Wgu[2 + fc // 11]
            fo = (fc % 11) * 128
            for kc in range(8):
                P.mm(gp[:], wq[:, kc, fo:fo + 128], h2[:, kc, :], start=(kc == 0), stop=(kc == 7), r=[wq, h2], w=[gp])
            for kc in range(8):
                P.mm(up[:], wu[:, kc, fo:fo + 128], h2[:, kc, :], start=(kc == 0), stop=(kc == 7), r=[wu, h2], w=[up])
            P.act(s_[:], gp[:], AF.Silu, r=[gp], w=[s_])
            P.tt(aT[:, fc, :], s_[:], up[:], ALU.mult, r=[s_, up], w=[aT])
        if pend is not None:
            norm_p2(P, K, ident, hto)
            P.dma(hTv[:, :, pend:pend + TT], hto[:], r=[hto], eng="pool")
            pend = None
        for j in range(J):
            for hf in range(2):
                op_ = Op[oc % 2]
                oc += 1
                for fc in range(NF):
                    P.mm(op_[:], aT[:, fc, j * 128:(j + 1) * 128], Wd[:, fc, hf * 512:(hf + 1) * 512], start=(fc == 0),
                         stop=(fc == NF - 1), r=[aT, Wd], w=[op_])
                P.tt(xo[:, j, hf * 512:(hf + 1) * 512], xo[:, j, hf * 512:(hf + 1) * 512], op_[:], ALU.add, r=[xo, op_], w=[xo])
        if not last:
            P.dma(rep(G.xa[t0:t0 + TT, :], "(j p) d -> p j d", p=128), xo[:], r=[xo], eng="pool")
            norm_p1(P, K, xo, gb)
            pend = t0
        else:
            norm_stats(P, K, xo, J)
            for j in range(J):
                P.stt(xo[:, j, :], xo[:, j, :], K.rstd[:, j:j + 1], fgb[:], ALU.mult, ALU.mult, r=[xo, K.rstd, fgb], w=[xo])
            P.dma(rep(G.out[t0:t0 + TT, :], "(j p) d -> p j d", p=128), xo[:], r=[xo], eng="pool")
    if pend is not None:
        norm_p2(P, K, ident, hto)
        P.dma(hTv[:, :, pend:pend + TT], hto[:], r=[hto], eng="pool")
    P.finalize()


W_SPECS = [
    ("norm1_g", lambda L: [L, D]), ("w_in", lambda L: [L, D, NIN]), ("conv_a_w", lambda L: [L, 3, 256]),
    ("ssm_conv_w", lambda L: [L, 4, 512]), ("ssm_conv_b", lambda L: [L, 512]), ("ssm_dt_bias", lambda L: [L, 4]),
    ("ssm_a_log", lambda L: [L, 4]), ("ssm_d", lambda L: [L, 4]), ("ssm_norm_g", lambda L: [L, 256]),
    ("w_branch", lambda L: [L, 4, 256, D]), ("w_o", lambda L: [L, D, D]), ("norm2_g", lambda L: [L, D]),
    ("w_gate_up", lambda L: [L, D, 2 * FF]), ("w_down", lambda L: [L, FF, D]), ("final_g", lambda L: [D]),
]


def build(S, L, dbg=(), phases=None):
    nc = bass.Bass("TRN2", target_bir_lowering=False)
    G = NS()
    G.S = S
    G.L = L
    G.pool = SemPool(nc)
    G.x_in = nc.dram_tensor("x", [S, D], F32, kind="ExternalInput").ap()
    for name, shp in W_SPECS:
        setattr(G, name, nc.dram_tensor(name, shp(L), F32, kind="ExternalInput").ap())
    G.out = nc.dram_tensor("out", [S, D], F32, kind="ExternalOutput").ap()

    def scr(name, shape, dt):
        kind = "ExternalOutput" if name in dbg else "Internal"
        t = nc.dram_tensor(name, list(shape), dt, kind=kind).ap()
        setattr(G, name, t)
        return t

    scr("hT", [D, S], BF16)
    scr("h2T", [D, S], BF16)
    scr("xa", [S, D], F32)
    scr("xm", [S, D], F32)
    scr("uaT", [768, S], BF16)
    scr("qbT", [256, S], BF16)
    scr("kbT", [256, S], BF16)
    scr("vb", [S, 256], BF16)
    scr("zT", [256, S], BF16)
    scr("xbcT", [512, S], BF16)
    scr("dt", [S, 4], F32)
    scr("qdT", [256, S], BF16)
    scr("kdT", [256, S], BF16)
    scr("vd", [S, 256], BF16)
    scr("yT", [4, 256, S], BF16)
    run = (lambda p: True) if phases is None else (lambda p: p in phases)
    if run("norm0"):
        ph_norm0(nc, G)
    for l in range(L):
        if run("proj"):
            ph_proj(nc, G, l)
        if run("conva"):
            ph_conva(nc, G, l)
        if run("ssd"):
            ph_ssd(nc, G, l)
        if run("sb"):
            ph_sb(nc, G, l)
        if run("moba"):
            ph_moba(nc, G, l)
        if run("c1"):
            ph_c1(nc, G, l)
        if run("c2"):
            ph_c2(nc, G, l)
    G.pool.es.close()
    return nc


from concourse.bass_utils import run_bass_kernel_spmd

_W_NAMES = [n for n, _ in W_SPECS]


def kernel(**inputs):
    x = np.ascontiguousarray(np.asarray(inputs["x"], dtype=np.float32))
    B, S, _ = x.shape
    L = int(np.asarray(inputs["w_in"]).shape[0])
    assert B == 8
    nc = build(S, L)
    w = {n: np.ascontiguousarray(np.asarray(inputs[n], dtype=np.float32)) for n in _W_NAMES}
    in_maps = []
    for b in range(B):
        m = {"x": x[b]}
        m.update(w)
        in_maps.append(m)
    res = run_bass_kernel_spmd(nc, in_maps, core_ids=list(range(B)))
    return np.stack([np.asarray(res.results[b]["out"], dtype=np.float32) for b in range(B)], axis=0)
```

```python
import numpy as np
import concourse.bass as bass
import concourse.mybir as mybir
from contextlib import ExitStack

F32 = mybir.dt.float32
BF16 = mybir.dt.bfloat16
AF = mybir.ActivationFunctionType
ALU = mybir.AluOpType
AX = mybir.AxisListType

ENGS = ("pe", "act", "dve", "pool", "sp")
NDSEM = 8


class Res:
    __slots__ = ("name", "w", "r")

    def __init__(self, name=""):
        self.name = name
        self.w = None
        self.r = {}


class T:
    def __init__(self, h, name=""):
        self.h = h
        self.res = Res(name)

    def __getitem__(self, idx):
        return self.h[idx]


class Op:
    __slots__ = ("eng", "fn", "deps", "needs_inc", "tok", "is_dma", "gi")


class SemPool:
    def __init__(self, nc):
        self.nc = nc
        self.es = ExitStack()
        self.sems = {}
        self.counts = {}
        self.prev = []

    def get(self, key):
        if key not in self.sems:
            self.sems[key] = self.es.enter_context(self.nc.semaphore("sem_" + key))
            self.counts[key] = 0
        return self.sems[key]


class Prog:
    _uid = [0]

    def __init__(self, nc, pool=None):
        self.nc = nc
        self.pool = pool if pool is not None else SemPool(nc)
        Prog._uid[0] += 1
        self.pfx = "p%d_" % Prog._uid[0]
        self.ops = []
        self.last = {e: None for e in ENGS}
        self.pending = {e: [] for e in ENGS}
        self.dma_ops = []
        self.es = ExitStack()
        self.sems = {}
        self.dsems = {}
        self.ndma = {e: 0 for e in ENGS}
        self.last_on_dsem = {}

    def sb(self, name, shape, dt, stack=None):
        h = (stack or self.es).enter_context(self.nc.sbuf_tensor(self.pfx + name, list(shape), dt))
        return T(h, name)

    def ps(self, name, shape, dt=F32, stack=None):
        h = (stack or self.es).enter_context(self.nc.psum_tensor(self.pfx + name, list(shape), dt))
        return T(h, name)

    def op(self, eng, fn, r=(), w=(), dma=False):
        o = Op()
        o.eng = eng
        o.fn = fn
        o.is_dma = dma
        o.needs_inc = dma
        o.tok = None
        o.gi = len(self.ops)
        deps = {}
        for t in r:
            res = t.res if isinstance(t, T) else t
            if res.w is not None:
                deps[id(res.w)] = res.w
        for t in w:
            res = t.res if isinstance(t, T) else t
            if res.w is not None:
                deps[id(res.w)] = res.w
            for d in res.r.values():
                deps[id(d)] = d
        for d in self.pending[eng]:
            deps[id(d)] = d
        self.pending[eng] = []
        if dma:
            k = (eng, self.ndma[eng] % NDSEM)
            self.ndma[eng] += 1
            prev = self.last_on_dsem.get(k)
            if prev is not None:
                deps[id(prev)] = prev
            self.last_on_dsem[k] = o
            o.tok = k
        dl = []
        for d in deps.values():
            if d is o:
                continue
            if eng == "pe" and d.eng == "pe" and not d.is_dma:
                continue
            d.needs_inc = True
            dl.append(d)
        o.deps = dl
        for t in r:
            res = t.res if isinstance(t, T) else t
            key = ("dma", o.gi) if dma else eng
            res.r[key] = o
        for t in w:
            res = t.res if isinstance(t, T) else t
            res.w = o
            res.r = {}
        self.ops.append(o)
        self.last[eng] = o
        if dma:
            self.dma_ops.append(o)
        return o

    def barrier(self):
        outs = [o for o in self.last.values() if o is not None]
        outs += list(self.last_on_dsem.values())
        for e in ENGS:
            self.pending[e] = list(outs)
        for o in outs:
            o.needs_inc = True

    def dma(self, out_ap, in_ap, r=(), w=(), eng="sp", **kw):
        return self.op(eng, lambda e: e.dma_start(out=out_ap, in_=in_ap, **kw), r=r, w=w, dma=True)

    def mm(self, out_ap, lhsT, rhs, start=True, stop=True, r=(), w=(), sgc=False):
        if sgc:
            return self.op("pe", lambda e: e.matmul(out_ap, lhsT, rhs, start=start, stop=stop, skip_group_check=True), r=r, w=w)
        return self.op("pe", lambda e: e.matmul(out_ap, lhsT, rhs, start=start, stop=stop), r=r, w=w)

    def tr(self, out_ap, in_ap, ident, r=(), w=()):
        return self.op("pe", lambda e: e.transpose(out_ap, in_ap, ident), r=r, w=w)

    def act(self, out_ap, in_ap, func, r=(), w=(), **kw):
        return self.op("act", lambda e: e.activation(out_ap, in_ap, func, **kw), r=r, w=w)


    def stt(self, out, in0, scalar, in1, op0, op1, r=(), w=(), accum_out=None):
        if accum_out is None:
            return self.op("dve", lambda e: e.scalar_tensor_tensor(out=out, in0=in0, scalar=scalar, in1=in1, op0=op0, op1=op1), r=r, w=w)
        return self.op("dve", lambda e: e.scalar_tensor_tensor(out=out, in0=in0, scalar=scalar, in1=in1, op0=op0, op1=op1, accum_out=accum_out), r=r, w=w)

    def ts(self, out, in0, s1, s2, op0, op1=None, r=(), w=(), eng="dve"):
        if op1 is None:
            return self.op(eng, lambda e: e.tensor_scalar(out=out, in0=in0, scalar1=s1, scalar2=None, op0=op0), r=r, w=w)
        return self.op(eng, lambda e: e.tensor_scalar(out=out, in0=in0, scalar1=s1, scalar2=s2, op0=op0, op1=op1), r=r, w=w)

    def tt(self, out, in0, in1, op, r=(), w=(), eng="dve"):
        return self.op(eng, lambda e: e.tensor_tensor(out=out, in0=in0, in1=in1, op=op), r=r, w=w)

    def copy(self, out, in_, r=(), w=(), eng="dve"):
        if eng == "act":
            return self.op("act", lambda e: e.copy(out=out, in_=in_), r=r, w=w)
        return self.op(eng, lambda e: e.tensor_copy(out=out, in_=in_), r=r, w=w)

    def memset(self, ap, val, w=(), eng="pool"):
        return self.op(eng, lambda e: e.memset(ap, val), w=w)

    def affine(self, t, pattern, base, cm, cmp, fill, val, eng="pool"):
        self.memset(t[:], val, w=[t])
        return self.op("pool", lambda e: e.affine_select(out=t[:], in_=t[:], pattern=pattern, compare_op=cmp, fill=fill, base=base, channel_multiplier=cm), r=[t], w=[t])

    def finalize(self):
        nc = self.nc
        es = self.es
        self.barrier()
        self.op("sp", None, r=(), w=())
        pool = self.pool
        for o in self.ops:
            if o.is_dma:
                key = "d_%s%d" % o.tok
                sem = pool.get(key)
                pool.counts[key] += 16
                o.tok = (sem, pool.counts[key], 16)
            elif o.needs_inc:
                key = "e_" + o.eng
                sem = pool.get(key)
                pool.counts[key] += 1
                o.tok = (sem, pool.counts[key], 1)
        streams = {e: [] for e in ENGS}
        seen = {e: {} for e in ENGS}
        first = {e: True for e in ENGS}
        for o in self.ops:
            waits = {}
            deptoks = [d.tok for d in o.deps]
            if first[o.eng]:
                first[o.eng] = False
                deptoks += [(sem, val, 0) for sem, val in pool.prev]
            for sem, val, _ in deptoks:
                sid = id(sem)
                if seen[o.eng].get(sid, 0) >= val:
                    continue
                if sid not in waits or waits[sid][1] < val:
                    waits[sid] = (sem, val)
            for sid, (sem, val) in waits.items():
                seen[o.eng][sid] = val
            streams[o.eng].append((list(waits.values()), o.fn, o.tok if (o.is_dma or o.needs_inc) else None))
        pool.prev = [(pool.sems[k], pool.counts[k]) for k in pool.sems if pool.counts[k] > 0]
        self.stats = {e: len(streams[e]) for e in ENGS}

        def run(stream):
            def f(e):
                for waits, fn, tok in stream:
                    for sem, val in waits:
                        e.wait_ge(sem, val)
                    if fn is None:
                        continue
                    ins = fn(e)
                    if tok is not None:
                        ins.then_inc(tok[0], tok[2])
            return f

        with nc.Block() as block:
            block.tensor(run(streams["pe"]))
            block.scalar(run(streams["act"]))
            block.vector(run(streams["dve"]))
            block.gpsimd(run(streams["pool"]))
            block.sync(run(streams["sp"]))
        es.close()


D = 1024
NIN = 7172
FF = 2816
BIG = 30000.0
EPS = 1e-6


class NS:
    pass


def rep(ap, pattern, **kw):
    return ap.rearrange(pattern, **kw)


def mk_ident(P, name="ident"):
    f = P.sb(name + "_f", [128, 128], F32)
    P.affine(f, [[-1, 128]], 0, 1, ALU.is_equal, 0.0, 1.0)
    b = P.sb(name, [128, 128], BF16)
    P.copy(b[:], f[:], r=[f], w=[b])
    return b, f


def norm_alloc(P, J):
    K = NS()
    K.J = J
    K.junk = P.sb("n_junk", [128, D], F32)
    K.ss = P.sb("n_ss", [128, 4], F32)
    K.ms = P.sb("n_ms", [128, 4], F32)
    K.rstd = P.sb("n_rstd", [128, 4], F32)
    K.mh = P.sb("n_mh", [128, 4], F32)
    P.memset(K.mh[:], -0.5, w=[K.mh])
    K.xn = P.sb("n_xn", [128, J, D], BF16)
    K.pT = [P.ps("n_pT%d" % i, [128, D], BF16) for i in range(2)]
    K.cnt = 0
    return K


def norm_stats(P, K, xt, J):
    for j in range(J):
        P.stt(K.junk[:], xt[:, j, :], 1.0, xt[:, j, :], ALU.mult, ALU.mult, r=[xt], w=[K.junk, K.ss],
              accum_out=K.ss[:, j:j + 1])
    P.ts(K.ms[:, 0:J], K.ss[:, 0:J], 1.0 / D, EPS, ALU.mult, ALU.add, r=[K.ss], w=[K.ms])
    P.tt(K.rstd[:, 0:J], K.ms[:, 0:J], K.mh[:, 0:J], ALU.pow, r=[K.ms, K.mh], w=[K.rstd], eng="pool")


def norm_p1(P, K, xt, gb=None):
    J = K.J
    norm_stats(P, K, xt, J)
    for j in range(J):
        if gb is None:
            P.act(K.xn[:, j, :], xt[:, j, :], AF.Copy, r=[xt, K.rstd], w=[K.xn], scale=K.rstd[:, j:j + 1])
        else:
            P.stt(K.xn[:, j, :], xt[:, j, :], K.rstd[:, j:j + 1], gb[:], ALU.mult, ALU.mult, r=[xt, K.rstd, gb], w=[K.xn])


def norm_p2(P, K, ident, hTt):
    J = K.J
    for j in range(J):
        pT = K.pT[K.cnt % 2]
        K.cnt += 1
        for kc in range(8):
            P.tr(pT[:, kc * 128:(kc + 1) * 128], K.xn[:, j, kc * 128:(kc + 1) * 128], ident[:], r=[K.xn, ident], w=[pT])
        P.copy(hTt[:, :, j * 128:(j + 1) * 128], rep(pT[:, :], "p (k t) -> p k t", k=8), r=[pT], w=[hTt])


def norm_tt(P, K, ident, xt, hTt, gb=None):
    norm_p1(P, K, xt, gb)
    norm_p2(P, K, ident, hTt)


def load_gain(P, name, g_ap):
    t = P.sb(name, [128, D], F32)
    P.dma(t[:], g_ap.partition_broadcast(128), w=[t])
    return t


def load_w(P, dst, dst_sl, src_ap):
    P.dma(dst_sl, src_ap, w=[dst], eng="pool")


def load_w_cast(P, dst, dst_sl, src_ap, ncols, stage, si, gcol=None, eng="dve", r_extra=()):
    st = stage[si % len(stage)]
    P.dma(st[:, 0:ncols], src_ap, w=[st])
    if gcol is None:
        P.copy(dst_sl, st[:, 0:ncols], r=[st], w=[dst], eng=eng)
    elif eng == "act":
        P.act(dst_sl, st[:, 0:ncols], AF.Copy, r=[st] + list(r_extra), w=[dst], scale=gcol)
    else:
        P.ts(dst_sl, st[:, 0:ncols], gcol, None, ALU.mult, r=[st] + list(r_extra), w=[dst], eng=eng)


def ph_norm0(nc, G):
    S = G.S
    P = Prog(nc, G.pool)
    ident, _ = mk_ident(P)
    J = 4
    K = norm_alloc(P, J)
    xts = [P.sb("xt%d" % i, [128, J, D], F32) for i in range(2)]
    hts = [P.sb("ht%d" % i, [128, 8, J * 128], BF16) for i in range(2)]
    hTv = rep(G.hT, "(k p) s -> p k s", p=128)
    gb = load_gain(P, "gb", G.norm1_g[0])
    for t in range(S // (128 * J)):
        xt = xts[t % 2]
        ht = hts[t % 2]
        t0 = t * 128 * J
        P.dma(xt[:], rep(G.x_in[t0:t0 + 128 * J, :], "(j p) d -> p j d", p=128), w=[xt])
        norm_tt(P, K, ident, xt, ht, gb)
        P.dma(hTv[:, :, t0:t0 + 128 * J], ht[:], r=[ht], eng="pool")
    P.finalize()


def ph_proj(nc, G, l):
    S = G.S
    P = Prog(nc, G.pool)
    NC_A = 3076
    WA = P.sb("WA", [128, 8, NC_A], BF16)
    for kc in range(8):
        load_w(P, WA, WA[:, kc, :], G.w_in[l, kc * 128:(kc + 1) * 128, 0:NC_A])
    hts = [P.sb("ht%d" % i, [128, 8, 512], BF16) for i in range(2)]
    groups = [
        (G.uaT, 0, 6, 1.0),
        (G.qbT, 768, 2, 0.125),
        (G.kbT, 1024, 2, 1.0),
        (G.zT, 1536, 2, 1.0),
        (G.xbcT, 1792, 4, 1.0),
        (G.qdT, 2308, 2, 0.125),
        (G.kdT, 2564, 2, 1.0),
    ]
    stg = {}
    for gi, (dst, c0, nch, sc) in enumerate(groups):
        stg[gi] = [P.sb("stg%d_%d" % (gi, i), [128, nch, 512], BF16) for i in range(2)]
    vst = [P.sb("vst%d" % i, [128, 4, 512], BF16) for i in range(2)]
    dst_ = [P.sb("dtst%d" % i, [128, 4, 4], F32) for i in range(2)]
    pf = [P.ps("pf%d" % i, [128, 512]) for i in range(6)]
    pv = P.ps("pv", [128, 512])
    pd = P.ps("pd", [128, 4, 4])
    hTv = rep(G.hT, "(k p) s -> p k s", p=128)
    cnt = 0
    for t in range(S // 512):
        t0 = t * 512
        ht = hts[t % 2]
        P.dma(ht[:], hTv[:, :, t0:t0 + 512], w=[ht])
        for gi, (dst, c0, nch, sc) in enumerate(groups):
            sg = stg[gi][t % 2]
            for ch in range(nch):
                ps = pf[cnt % 6]
                for kc in range(8):
                    P.mm(ps[:], WA[:, kc, c0 + ch * 128:c0 + (ch + 1) * 128], ht[:, kc, :], start=(kc == 0), stop=(kc == 7),
                         r=[WA, ht], w=[ps])
                if cnt % 2 == 0:
                    P.act(sg[:, ch, :], ps[:], AF.Copy, r=[ps], w=[sg], scale=sc)
                else:
                    P.ts(sg[:, ch, :], ps[:], sc, None, ALU.mult, r=[ps], w=[sg])
                cnt += 1
            P.dma(rep(dst, "(c p) s -> p c s", p=128)[:, :, t0:t0 + 512], sg[:], r=[sg], eng="pool")
        vs = vst[t % 2]
        ds = dst_[t % 2]
        for j in range(4):
            for half, c0 in enumerate((1280, 2820)):
                for kc in range(8):
                    P.mm(pv[:, half * 256:(half + 1) * 256], ht[:, kc, j * 128:(j + 1) * 128], WA[:, kc, c0:c0 + 256],
                         start=(kc == 0), stop=(kc == 7), r=[WA, ht], w=[pv])
            P.copy(vs[:, j, :], pv[:], r=[pv], w=[vs], eng=("dve" if j % 2 == 0 else "act"))
            for kc in range(8):
                P.mm(pd[:, j, :], ht[:, kc, j * 128:(j + 1) * 128], WA[:, kc, 2304:2308], start=(kc == 0), stop=(kc == 7),
                     r=[WA, ht], w=[pd])
        P.copy(ds[:], pd[:], r=[pd], w=[ds])
        P.dma(rep(G.vb[t0:t0 + 512, :], "(j p) c -> p j c", p=128), vs[:, :, 0:256], r=[vs], eng="pool")
        P.dma(rep(G.vd[t0:t0 + 512, :], "(j p) c -> p j c", p=128), vs[:, :, 256:512], r=[vs], eng="pool")
        P.dma(rep(G.dt[t0:t0 + 512, :], "(j p) h -> p j h", p=128), ds[:], r=[ds], eng="pool")
    P.finalize()


def ph_conva(nc, G, l):
    S = G.S
    P = Prog(nc, G.pool)
    cw = P.sb("cw", [128, 3, 2], F32)
    for k in range(3):
        P.dma(cw[:, k, :], rep(G.conv_a_w[l, k], "(c p) -> p c", p=128), w=[cw], allow_slow_non_contiguous=True)
    uv = rep(G.uaT, "(c p) s -> p c s", p=128)
    TT = 512
    ins = [P.sb("cin%d" % i, [128, 6, TT + 2], BF16) for i in range(2)]
    for i in range(2):
        P.memset(ins[i][:, :, 0:2], 0.0, w=[ins[i]])
    pt = P.sb("cp", [128, TT + 2], F32)
    acc = P.sb("cacc", [128, TT], F32)
    ys = [P.sb("cy%d" % i, [128, 2, TT], BF16) for i in range(2)]
    for t in range(S // TT):
        t0 = t * TT
        it = ins[t % 2]
        if t == 0:
            P.dma(it[:, :, 2:TT + 2], uv[:, :, 0:TT], w=[it])
        else:
            P.dma(it[:, :, :], uv[:, :, t0 - 2:t0 + TT], w=[it])
        y = ys[t % 2]
        for fc in range(2):
            P.tt(pt[:], it[:, fc, :], it[:, 4 + fc, :], ALU.mult, r=[it], w=[pt])
            P.ts(acc[:], pt[:, 2:TT + 2], cw[:, 2, fc:fc + 1], None, ALU.mult, r=[pt, cw], w=[acc])
            P.stt(acc[:], pt[:, 1:TT + 1], cw[:, 1, fc:fc + 1], acc[:], ALU.mult, ALU.add, r=[pt, cw, acc], w=[acc])
            P.stt(acc[:], pt[:, 0:TT], cw[:, 0, fc:fc + 1], acc[:], ALU.mult, ALU.add, r=[pt, cw, acc], w=[acc])
            P.tt(y[:, fc, :], acc[:], it[:, 2 + fc, 2:TT + 2], ALU.mult, r=[acc, it], w=[y])
        P.dma(rep(G.yT[0], "(c p) s -> p c s", p=128)[:, :, t0:t0 + TT], y[:], r=[y], eng="pool")
    P.finalize()


def ph_ssd(nc, G, l):
    S = G.S
    P = Prog(nc, G.pool)
    T_ = 256
    ident, identf = mk_ident(P)
    U32 = P.sb("U32", [128, 128], F32)
    P.affine(U32, [[-1, 128]], 0, 1, ALU.is_gt, 0.0, 1.0)
    ones32 = P.sb("ones32", [128, 128], F32)
    P.memset(ones32[:], 1.0, w=[ones32])
    onesb = P.sb("onesb", [128, 128], BF16)
    P.memset(onesb[:], 1.0, w=[onesb])
    L0 = P.sb("L0", [128, 256], F32)
    P.affine(L0, [[1, 256]], 0, -1, ALU.is_ge, 0.0, 1.0)
    NEG0 = P.sb("NEG0", [128, 256], F32)
    P.affine(NEG0, [[1, 256]], 0, -1, ALU.is_ge, -BIG, 0.0)
    mh = P.sb("mh", [128, 256], F32)
    P.memset(mh[:], -0.5, w=[mh])
    cw = P.sb("cw", [128, 4, 4], F32)
    for k in range(4):
        P.dma(cw[:, k, :], rep(G.ssm_conv_w[l, k], "(c p) -> p c", p=128), w=[cw], allow_slow_non_contiguous=True)
    cb = P.sb("cb", [128, 4], F32)
    P.dma(cb[:], rep(G.ssm_conv_b[l], "(c p) -> p c", p=128), w=[cb], allow_slow_non_contiguous=True)
    dtb = P.sb("dtb", [128, 4], F32)
    P.dma(dtb[:], G.ssm_dt_bias[l].partition_broadcast(128), w=[dtb])
    Arow = P.sb("Arow", [128, 4], F32)
    P.dma(Arow[:], G.ssm_a_log[l].partition_broadcast(128), w=[Arow])
    P.act(Arow[:], Arow[:], AF.Exp, r=[Arow], w=[Arow])
    P.ts(Arow[:], Arow[:], -1.0, None, ALU.mult, r=[Arow], w=[Arow])
    Dcol = P.sb("Dcol", [128, 2], F32)
    for g in range(2):
        for e_ in range(2):
            P.dma(Dcol[e_ * 64:(e_ + 1) * 64, g:g + 1], G.ssm_d[l, 2 * g + e_:2 * g + e_ + 1].partition_broadcast(64), w=[Dcol])
    ng = P.sb("ng", [128, 2], F32)
    P.dma(ng[:], rep(G.ssm_norm_g[l], "(g p) -> p g", p=128), w=[ng], allow_slow_non_contiguous=True)

    XIN = [P.sb("XIN%d" % i, [128, 4, T_ + 3], BF16) for i in range(2)]
    ZIN = [P.sb("ZIN%d" % i, [128, 2, T_], BF16) for i in range(2)]
    DTIN = [P.sb("DTIN%d" % i, [128, 2, 4], F32) for i in range(2)]
    P.memset(XIN[0][:, :, 0:3], 0.0, w=[XIN[0]])
    acc = [P.sb("acc%d" % i, [128, T_], F32) for i in range(4)]
    xcs = [[P.sb("xc%d_%d" % (j, i), [128, T_], BF16) for i in range(4)] for j in range(2)]
    dts = P.sb("dts", [128, 2, 4], F32)
    dte_ = P.sb("dte_", [128, 2, 4], F32)
    dsps = [P.sb("dsp%d" % j, [128, 2, 4], F32) for j in range(2)]
    a_ts = [P.sb("a_t%d" % j, [128, 2, 4], F32) for j in range(2)]
    Xpads = [[P.sb("Xpad%d_%d" % (j, i), [128, 2, 2, 128], BF16) for i in range(2)] for j in range(2)]
    for j in range(2):
        for i in range(2):
            P.memset(Xpads[j][i][:], 0.0, w=[Xpads[j][i]], eng=("dve" if i == 0 else "pool"))
    Btoks = [[P.sb("Btok%d_%d" % (j, i), [128, 128], BF16) for i in range(2)] for j in range(2)]
    Xdte = [P.sb("Xdte%d" % i, [128, 256], BF16) for i in range(2)]
    aL0 = [P.sb("aL0_%d" % i, [128, 256], F32) for i in range(2)]
    aL1 = [P.sb("aL1_%d" % i, [128, 128], F32) for i in range(2)]
    Dt = [P.sb("Dt%d" % i, [128, 384], F32) for i in range(2)]
    Et = [P.sb("Et%d" % i, [128, 256], F32) for i in range(4)]
    Mt = [P.sb("Mt%d" % i, [128, 384], BF16) for i in range(2)]
    Cpt = [P.sb("Cpt%d" % i, [128, 256], BF16) for i in range(2)]
    H32 = P.sb("H32", [128, 2, 64], F32)
    Hpad = P.sb("Hpad", [128, 2, 128], BF16)
    P.memset(H32[:], 0.0, w=[H32])
    P.memset(Hpad[:], 0.0, w=[Hpad])
    yf = P.sb("yf", [128, 256], F32)
    szs = [[P.sb("sz%d_%d" % (j, g), [128, 256], F32) for g in range(2)] for j in range(2)]
    gt = P.sb("gt", [128, 256], F32)
    sq = P.sb("sq", [128, 256], BF16)
    rs = P.sb("rs", [128, 256], F32)
    yst = [P.sb("yst%d" % i, [128, 2, 256], BF16) for i in range(2)]
    segp = [P.ps("segp%d" % i, [128, 384]) for i in range(2)]
    csbp = P.ps("csbp", [128, 256])
    Gp = [P.ps("Gp%d" % i, [128, 384]) for i in range(2)]
    Yg = [P.ps("Yg%d" % i, [128, 256]) for i in range(2)]
    misc = P.ps("misc", [128, 512])
    ptb = T(misc.h[:, 0:256].bitcast(BF16), "ptb")
    HSb = T(misc.h[:, 256:512], "HSb")
    ptb.res = misc.res
    HSb.res = misc.res

    xv = rep(G.xbcT, "(c p) s -> p c s", p=128)
    zv = rep(G.zT, "(g p) s -> p g s", p=128)
    yv = rep(G.yT[2], "(g p) s -> p g s", p=128)
    nchunks = S // T_
    hc = 0

    def front_parts(c):
        t0 = c * T_
        xin = XIN[c % 2]
        zin = ZIN[c % 2]
        dtin = DTIN[c % 2]
        xc = xcs[c % 2]
        dsp = dsps[c % 2]
        a_t = a_ts[c % 2]
        Xpad = Xpads[c % 2]
        Btok = Btoks[c % 2]

        def conv(ct):
            P.ts(acc[ct][:], xin[:, ct, 0:T_], cw[:, 0, ct:ct + 1], None, ALU.mult, r=[xin, cw], w=[acc[ct]])
            for k in range(1, 4):
                P.stt(acc[ct][:], xin[:, ct, k:k + T_], cw[:, k, ct:ct + 1], acc[ct][:], ALU.mult, ALU.add,
                      r=[xin, cw, acc[ct]], w=[acc[ct]])
            P.act(xc[ct][:], acc[ct][:], AF.Silu, r=[acc[ct], cb], w=[xc[ct]], bias=cb[:, ct:ct + 1])

        def xtr(g):
            for t in range(2):
                sl = ptb[:, (2 * t + g) * 128:(2 * t + g + 1) * 128]
                P.tr(sl, xc[g][:, t * 128:(t + 1) * 128], ident[:], r=[xc[g], ident], w=[ptb])
                for e_ in range(2):
                    P.ts(Xpad[t][:, g, e_, e_ * 64:(e_ + 1) * 64],
                         ptb[:, (2 * t + g) * 128 + e_ * 64:(2 * t + g) * 128 + (e_ + 1) * 64],
                         dsp[:, t, 2 * g + e_:2 * g + e_ + 1], None, ALU.mult, r=[ptb, dsp], w=[Xpad[t]])

        def p0():
            if c == 0:
                P.dma(xin[:, :, 3:T_ + 3], xv[:, :, 0:T_], w=[xin])
            else:
                P.dma(xin[:, :, :], xv[:, :, t0 - 3:t0 + T_], w=[xin])
            P.dma(zin[:], zv[:, :, t0:t0 + T_], w=[zin])
            P.dma(dtin[:], rep(G.dt[t0:t0 + T_, :], "(t p) h -> p t h", p=128), w=[dtin])
            for t in range(2):
                P.tt(dts[:, t, :], dtin[:, t, :], dtb[:], ALU.add, r=[dtin, dtb], w=[dts])
            P.act(dte_[:], dts[:], AF.Exp, r=[dts], w=[dte_])
            P.act(dsp[:], dte_[:], AF.Ln, r=[dte_], w=[dsp], bias=1.0)
            for t in range(2):
                P.tt(a_t[:, t, :], dsp[:, t, :], Arow[:], ALU.mult, r=[dsp, Arow], w=[a_t])
            conv(0)

        def p1():
            conv(1)
            xtr(0)

        def p2():
            conv(2)
            xtr(1)

        def p3():
            conv(3)
            for g in range(2):
                P.act(szs[c % 2][g][:], zin[:, g, :], AF.Silu, r=[zin], w=[szs[c % 2][g]])
            for t in range(2):
                sl = ptb[:, t * 128:(t + 1) * 128]
                P.tr(sl, xc[2][:, t * 128:(t + 1) * 128], ident[:], r=[xc[2], ident], w=[ptb])
                P.copy(Btok[t][:], sl, r=[ptb], w=[Btok[t]], eng="act")

        return [p0, p1, p2, p3]

    for f_ in front_parts(0):
        f_()
    for c in range(nchunks):
        t0 = c * T_
        zin = ZIN[c % 2]
        xc = xcs[c % 2]
        dsp = dsps[c % 2]
        a_t = a_ts[c % 2]
        Xpad = Xpads[c % 2]
        Btok = Btoks[c % 2]
        if c > 0:
            for f_ in front_parts(c):
                f_()
        nxt = [None] * 4
        for g in range(2):
            gr = slice(g * 64, (g + 1) * 64)
            P.mm(Gp[g][:, 0:256], xc[2][gr, 0:128], xc[3][gr, 0:256], r=[xc[2], xc[3]], w=[Gp[g]])
            P.mm(Gp[g][:, 256:384], xc[2][gr, 128:256], xc[3][gr, 128:256], r=[xc[2], xc[3]], w=[Gp[g]])
        def h_pre(h):
            b = (hc + h) % 2
            P.ts(aL0[b][:], L0[:], a_t[:, 0, h:h + 1], None, ALU.mult, r=[L0, a_t], w=[aL0[b]])
            P.ts(aL1[b][:], L0[:, 0:128], a_t[:, 1, h:h + 1], None, ALU.mult, r=[L0, a_t], w=[aL1[b]])

        def h_seg(h):
            b = (hc + h) % 2
            sp_ = segp[b]
            P.mm(sp_[:, 0:256], U32[:], aL0[b][:], start=True, stop=False, r=[U32, aL0[b]], w=[sp_])
            P.mm(sp_[:, 128:256], ones32[:], aL1[b][:], start=False, stop=False, r=[ones32, aL1[b]], w=[sp_])
            P.mm(sp_[:, 0:256], identf[:], NEG0[:], start=False, stop=True, r=[identf, NEG0], w=[sp_])
            P.mm(sp_[:, 256:384], U32[:], aL1[b][:], start=True, stop=False, r=[U32, aL1[b]], w=[sp_])
            P.mm(sp_[:, 256:384], identf[:], NEG0[:, 0:128], start=False, stop=True, r=[identf, NEG0], w=[sp_])
            P.mm(csbp[:, 0:256], ones32[:], aL0[b][:], start=True, stop=False, r=[ones32, aL0[b]], w=[csbp])
            P.mm(csbp[:, 128:256], ones32[:], aL1[b][:], start=False, stop=True, r=[ones32, aL1[b]], w=[csbp])

        def h_act(h):
            b = (hc + h) % 2
            P.act(Dt[b][:], segp[b][:], AF.Exp, r=[segp[b]], w=[Dt[b]])
            P.act(Et[h][:], csbp[:], AF.Exp, r=[csbp], w=[Et[h]])

        def h_post(h):
            b = (hc + h) % 2
            g, e_ = h // 2, h % 2
            gr = slice(g * 64, (g + 1) * 64)
            P.tt(Mt[b][:], Dt[b][:], Gp[g][:], ALU.mult, r=[Dt[b], Gp[g]], w=[Mt[b]])
            P.tt(Cpt[b][gr, :], xc[3][gr, :], Et[h][gr, :], ALU.mult, r=[xc[3], Et[h]], w=[Cpt[b]])
            P.mm(Yg[g][:, 0:256], Xpad[0][:, g, e_, :], Mt[b][:, 0:256], start=(e_ == 0), stop=False,
                 r=[Xpad[0], Mt[b]], w=[Yg[g]])
            P.mm(Yg[g][:, 128:256], Xpad[1][:, g, e_, :], Mt[b][:, 256:384], start=False, stop=False,
                 r=[Xpad[1], Mt[b]], w=[Yg[g]])
            P.mm(Yg[g][:, 0:256], Hpad[gr, e_, :], Cpt[b][gr, :], start=False, stop=(e_ == 1),
                 r=[Hpad, Cpt[b]], w=[Yg[g]])
            P.ts(Xdte[0][:, h * 64:(h + 1) * 64], Xpad[0][:, g, e_, e_ * 64:(e_ + 1) * 64], Dt[b][:, 255:256], None,
                 ALU.mult, r=[Xpad[0], Dt[b]], w=[Xdte[0]])
            P.ts(Xdte[1][:, h * 64:(h + 1) * 64], Xpad[1][:, g, e_, e_ * 64:(e_ + 1) * 64], Dt[b][:, 383:384], None,
                 ALU.mult, r=[Xpad[1], Dt[b]], w=[Xdte[1]])

        h_pre(0)
        h_seg(0)
        for h in range(4):
            if h + 1 < 4:
                h_pre(h + 1)
            h_act(h)
            if h + 1 < 4:
                h_seg(h + 1)
            h_post(h)
            if nxt[h] is not None:
                nxt[h]()
        P.mm(HSb[:], Btok[0][:], Xdte[0][:], start=True, stop=False, r=[Btok[0], Xdte[0]], w=[HSb])
        P.mm(HSb[:], Btok[1][:], Xdte[1][:], start=False, stop=True, r=[Btok[1], Xdte[1]], w=[HSb])
        for g in range(2):
            gr = slice(g * 64, (g + 1) * 64)
            for e_ in range(2):
                h = 2 * g + e_
                P.stt(H32[gr, e_, :], H32[gr, e_, :], Et[h][gr, 255:256], HSb[gr, h * 64:(h + 1) * 64], ALU.mult, ALU.add,
                      r=[H32, Et[h], HSb], w=[H32])
                P.copy(Hpad[gr, e_, e_ * 64:(e_ + 1) * 64], H32[gr, e_, :], r=[H32], w=[Hpad])
        ys = yst[c % 2]
        for g in range(2):
            P.stt(yf[:], xc[g][:], Dcol[:, g:g + 1], Yg[g][:], ALU.mult, ALU.add, r=[xc[g], Dcol, Yg[g]], w=[yf])
            sz = szs[c % 2][g]
            P.tt(gt[:], yf[:], sz[:], ALU.mult, r=[yf, sz], w=[gt])
            P.tt(sq[:], gt[:], gt[:], ALU.mult, r=[gt], w=[sq])
            P.mm(csbp[:], onesb[:], sq[:], r=[onesb, sq], w=[csbp])
            P.ts(rs[:], csbp[:], 1.0 / 128, EPS, ALU.mult, ALU.add, r=[csbp], w=[rs])
            P.act(rs[:], rs[:], AF.Ln, r=[rs], w=[rs])
            P.act(rs[:], rs[:], AF.Exp, r=[rs], w=[rs], scale=-0.5)
            P.stt(ys[:, g, :], gt[:], ng[:, g:g + 1], rs[:], ALU.mult, ALU.mult, r=[gt, ng, rs], w=[ys])
        P.dma(yv[:, :, t0:t0 + T_], ys[:], r=[ys], eng="pool")
    P.finalize()


def ph_sb(nc, G, l):
    S = G.S
    P = Prog(nc, G.pool)
    NT = S // 512
    NK = S // 128
    ident, identf = mk_ident(P)
    IUf = P.sb("IUf", [128, 128], F32)
    P.affine(IUf, [[-1, 128]], 0, 1, ALU.is_ge, 0.0, -1.0)
    IUn = P.sb("IUn", [128, 128], BF16)
    P.copy(IUn[:], IUf[:], r=[IUf], w=[IUn])
    onesb = P.sb("onesb", [128, 2], BF16)
    P.memset(onesb[:], 1.0, w=[onesb])
    CM = []
    f = P.sb("CMf", [128, 512], F32)
    for d in range(4):
        P.affine(f, [[1, 512]], -128 * d, -1, ALU.is_gt, -BIG, 0.0)
        b = P.sb("CM%d" % d, [128, 512], BF16)
        P.copy(b[:], f[:], r=[f], w=[b])
        CM.append(b)
    qz = [[P.sb("qz%d_%d" % (j, i), [128, S], BF16) for i in range(2)] for j in range(2)]
    kz = [[P.sb("kz%d_%d" % (j, i), [128, S], BF16) for i in range(2)] for j in range(2)]
    for j in range(2):
        for i in range(2):
            P.memset(qz[j][i][64:128, :], 0.0, w=[qz[j][i]], eng=("dve" if i == 0 else "pool"))
            P.memset(kz[j][i][64:128, :], 0.0, w=[kz[j][i]], eng=("dve" if i == 0 else "pool"))
    v = P.sb("v", [128, NK, 256], BF16)
    vv = rep(G.vb, "(n p) c -> p n c", p=128)
    step = 2048
    for s0 in range(0, S, step):
        s1 = min(S, s0 + step)
        P.dma(v[:, s0 // 128:s1 // 128, :], vv[:, s0 // 128:s1 // 128, :], w=[v])

    def load_qk(hp):
        for i in range(2):
            h = 2 * hp + i
            for s0 in range(0, S, 4096):
                s1 = min(S, s0 + 4096)
                P.dma(qz[hp % 2][i][0:64, s0:s1], G.qbT[h * 64:(h + 1) * 64, s0:s1], w=[qz[hp % 2][i]])
                P.dma(kz[hp % 2][i][0:64, s0:s1], G.kbT[h * 64:(h + 1) * 64, s0:s1], w=[kz[hp % 2][i]])

    load_qk(0)
    load_qk(1)
    Zb = [[P.ps("Zb%d_%d" % (i, j), [128, 512]) for j in range(3)] for i in range(2)]
    OC = [P.ps("OC%d" % i, [128, 512]) for i in range(2)]
    Op = [T(rep(OC[i].h[:, 0:256], "p (q d) -> p q d", d=64), "Op%d" % i) for i in range(2)]
    csp = [T(OC[0].h[:, 256 + 4 * i:260 + 4 * i], "csp%d" % i) for i in range(2)]
    csp2 = T(OC[0].h[:, 256:264], "csp2")
    csp2.res = OC[0].res
    pTv = [T(OC[i].h[:, 384:512].bitcast(BF16), "pTv%d" % i) for i in range(2)]
    for i in range(2):
        Op[i].res = OC[i].res
        csp[i].res = OC[0].res
        pTv[i].res = OC[i].res
    Et = [[P.sb("Et%d_%d" % (i, j), [128, 512], F32) for j in range(2)] for i in range(2)]
    SPt = [[P.sb("SPt%d_%d" % (i, j), [128, 512], BF16) for j in range(2)] for i in range(2)]
    Pt = [[P.sb("Pt%d_%d" % (i, j), [128, 512], BF16) for j in range(2)] for i in range(2)]
    acc = [[P.sb("acc%d_%d" % (i, j), [128, 4, 64], F32) for j in range(2)] for i in range(2)]
    tmpa = [P.sb("tmpa%d" % i, [128, 4, 64], F32) for i in range(2)]
    accb = [P.sb("accb%d" % i, [128, 4, 64], BF16) for i in range(2)]
    dd2 = [P.sb("dd2_%d" % j, [128, 8], F32) for j in range(2)]
    dd = [[T(dd2[j].h[:, 4 * i:4 * i + 4], "dd%d_%d" % (i, j)) for j in range(2)] for i in range(2)]
    for i in range(2):
        for j in range(2):
            dd[i][j].res = dd2[j].res
    yst = [P.sb("yst%d" % i, [64, 512], BF16) for i in range(4)]
    yc = [0]

    for hp in range(2):
        its = [(c, n) for c in range(NT) for n in range(0, 4 * c + 4)]
        N = len(its)

        def dof(k):
            c, n = its[k]
            return max(0, n - 4 * c)

        def stA(k):
            c, n = its[k]
            co = 128 * dof(k)
            qs = slice(c * 512 + co, (c + 1) * 512)
            ks = slice(n * 128, (n + 1) * 128)
            diag = n >= 4 * c
            for i in range(2):
                Z = Zb[i][k % 3]
                P.mm(Z[:, co:512], kz[hp % 2][i][:, ks], qz[hp % 2][i][:, qs], start=True, stop=(not diag),
                     r=[kz[hp % 2][i], qz[hp % 2][i]], w=[Z])
                if diag:
                    P.mm(Z[:, co:512], ident[:], CM[n - 4 * c][:, co:512], start=False, stop=True,
                         r=[ident, CM[n - 4 * c]], w=[Z])

        def stB(k):
            co = 128 * dof(k)
            for i in range(2):
                P.act(Et[i][k % 2][:, co:512], Zb[i][k % 3][:, co:512], AF.Exp, r=[Zb[i][k % 3]], w=[Et[i][k % 2]])
            for i in range(2):
                P.act(SPt[i][k % 2][:, co:512], Et[i][k % 2][:, co:512], AF.Ln, r=[Et[i][k % 2]], w=[SPt[i][k % 2]], bias=1.0)

        def stC(k):
            co = 128 * dof(k)
            for i in range(2):
                Z = Zb[i][k % 3]
                P.mm(Z[:, co:512], IUn[:], SPt[i][k % 2][:, co:512], start=False, stop=True, r=[IUn, SPt[i][k % 2]], w=[Z], sgc=True)

        def stD(k):
            co = 128 * dof(k)
            for i in range(2):
                P.act(Pt[i][k % 2][:, co:512], Zb[i][k % 3][:, co:512], AF.Exp, r=[Zb[i][k % 3]], w=[Pt[i][k % 2]])

        def stE(k):
            c, n = its[k]
            d0 = dof(k)
            for i in range(2):
                h = 2 * hp + i
                for qi in range(d0, 4):
                    P.mm(Op[i][:, qi, :], Pt[i][k % 2][:, qi * 128:(qi + 1) * 128], v[:, n, h * 64:(h + 1) * 64],
                         r=[Pt[i][k % 2], v], w=[Op[i]])
                for qi in range(d0, 4):
                    P.mm(csp[i][:, qi:qi + 1], SPt[i][k % 2][:, qi * 128:(qi + 1) * 128], onesb[:, 0:1],
                         r=[SPt[i][k % 2], onesb], w=[csp[i]])

        def stFd(k):
            c, n = its[k]
            if n > 0:
                P.act(dd2[k % 2][:], csp2[:], AF.Exp, r=[csp2], w=[dd2[k % 2]], scale=-1.0)

        def stF(k):
            c, n = its[k]
            cb = c % 2
            for i in range(2):
                if n == 0:
                    P.copy(acc[i][cb][:], Op[i][:], r=[Op[i]], w=[acc[i][cb]])
                else:
                    d0 = dof(k)
                    P.tt(tmpa[i][:, d0:4, :], acc[i][cb][:, d0:4, :],
                         dd[i][k % 2][:, d0:4].unsqueeze(2).to_broadcast([128, 4 - d0, 64]), ALU.mult,
                         r=[acc[i][cb], dd[i][k % 2]], w=[tmpa[i]])
                    P.tt(acc[i][cb][:, d0:4, :], tmpa[i][:, d0:4, :], Op[i][:, d0:4, :], ALU.add, r=[tmpa[i], Op[i]], w=[acc[i][cb]])
            if n == 4 * c + 3:
                qs = slice(c * 512, (c + 1) * 512)
                for i in range(2):
                    h = 2 * hp + i
                    P.copy(accb[i][:], acc[i][cb][:], r=[acc[i][cb]], w=[accb[i]])
                    ys = yst[yc[0] % 4]
                    yc[0] += 1
                    for half in range(2):
                        for q2 in range(2):
                            qi = 2 * half + q2
                            P.tr(pTv[i][0:64, q2 * 128:(q2 + 1) * 128], accb[i][:, qi, :], ident[:], r=[accb[i], ident], w=[pTv[i]])
                        P.copy(ys[:, half * 256:(half + 1) * 256], pTv[i][0:64, :], r=[pTv[i]], w=[ys], eng="act")
                    P.dma(G.yT[1, h * 64:(h + 1) * 64, qs], ys[:], r=[ys], eng="pool")

        stA(0)
        for k in range(N + 2):
            if k + 1 < N:
                stA(k + 1)
            if k < N:
                stB(k)
            if 0 <= k - 2 < N:
                stFd(k - 2)
            if 0 <= k - 1 < N:
                stD(k - 1)
            if k < N:
                stC(k)
            if 0 <= k - 2 < N:
                stF(k - 2)
            if 0 <= k - 1 < N:
                stE(k - 1)
    P.finalize()


def ph_moba(nc, G, l, stabilize=True):
    S = G.S
    P = Prog(nc, G.pool)
    NT = S // 512
    NK = S // 128
    NB = S // 256
    assert NB <= 32
    ident, identf = mk_ident(P)
    ones32 = P.sb("ones32", [128, 64], F32)
    P.memset(ones32[:], 1.0, w=[ones32])
    onesb = P.sb("onesb", [128, 128], BF16)
    P.memset(onesb[:], 1.0, w=[onesb])
    CM = []
    f = P.sb("CMf", [128, 512], F32)
    for d in range(4):
        P.affine(f, [[1, 512]], -128 * d, -1, ALU.is_ge, -BIG, 0.0)
        b = P.sb("CM%d" % d, [128, 512], BF16)
        P.copy(b[:], f[:], r=[f], w=[b])
        CM.append(b)
    KE = [P.sb("KE%d" % i, [128, S], BF16) for i in range(2)]
    QN = [P.sb("QN%d" % i, [128, S], BF16) for i in range(2)]
    for i in range(2):
        P.memset(KE[i][64:128, :], 0.0, w=[KE[i]], eng=("dve" if i == 0 else "pool"))
        P.memset(QN[i][64:128, :], 0.0, w=[QN[i]], eng=("dve" if i == 0 else "pool"))
    ohf = P.sb("ohf", [128, 2048], F32)
    for s0 in range(0, S, 2048):
        w_ = min(2048, S - s0)
        ohv = T(rep(ohf.h[64:96, 0:w_], "p (b k) -> p b k", k=256), "ohv")
        ohv.res = ohf.res
        P.affine(ohv, [[-1, w_ // 256], [0, 256]], -(s0 // 256), 1, ALU.is_equal, 0.0, 1.0)
        for i in range(2):
            P.copy(KE[i][64:96, s0:s0 + w_], ohf[64:96, 0:w_], r=[ohf], w=[KE[i]], eng=("dve" if i == 0 else "act"))
    Vaug = P.sb("Vaug", [128, NK, 4, 65], BF16)
    vtmp = [P.sb("vtmp%d" % i, [128, 8, 256], BF16) for i in range(2)]
    vv = rep(G.vd, "(n p) c -> p n c", p=128)
    P.memset(Vaug[:, :, :, 64:65], 1.0, w=[Vaug])
    step = 1024
    for si, s0 in enumerate(range(0, S, step)):
        s1 = min(S, s0 + step)
        n0, n1 = s0 // 128, s1 // 128
        vt = vtmp[si % 2]
        P.dma(vt[:, 0:n1 - n0, :], vv[:, n0:n1, :], w=[vt])
        for h in range(4):
            P.copy(Vaug[:, n0:n1, h, 0:64], vt[:, 0:n1 - n0, h * 64:(h + 1) * 64], r=[vt], w=[Vaug],
                   eng=("act" if h % 2 == 0 else "dve"))
    km32 = [P.sb("km32_%d" % i, [64, 32], F32) for i in range(2)]
    kmhi = [P.sb("kmhi%d" % i, [64, 32], BF16) for i in range(2)]
    kmhf = [P.sb("kmhf%d" % i, [64, 32], F32) for i in range(2)]
    kmlo = [P.sb("kmlo%d" % i, [64, 32], BF16) for i in range(2)]
    for i in range(2):
        P.memset(km32[i][:], 0.0, w=[km32[i]])

    Zp = [[P.ps("Zp%d_%d" % (i, j), [128, 512]) for j in range(2)] for i in range(2)]
    OT = [[P.ps("OT%d_%d" % (i, j), [128, 512]) for j in range(2)] for i in range(2)]
    KM = P.sb("KM", [128, 2], F32)
    ksqa = P.sb("ksqa", [64, S], BF16)
    kmx = P.sb("kmx", [128, 16], F32)
    kqs = [OT[1][0], OT[1][1]]
    NEGVB = P.sb("NEGVB", [128, 32, 2, 32], F32)
    P.affine(NEGVB, [[1, 32], [0, 2], [-1, 32]], 0, 0, ALU.is_gt, -BIG, 0.0)
    OWNB = P.sb("OWNB", [128, 32, 2, 32], F32)
    P.affine(OWNB, [[1, 32], [0, 2], [-1, 32]], 0, 0, ALU.is_equal, 0.0, 1.0)
    NEGV3 = rep(NEGVB[:], "p a b n -> p (a b) n")
    OWN3 = rep(OWNB[:], "p a b n -> p (a b) n")
    QB = 16
    qsq = [P.sb("qsq%d" % i, [64, QB * 128], BF16) for i in range(2)]
    gm = [P.sb("gm%d" % i, [128, QB, 32], F32) for i in range(2)]
    g2 = [P.sb("g2_%d" % i, [128, QB, 32], F32) for i in range(2)]
    eq = [P.sb("eq%d" % i, [128, QB, 32], F32) for i in range(2)]
    mx = [P.sb("mx%d" % i, [128, QB], F32) for i in range(2)]
    mq = [P.sb("mq%d" % i, [128, QB], F32) for i in range(2)]
    nb = [P.sb("nb%d" % i, [128, QB, 32], BF16) for i in range(2)]
    Pt = [[P.sb("Pt%d_%d" % (i, j), [128, 512], BF16) for j in range(2)] for i in range(2)]
    RL = [P.sb("RL%d" % i, [128, 512], F32) for i in range(2)]
    bcs = [P.sb("bcs%d" % i, [64, 512], F32) for i in range(2)]
    yo = [P.sb("yo%d" % i, [64, 512], BF16) for i in range(4)]
    gpb = [T(rep(Zp[i][0].h[:, :], "p (q n) -> p q n", n=32), "gpb%d" % i) for i in range(2)]
    pTb = [T(Zp[i][1].h[:, :].bitcast(BF16), "pTb%d" % i) for i in range(2)]
    qnp = [T(OT[i][0].h[:, 0:QB], "qnp%d" % i) for i in range(2)]
    for i in range(2):
        gpb[i].res = Zp[i][0].res
        pTb[i].res = Zp[i][1].res
        qnp[i].res = OT[i][0].res
    yc = [0]
    zc = [0]
    kc_ = 0
    for hp in range(2):
        for i in range(2):
            h = 2 * hp + i
            for s0 in range(0, S, 4096):
                s1 = min(S, s0 + 4096)
                P.dma(QN[i][0:64, s0:s1], G.qdT[h * 64:(h + 1) * 64, s0:s1], w=[QN[i]])
                P.dma(KE[i][0:64, s0:s1], G.kdT[h * 64:(h + 1) * 64, s0:s1], w=[KE[i]])
        for i in range(2):
            P.op("dve", (lambda i_: (lambda e: e.tensor_reduce(out=km32[i_][:, 0:NB], in_=rep(KE[i_][0:64, :], "p (b k) -> p b k", k=256),
                                                               axis=AX.X, op=ALU.add)))(i), r=[KE[i]], w=[km32[i]])
            P.copy(kmhi[i][:], km32[i][:], r=[km32[i]], w=[kmhi[i]])
            P.copy(kmhf[i][:], kmhi[i][:], r=[kmhi[i]], w=[kmhf[i]])
            P.tt(kmlo[i][:], km32[i][:], kmhf[i][:], ALU.subtract, r=[km32[i], kmhf[i]], w=[kmlo[i]])
            if stabilize:
                nch = S // 512
                for ci in range(nch):
                    s0 = ci * 512
                    P.tt(ksqa[:, s0:s0 + 512], KE[i][0:64, s0:s0 + 512], KE[i][0:64, s0:s0 + 512], ALU.mult, r=[KE[i]], w=[ksqa])
                for ci in range(nch):
                    s0 = ci * 512
                    kqb = kqs[ci % 2]
                    P.mm(kqb[:], onesb[0:64, :], ksqa[:, s0:s0 + 512], r=[onesb, ksqa], w=[kqb])
                    P.op("dve", (lambda o_, i_: (lambda e: e.tensor_reduce(out=o_, in_=i_, axis=AX.X, op=ALU.max)))(
                        kmx[:, ci:ci + 1], kqb[:]), r=[kqb], w=[kmx])
                P.op("dve", (lambda o_, i_: (lambda e: e.tensor_reduce(out=o_, in_=i_, axis=AX.X, op=ALU.max)))(
                    KM[:, i:i + 1], kmx[:, 0:nch]), r=[kmx], w=[KM])
        for q0 in range(0, NK, QB):
            nq = min(QB, NK - q0)
            cs_ = slice(q0 * 128, (q0 + nq) * 128)
            for i in range(2):
                if stabilize:
                    P.tt(qsq[i][:, 0:nq * 128], QN[i][0:64, cs_], QN[i][0:64, cs_], ALU.mult, r=[QN[i]], w=[qsq[i]])
                for j in range(nq):
                    cj = slice((q0 + j) * 128, (q0 + j + 1) * 128)
                    P.mm(gpb[i][:, j, :], QN[i][0:64, cj], kmhi[i][:], start=True, stop=False, r=[QN[i], kmhi[i]], w=[gpb[i]])
                    P.mm(gpb[i][:, j, :], QN[i][0:64, cj], kmlo[i][:], start=False, stop=True, r=[QN[i], kmlo[i]], w=[gpb[i]])
                    if stabilize:
                        P.mm(qnp[i][:, j:j + 1], qsq[i][:, j * 128:(j + 1) * 128], onesb[0:64, 0:1],
                             r=[qsq[i], onesb], w=[qnp[i]])
            for i in range(2):
                G_ = gm[i]
                P.tt(G_[:, 0:nq, :], gpb[i][:, 0:nq, :], NEGV3[:, q0:q0 + nq, :], ALU.add, r=[gpb[i], NEGVB], w=[G_])
                src = G_
                for it in range(3):
                    P.op("dve", (lambda o_, s_: (lambda e: e.tensor_reduce(out=o_, in_=s_, axis=AX.X, op=ALU.max)))(
                        mx[i][:, 0:nq], src[:, 0:nq, :]), r=[src], w=[mx[i]])
                    if it < 2:
                        P.tt(eq[i][:, 0:nq, :], src[:, 0:nq, :], mx[i][:, 0:nq].unsqueeze(2).to_broadcast([128, nq, 32]),
                             ALU.is_equal, r=[src, mx[i]], w=[eq[i]])
                        P.stt(g2[i][:, 0:nq, :], eq[i][:, 0:nq, :], -1e6, src[:, 0:nq, :], ALU.mult, ALU.add,
                              r=[eq[i], src], w=[g2[i]])
                        src = g2[i]
                P.ts(mx[i][:, 0:nq], mx[i][:, 0:nq], -BIG / 2, None, ALU.max, r=[mx[i]], w=[mx[i]])
                P.tt(eq[i][:, 0:nq, :], G_[:, 0:nq, :], mx[i][:, 0:nq].unsqueeze(2).to_broadcast([128, nq, 32]),
                     ALU.is_ge, r=[G_, mx[i]], w=[eq[i]])
                P.tt(eq[i][:, 0:nq, :], eq[i][:, 0:nq, :], OWN3[:, q0:q0 + nq, :], ALU.max, r=[eq[i], OWNB], w=[eq[i]])
                if stabilize:
                    P.act(mq[i][:, 0:nq], qnp[i][:, 0:nq], AF.Sqrt, r=[qnp[i], KM], w=[mq[i]], scale=KM[:, i:i + 1])
                    P.ts(mq[i][:, 0:nq], mq[i][:, 0:nq], -1.0, -BIG, ALU.mult, ALU.add, r=[mq[i]], w=[mq[i]])
                else:
                    P.memset(mq[i][:], -BIG, w=[mq[i]], eng="dve")
                P.stt(nb[i][:, 0:nq, :], eq[i][:, 0:nq, :], BIG, mq[i][:, 0:nq].unsqueeze(2).to_broadcast([128, nq, 32]),
                      ALU.mult, ALU.add, r=[eq[i], mq[i]], w=[nb[i]])
                for j0 in range(0, nq, 8):
                    nj = min(8, nq - j0)
                    for j in range(j0, j0 + nj):
                        P.tr(pTb[i][64:96, (j - j0) * 128:(j - j0 + 1) * 128], nb[i][:, j, :], ident[:], r=[nb[i], ident], w=[pTb[i]])
                    P.copy(QN[i][64:96, (q0 + j0) * 128:(q0 + j0 + nj) * 128], pTb[i][64:96, 0:nj * 128], r=[pTb[i]], w=[QN[i]],
                           eng="act")
        its = [(c, n) for c in range(NT) for n in range(0, 4 * c + 4)]
        N = len(its)
        zslot = {}

        def mA(k):
            c, n = its[k]
            qs = slice(c * 512, (c + 1) * 512)
            ks = slice(n * 128, (n + 1) * 128)
            diag = n >= 4 * c
            zslot[k] = zc[0] % 2
            zc[0] += 1
            for i in range(2):
                Z = Zp[i][zslot[k]]
                P.mm(Z[:], KE[i][:, ks], QN[i][:, qs], start=True, stop=(not diag), r=[KE[i], QN[i]], w=[Z])
                if diag:
                    P.mm(Z[:], ident[:], CM[n - 4 * c][:], start=False, stop=True, r=[ident, CM[n - 4 * c]], w=[Z])

        def mB(k):
            for i in range(2):
                P.act(Pt[i][k % 2][:], Zp[i][zslot[k]][:], AF.Exp, r=[Zp[i][zslot[k]]], w=[Pt[i][k % 2]])

        def mC(k):
            c, n = its[k]
            for i in range(2):
                h = 2 * hp + i
                P.mm(OT[i][c % 2][0:65, :], Vaug[:, n, h, :], Pt[i][k % 2][:], start=(n == 0), stop=(n == 4 * c + 3),
                     r=[Vaug, Pt[i][k % 2]], w=[OT[i][c % 2]])

        def mFin(c):
            qs = slice(c * 512, (c + 1) * 512)
            for i in range(2):
                h = 2 * hp + i
                O_ = OT[i][c % 2]
                P.op("dve", (lambda o_, i_: (lambda e: e.reciprocal(out=o_, in_=i_)))(RL[i][64:65, :], O_[64:65, :]),
                     r=[O_], w=[RL[i]])
                Zf = Zp[i][zfree[0]]
                P.mm(Zf[0:64, :], ones32[64:65, 0:64], RL[i][64:65, :], r=[ones32, RL[i]], w=[Zf])
                P.copy(bcs[i][:], Zf[0:64, :], r=[Zf], w=[bcs[i]], eng="act")
                y = yo[yc[0] % 4]
                yc[0] += 1
                P.tt(y[:], O_[0:64, :], bcs[i][:], ALU.mult, r=[O_, bcs[i]], w=[y])
                P.dma(G.yT[3, h * 64:(h + 1) * 64, qs], y[:], r=[y], eng="pool")

        zfree = [0]
        mA(0)
        pend = None
        for k in range(N):
            if k + 1 < N:
                mA(k + 1)
            mB(k)
            zfree[0] = zslot[k]
            if pend is not None:
                mFin(pend)
                pend = None
            mC(k)
            c, n = its[k]
            if n == 4 * c + 3:
                pend = c
        mFin(pend)
    P.finalize()


def ph_c1(nc, G, l):
    S = G.S
    P = Prog(nc, G.pool)
    TT = 256
    J = 2
    ident, _ = mk_ident(P)
    K = norm_alloc(P, J)
    Wg = P.sb("Wg", [128, 8, 4096], BF16)
    Wb = P.sb("Wb", [128, 4, 2, D], BF16)
    Wo = P.sb("Wo", [128, 8, D], BF16)
    for kc in range(8):
        load_w(P, Wg, Wg[:, kc, :], G.w_in[l, kc * 128:(kc + 1) * 128, 3076:3076 + 4096])
    for br in range(4):
        for k2 in range(2):
            load_w(P, Wb, Wb[:, br, k2, :], G.w_branch[l, br, k2 * 128:(k2 + 1) * 128, :])
    for kc in range(8):
        load_w(P, Wo, Wo[:, kc, :], G.w_o[l, kc * 128:(kc + 1) * 128, :])
    gb = load_gain(P, "gb", G.norm2_g[l])
    hts = [P.sb("ht%d" % i, [128, 8, TT], BF16) for i in range(2)]
    yts = [P.sb("yt%d" % i, [128, 4, 2, TT], BF16) for i in range(2)]
    xos = [P.sb("xo%d" % i, [128, J, D], F32) for i in range(2)]
    h2s = [P.sb("h2s%d" % i, [128, 8, TT], BF16) for i in range(2)]
    mT = P.sb("mT", [128, 8, TT], BF16)
    sg = [P.sb("sg%d" % i, [128, TT], F32) for i in range(2)]
    tmp = [P.sb("tmp%d" % i, [128, TT], F32) for i in range(2)]
    mg = [P.sb("mg%d" % i, [128, TT], F32) for i in range(2)]
    Gp = [P.ps("Gp%d" % i, [128, TT]) for i in range(2)]
    Pj = [P.ps("Pj%d" % i, [128, TT]) for i in range(2)]
    Op = [P.ps("Op%d" % i, [128, 512]) for i in range(2)]
    xsrc = G.x_in if l == 0 else G.xa
    hTv = rep(G.hT, "(k p) s -> p k s", p=128)
    h2v = rep(G.h2T, "(k p) s -> p k s", p=128)
    cnt = 0
    oc = 0
    pend = None
    for t in range(S // TT):
        t0 = t * TT
        ht = hts[t % 2]
        yt = yts[t % 2]
        xo = xos[t % 2]
        P.dma(ht[:], hTv[:, :, t0:t0 + TT], w=[ht])
        for br in range(4):
            P.dma(yt[:, br, :, :], rep(G.yT[br], "(k p) s -> p k s", p=128)[:, :, t0:t0 + TT], w=[yt])
        P.dma(xo[:], rep(xsrc[t0:t0 + TT, :], "(j p) d -> p j d", p=128), w=[xo])
        for dmc in range(8):
            m = mg[dmc % 2]
            for br in range(4):
                gp = Gp[cnt % 2]
                pj = Pj[cnt % 2]
                s_ = sg[cnt % 2]
                tm = tmp[cnt % 2]
                cnt += 1
                cg = br * 1024 + dmc * 128
                for kc in range(8):
                    P.mm(gp[:], Wg[:, kc, cg:cg + 128], ht[:, kc, :], start=(kc == 0), stop=(kc == 7), r=[Wg, ht], w=[gp])
                for k2 in range(2):
                    P.mm(pj[:], Wb[:, br, k2, dmc * 128:(dmc + 1) * 128], yt[:, br, k2, :], start=(k2 == 0), stop=(k2 == 1),
                         r=[Wb, yt], w=[pj])
                P.act(s_[:], gp[:], AF.Sigmoid, r=[gp], w=[s_])
                if br == 0:
                    P.tt(m[:], s_[:], pj[:], ALU.mult, r=[s_, pj], w=[m])
                else:
                    P.tt(tm[:], s_[:], pj[:], ALU.mult, r=[s_, pj], w=[tm])
                    if br < 3:
                        P.tt(m[:], m[:], tm[:], ALU.add, r=[m, tm], w=[m])
                    else:
                        P.tt(mT[:, dmc, :], m[:], tm[:], ALU.add, r=[m, tm], w=[mT])
        if pend is not None:
            h2 = h2s[pend[0] % 2]
            norm_p2(P, K, ident, h2)
            P.dma(h2v[:, :, pend[1]:pend[1] + TT], h2[:], r=[h2], eng="pool")
            pend = None
        for j in range(J):
            for hf in range(2):
                op_ = Op[oc % 2]
                oc += 1
                for dmc in range(8):
                    P.mm(op_[:], mT[:, dmc, j * 128:(j + 1) * 128], Wo[:, dmc, hf * 512:(hf + 1) * 512], start=(dmc == 0),
                         stop=(dmc == 7), r=[mT, Wo], w=[op_])
                P.tt(xo[:, j, hf * 512:(hf + 1) * 512], xo[:, j, hf * 512:(hf + 1) * 512], op_[:], ALU.add, r=[xo, op_], w=[xo])
        P.dma(rep(G.xm[t0:t0 + TT, :], "(j p) d -> p j d", p=128), xo[:], r=[xo], eng="pool")
        norm_p1(P, K, xo, gb)
        pend = (t, t0)
    if pend is not None:
        h2 = h2s[pend[0] % 2]
        norm_p2(P, K, ident, h2)
        P.dma(h2v[:, :, pend[1]:pend[1] + TT], h2[:], r=[h2], eng="pool")
    P.finalize()


def ph_c2(nc, G, l):
    S = G.S
    P = Prog(nc, G.pool)
    TT = 256
    J = 2
    NF = FF // 128
    last = (l == G.L - 1)
    ident, _ = mk_ident(P)
    K = norm_alloc(P, J)
    SW = 1408
    Wgu = [P.sb("Wgu%d" % q, [128, 8, SW], BF16) for q in range(4)]
    Wd = P.sb("Wd", [128, NF, D], BF16)
    for q in (0, 2, 1, 3):
        for kc in range(8):
            c0 = q * SW
            load_w(P, Wgu[q], Wgu[q][:, kc, :], G.w_gate_up[l, kc * 128:(kc + 1) * 128, c0:c0 + SW])
    for fc in range(NF):
        load_w(P, Wd, Wd[:, fc, :], G.w_down[l, fc * 128:(fc + 1) * 128, :])
    if not last:
        gb = load_gain(P, "gb", G.norm1_g[l + 1])
    if last:
        fgb = P.sb("fgb", [128, D], F32)
        P.dma(fgb[:], G.final_g.partition_broadcast(128), w=[fgb])
    h2s = [P.sb("h2s%d" % i, [128, 8, TT], BF16) for i in range(2)]
    xos = [P.sb("xo%d" % i, [128, J, D], F32) for i in range(2)]
    aT = P.sb("aT", [128, NF, TT], BF16)
    hto = P.sb("hto", [128, 8, TT], BF16)
    sg = [P.sb("sg%d" % i, [128, TT], F32) for i in range(2)]
    Gp = [P.ps("Gp%d" % i, [128, TT]) for i in range(2)]
    Up = [P.ps("Up%d" % i, [128, TT]) for i in range(2)]
    Op = [P.ps("Op%d" % i, [128, 512]) for i in range(2)]
    hTv = rep(G.hT, "(k p) s -> p k s", p=128)
    h2v = rep(G.h2T, "(k p) s -> p k s", p=128)
    cnt = 0
    oc = 0
    pend = None
    for t in range(S // TT):
        t0 = t * TT
        h2 = h2s[t % 2]
        xo = xos[t % 2]
        P.dma(h2[:], h2v[:, :, t0:t0 + TT], w=[h2])
        P.dma(xo[:], rep(G.xm[t0:t0 + TT, :], "(j p) d -> p j d", p=128), w=[xo])
        for fc in range(NF):
            gp = Gp[cnt % 2]
            up = Up[cnt % 2]
            s_ = sg[cnt % 2]
            cnt += 1
            wq = Wgu[fc // 11]
            wu = Wgu[2 + fc // 11]
            fo = (fc % 11) * 128
            for kc in range(8):
                P.mm(gp[:], wq[:, kc, fo:fo + 128], h2[:, kc, :], start=(kc == 0), stop=(kc == 7), r=[wq, h2], w=[gp])
            for kc in range(8):
                P.mm(up[:], wu[:, kc, fo:fo + 128], h2[:, kc, :], start=(kc == 0), stop=(kc == 7), r=[wu, h2], w=[up])
            P.act(s_[:], gp[:], AF.Silu, r=[gp], w=[s_])
            P.tt(aT[:, fc, :], s_[:], up[:], ALU.mult, r=[s_, up], w=[aT])
        if pend is not None:
            norm_p2(P, K, ident, hto)
            P.dma(hTv[:, :, pend:pend + TT], hto[:], r=[hto], eng="pool")
            pend = None
        for j in range(J):
            for hf in range(2):
                op_ = Op[oc % 2]
                oc += 1
                for fc in range(NF):
                    P.mm(op_[:], aT[:, fc, j * 128:(j + 1) * 128], Wd[:, fc, hf * 512:(hf + 1) * 512], start=(fc == 0),
                         stop=(fc == NF - 1), r=[aT, Wd], w=[op_])
                P.tt(xo[:, j, hf * 512:(hf + 1) * 512], xo[:, j, hf * 512:(hf + 1) * 512], op_[:], ALU.add, r=[xo, op_], w=[xo])
        if not last:
            P.dma(rep(G.xa[t0:t0 + TT, :], "(j p) d -> p j d", p=128), xo[:], r=[xo], eng="pool")
            norm_p1(P, K, xo, gb)
            pend = t0
        else:
            norm_stats(P, K, xo, J)
            for j in range(J):
                P.stt(xo[:, j, :], xo[:, j, :], K.rstd[:, j:j + 1], fgb[:], ALU.mult, ALU.mult, r=[xo, K.rstd, fgb], w=[xo])
            P.dma(rep(G.out[t0:t0 + TT, :], "(j p) d -> p j d", p=128), xo[:], r=[xo], eng="pool")
    if pend is not None:
        norm_p2(P, K, ident, hto)
        P.dma(hTv[:, :, pend:pend + TT], hto[:], r=[hto], eng="pool")
    P.finalize()


W_SPECS = [
    ("norm1_g", lambda L: [L, D]), ("w_in", lambda L: [L, D, NIN]), ("conv_a_w", lambda L: [L, 3, 256]),
    ("ssm_conv_w", lambda L: [L, 4, 512]), ("ssm_conv_b", lambda L: [L, 512]), ("ssm_dt_bias", lambda L: [L, 4]),
    ("ssm_a_log", lambda L: [L, 4]), ("ssm_d", lambda L: [L, 4]), ("ssm_norm_g", lambda L: [L, 256]),
    ("w_branch", lambda L: [L, 4, 256, D]), ("w_o", lambda L: [L, D, D]), ("norm2_g", lambda L: [L, D]),
    ("w_gate_up", lambda L: [L, D, 2 * FF]), ("w_down", lambda L: [L, FF, D]), ("final_g", lambda L: [D]),
]


def build(S, L, dbg=(), phases=None):
    nc = bass.Bass("TRN2", target_bir_lowering=False)
    G = NS()
    G.S = S
    G.L = L
    G.pool = SemPool(nc)
    G.x_in = nc.dram_tensor("x", [S, D], F32, kind="ExternalInput").ap()
    for name, shp in W_SPECS:
        setattr(G, name, nc.dram_tensor(name, shp(L), F32, kind="ExternalInput").ap())
    G.out = nc.dram_tensor("out", [S, D], F32, kind="ExternalOutput").ap()

    def scr(name, shape, dt):
        kind = "ExternalOutput" if name in dbg else "Internal"
        t = nc.dram_tensor(name, list(shape), dt, kind=kind).ap()
        setattr(G, name, t)
        return t

    scr("hT", [D, S], BF16)
    scr("h2T", [D, S], BF16)
    scr("xa", [S, D], F32)
    scr("xm", [S, D], F32)
    scr("uaT", [768, S], BF16)
    scr("qbT", [256, S], BF16)
    scr("kbT", [256, S], BF16)
    scr("vb", [S, 256], BF16)
    scr("zT", [256, S], BF16)
    scr("xbcT", [512, S], BF16)
    scr("dt", [S, 4], F32)
    scr("qdT", [256, S], BF16)
    scr("kdT", [256, S], BF16)
    scr("vd", [S, 256], BF16)
    scr("yT", [4, 256, S], BF16)
    run = (lambda p: True) if phases is None else (lambda p: p in phases)
    if run("norm0"):
        ph_norm0(nc, G)
    for l in range(L):
        if run("proj"):
            ph_proj(nc, G, l)
        if run("conva"):
            ph_conva(nc, G, l)
        if run("ssd"):
            ph_ssd(nc, G, l)
        if run("sb"):
            ph_sb(nc, G, l)
        if run("moba"):
            ph_moba(nc, G, l)
        if run("c1"):
            ph_c1(nc, G, l)
        if run("c2"):
            ph_c2(nc, G, l)
    G.pool.es.close()
    return nc


from concourse.bass_utils import run_bass_kernel_spmd

_W_NAMES = [n for n, _ in W_SPECS]


def kernel(**inputs):
    x = np.ascontiguousarray(np.asarray(inputs["x"], dtype=np.float32))
    B, S, _ = x.shape
    L = int(np.asarray(inputs["w_in"]).shape[0])
    assert B == 8
    nc = build(S, L)
    w = {n: np.ascontiguousarray(np.asarray(inputs[n], dtype=np.float32)) for n in _W_NAMES}
    in_maps = []
    for b in range(B):
        m = {"x": x[b]}
        m.update(w)
        in_maps.append(m)
    res = run_bass_kernel_spmd(nc, in_maps, core_ids=list(range(B)))
    return np.stack([np.asarray(res.results[b]["out"], dtype=np.float32) for b in range(B)], axis=0)
```

```python
import numpy as np
import concourse.bass as bass
import concourse.mybir as mybir
from contextlib import ExitStack

F32 = mybir.dt.float32
BF16 = mybir.dt.bfloat16
AF = mybir.ActivationFunctionType
ALU = mybir.AluOpType
AX = mybir.AxisListType

ENGS = ("pe", "act", "dve", "pool", "sp")
NDSEM = 8


class Res:
    __slots__ = ("name", "w", "r")

    def __init__(self, name=""):
        self.name = name
        self.w = None
        self.r = {}


class T:
    def __init__(self, h, name=""):
        self.h = h
        self.res = Res(name)

    def __getitem__(self, idx):
        return self.h[idx]


class Op:
    __slots__ = ("eng", "fn", "deps", "needs_inc", "tok", "is_dma", "gi")


class SemPool:
    def __init__(self, nc):
        self.nc = nc
        self.es = ExitStack()
        self.sems = {}
        self.counts = {}
        self.prev = []

    def get(self, key):
        if key not in self.sems:
            self.sems[key] = self.es.enter_context(self.nc.semaphore("sem_" + key))
            self.counts[key] = 0
        return self.sems[key]


class Prog:
    _uid = [0]

    def __init__(self, nc, pool=None):
        self.nc = nc
        self.pool = pool if pool is not None else SemPool(nc)
        Prog._uid[0] += 1
        self.pfx = "p%d_" % Prog._uid[0]
        self.ops = []
        self.last = {e: None for e in ENGS}
        self.pending = {e: [] for e in ENGS}
        self.dma_ops = []
        self.es = ExitStack()
        self.sems = {}
        self.dsems = {}
        self.ndma = {e: 0 for e in ENGS}
        self.last_on_dsem = {}

    def sb(self, name, shape, dt, stack=None):
        h = (stack or self.es).enter_context(self.nc.sbuf_tensor(self.pfx + name, list(shape), dt))
        return T(h, name)

    def ps(self, name, shape, dt=F32, stack=None):
        h = (stack or self.es).enter_context(self.nc.psum_tensor(self.pfx + name, list(shape), dt))
        return T(h, name)

    def op(self, eng, fn, r=(), w=(), dma=False):
        o = Op()
        o.eng = eng
        o.fn = fn
        o.is_dma = dma
        o.needs_inc = dma
        o.tok = None
        o.gi = len(self.ops)
        deps = {}
        for t in r:
            res = t.res if isinstance(t, T) else t
            if res.w is not None:
                deps[id(res.w)] = res.w
        for t in w:
            res = t.res if isinstance(t, T) else t
            if res.w is not None:
                deps[id(res.w)] = res.w
            for d in res.r.values():
                deps[id(d)] = d
        for d in self.pending[eng]:
            deps[id(d)] = d
        self.pending[eng] = []
        if dma:
            k = (eng, self.ndma[eng] % NDSEM)
            self.ndma[eng] += 1
            prev = self.last_on_dsem.get(k)
            if prev is not None:
                deps[id(prev)] = prev
            self.last_on_dsem[k] = o
            o.tok = k
        dl = []
        for d in deps.values():
            if d is o:
                continue
            if eng == "pe" and d.eng == "pe" and not d.is_dma:
                continue
            d.needs_inc = True
            dl.append(d)
        o.deps = dl
        for t in r:
            res = t.res if isinstance(t, T) else t
            key = ("dma", o.gi) if dma else eng
            res.r[key] = o
        for t in w:
            res = t.res if isinstance(t, T) else t
            res.w = o
            res.r = {}
        self.ops.append(o)
        self.last[eng] = o
        if dma:
            self.dma_ops.append(o)
        return o

    def barrier(self):
        outs = [o for o in self.last.values() if o is not None]
        outs += list(self.last_on_dsem.values())
        for e in ENGS:
            self.pending[e] = list(outs)
        for o in outs:
            o.needs_inc = True

    def dma(self, out_ap, in_ap, r=(), w=(), eng="sp", **kw):
        return self.op(eng, lambda e: e.dma_start(out=out_ap, in_=in_ap, **kw), r=r, w=w, dma=True)

    def mm(self, out_ap, lhsT, rhs, start=True, stop=True, r=(), w=(), sgc=False):
        if sgc:
            return self.op("pe", lambda e: e.matmul(out_ap, lhsT, rhs, start=start, stop=stop, skip_group_check=True), r=r, w=w)
        return self.op("pe", lambda e: e.matmul(out_ap, lhsT, rhs, start=start, stop=stop), r=r, w=w)

    def tr(self, out_ap, in_ap, ident, r=(), w=()):
        return self.op("pe", lambda e: e.transpose(out_ap, in_ap, ident), r=r, w=w)

    def act(self, out_ap, in_ap, func, r=(), w=(), **kw):
        return self.op("act", lambda e: e.activation(out_ap, in_ap, func, **kw), r=r, w=w)


    def stt(self, out, in0, scalar, in1, op0, op1, r=(), w=(), accum_out=None):
        if accum_out is None:
            return self.op("dve", lambda e: e.scalar_tensor_tensor(out=out, in0=in0, scalar=scalar, in1=in1, op0=op0, op1=op1), r=r, w=w)
        return self.op("dve", lambda e: e.scalar_tensor_tensor(out=out, in0=in0, scalar=scalar, in1=in1, op0=op0, op1=op1, accum_out=accum_out), r=r, w=w)

    def ts(self, out, in0, s1, s2, op0, op1=None, r=(), w=(), eng="dve"):
        if op1 is None:
            return self.op(eng, lambda e: e.tensor_scalar(out=out, in0=in0, scalar1=s1, scalar2=None, op0=op0), r=r, w=w)
        return self.op(eng, lambda e: e.tensor_scalar(out=out, in0=in0, scalar1=s1, scalar2=s2, op0=op0, op1=op1), r=r, w=w)

    def tt(self, out, in0, in1, op, r=(), w=(), eng="dve"):
        return self.op(eng, lambda e: e.tensor_tensor(out=out, in0=in0, in1=in1, op=op), r=r, w=w)

    def copy(self, out, in_, r=(), w=(), eng="dve"):
        if eng == "act":
            return self.op("act", lambda e: e.copy(out=out, in_=in_), r=r, w=w)
        return self.op(eng, lambda e: e.tensor_copy(out=out, in_=in_), r=r, w=w)

    def memset(self, ap, val, w=(), eng="pool"):
        return self.op(eng, lambda e: e.memset(ap, val), w=w)

    def affine(self, t, pattern, base, cm, cmp, fill, val, eng="pool"):
        self.memset(t[:], val, w=[t])
        return self.op("pool", lambda e: e.affine_select(out=t[:], in_=t[:], pattern=pattern, compare_op=cmp, fill=fill, base=base, channel_multiplier=cm), r=[t], w=[t])

    def finalize(self):
        nc = self.nc
        es = self.es
        self.barrier()
        self.op("sp", None, r=(), w=())
        pool = self.pool
        for o in self.ops:
            if o.is_dma:
                key = "d_%s%d" % o.tok
                sem = pool.get(key)
                pool.counts[key] += 16
                o.tok = (sem, pool.counts[key], 16)
            elif o.needs_inc:
                key = "e_" + o.eng
                sem = pool.get(key)
                pool.counts[key] += 1
                o.tok = (sem, pool.counts[key], 1)
        streams = {e: [] for e in ENGS}
        seen = {e: {} for e in ENGS}
        first = {e: True for e in ENGS}
        for o in self.ops:
            waits = {}
            deptoks = [d.tok for d in o.deps]
            if first[o.eng]:
                first[o.eng] = False
                deptoks += [(sem, val, 0) for sem, val in pool.prev]
            for sem, val, _ in deptoks:
                sid = id(sem)
                if seen[o.eng].get(sid, 0) >= val:
                    continue
                if sid not in waits or waits[sid][1] < val:
                    waits[sid] = (sem, val)
            for sid, (sem, val) in waits.items():
                seen[o.eng][sid] = val
            streams[o.eng].append((list(waits.values()), o.fn, o.tok if (o.is_dma or o.needs_inc) else None))
        pool.prev = [(pool.sems[k], pool.counts[k]) for k in pool.sems if pool.counts[k] > 0]
        self.stats = {e: len(streams[e]) for e in ENGS}

        def run(stream):
            def f(e):
                for waits, fn, tok in stream:
                    for sem, val in waits:
                        e.wait_ge(sem, val)
                    if fn is None:
                        continue
                    ins = fn(e)
                    if tok is not None:
                        ins.then_inc(tok[0], tok[2])
            return f

        with nc.Block() as block:
            block.tensor(run(streams["pe"]))
            block.scalar(run(streams["act"]))
            block.vector(run(streams["dve"]))
            block.gpsimd(run(streams["pool"]))
            block.sync(run(streams["sp"]))
        es.close()


D = 1024
NIN = 7172
FF = 2816
BIG = 30000.0
EPS = 1e-6


class NS:
    pass


def rep(ap, pattern, **kw):
    return ap.rearrange(pattern, **kw)


def mk_ident(P, name="ident"):
    f = P.sb(name + "_f", [128, 128], F32)
    P.affine(f, [[-1, 128]], 0, 1, ALU.is_equal, 0.0, 1.0)
    b = P.sb(name, [128, 128], BF16)
    P.copy(b[:], f[:], r=[f], w=[b])
    return b, f


def norm_alloc(P, J):
    K = NS()
    K.J = J
    K.junk = P.sb("n_junk", [128, D], F32)
    K.ss = P.sb("n_ss", [128, 4], F32)
    K.ms = P.sb("n_ms", [128, 4], F32)
    K.rstd = P.sb("n_rstd", [128, 4], F32)
    K.mh = P.sb("n_mh", [128, 4], F32)
    P.memset(K.mh[:], -0.5, w=[K.mh])
    K.xn = P.sb("n_xn", [128, J, D], BF16)
    K.pT = [P.ps("n_pT%d" % i, [128, D], BF16) for i in range(2)]
    K.cnt = 0
    return K


def norm_stats(P, K, xt, J):
    for j in range(J):
        P.stt(K.junk[:], xt[:, j, :], 1.0, xt[:, j, :], ALU.mult, ALU.mult, r=[xt], w=[K.junk, K.ss],
              accum_out=K.ss[:, j:j + 1])
    P.ts(K.ms[:, 0:J], K.ss[:, 0:J], 1.0 / D, EPS, ALU.mult, ALU.add, r=[K.ss], w=[K.ms])
    P.tt(K.rstd[:, 0:J], K.ms[:, 0:J], K.mh[:, 0:J], ALU.pow, r=[K.ms, K.mh], w=[K.rstd], eng="pool")


def norm_p1(P, K, xt, gb=None):
    J = K.J
    norm_stats(P, K, xt, J)
    for j in range(J):
        if gb is None:
            P.act(K.xn[:, j, :], xt[:, j, :], AF.Copy, r=[xt, K.rstd], w=[K.xn], scale=K.rstd[:, j:j + 1])
        else:
            P.stt(K.xn[:, j, :], xt[:, j, :], K.rstd[:, j:j + 1], gb[:], ALU.mult, ALU.mult, r=[xt, K.rstd, gb], w=[K.xn])


def norm_p2(P, K, ident, hTt):
    J = K.J
    for j in range(J):
        pT = K.pT[K.cnt % 2]
        K.cnt += 1
        for kc in range(8):
            P.tr(pT[:, kc * 128:(kc + 1) * 128], K.xn[:, j, kc * 128:(kc + 1) * 128], ident[:], r=[K.xn, ident], w=[pT])
        P.copy(hTt[:, :, j * 128:(j + 1) * 128], rep(pT[:, :], "p (k t) -> p k t", k=8), r=[pT], w=[hTt])


def norm_tt(P, K, ident, xt, hTt, gb=None):
    norm_p1(P, K, xt, gb)
    norm_p2(P, K, ident, hTt)


def load_gain(P, name, g_ap):
    t = P.sb(name, [128, D], F32)
    P.dma(t[:], g_ap.partition_broadcast(128), w=[t])
    return t


def load_w(P, dst, dst_sl, src_ap):
    P.dma(dst_sl, src_ap, w=[dst], eng="pool")


def load_w_cast(P, dst, dst_sl, src_ap, ncols, stage, si, gcol=None, eng="dve", r_extra=()):
    st = stage[si % len(stage)]
    P.dma(st[:, 0:ncols], src_ap, w=[st])
    if gcol is None:
        P.copy(dst_sl, st[:, 0:ncols], r=[st], w=[dst], eng=eng)
    elif eng == "act":
        P.act(dst_sl, st[:, 0:ncols], AF.Copy, r=[st] + list(r_extra), w=[dst], scale=gcol)
    else:
        P.ts(dst_sl, st[:, 0:ncols], gcol, None, ALU.mult, r=[st] + list(r_extra), w=[dst], eng=eng)


def ph_norm0(nc, G):
    S = G.S
    P = Prog(nc, G.pool)
    ident, _ = mk_ident(P)
    J = 4
    K = norm_alloc(P, J)
    xts = [P.sb("xt%d" % i, [128, J, D], F32) for i in range(2)]
    hts = [P.sb("ht%d" % i, [128, 8, J * 128], BF16) for i in range(2)]
    hTv = rep(G.hT, "(k p) s -> p k s", p=128)
    gb = load_gain(P, "gb", G.norm1_g[0])
    for t in range(S // (128 * J)):
        xt = xts[t % 2]
        ht = hts[t % 2]
        t0 = t * 128 * J
        P.dma(xt[:], rep(G.x_in[t0:t0 + 128 * J, :], "(j p) d -> p j d", p=128), w=[xt])
        norm_tt(P, K, ident, xt, ht, gb)
        P.dma(hTv[:, :, t0:t0 + 128 * J], ht[:], r=[ht], eng="pool")
    P.finalize()


def ph_proj(nc, G, l):
    S = G.S
    P = Prog(nc, G.pool)
    NC_A = 3076
    WA = P.sb("WA", [128, 8, NC_A], BF16)
    for kc in range(8):
        load_w(P, WA, WA[:, kc, :], G.w_in[l, kc * 128:(kc + 1) * 128, 0:NC_A])
    hts = [P.sb("ht%d" % i, [128, 8, 512], BF16) for i in range(2)]
    groups = [
        (G.uaT, 0, 6, 1.0),
        (G.qbT, 768, 2, 0.125),
        (G.kbT, 1024, 2, 1.0),
        (G.zT, 1536, 2, 1.0),
        (G.xbcT, 1792, 4, 1.0),
        (G.qdT, 2308, 2, 0.125),
        (G.kdT, 2564, 2, 1.0),
    ]
    stg = {}
    for gi, (dst, c0, nch, sc) in enumerate(groups):
        stg[gi] = [P.sb("stg%d_%d" % (gi, i), [128, nch, 512], BF16) for i in range(2)]
    vst = [P.sb("vst%d" % i, [128, 4, 512], BF16) for i in range(2)]
    dst_ = [P.sb("dtst%d" % i, [128, 4, 4], F32) for i in range(2)]
    pf = [P.ps("pf%d" % i, [128, 512]) for i in range(6)]
    pv = P.ps("pv", [128, 512])
    pd = P.ps("pd", [128, 4, 4])
    hTv = rep(G.hT, "(k p) s -> p k s", p=128)
    cnt = 0
    for t in range(S // 512):
        t0 = t * 512
        ht = hts[t % 2]
        P.dma(ht[:], hTv[:, :, t0:t0 + 512], w=[ht])
        for gi, (dst, c0, nch, sc) in enumerate(groups):
            sg = stg[gi][t % 2]
            for ch in range(nch):
                ps = pf[cnt % 6]
                for kc in range(8):
                    P.mm(ps[:], WA[:, kc, c0 + ch * 128:c0 + (ch + 1) * 128], ht[:, kc, :], start=(kc == 0), stop=(kc == 7),
                         r=[WA, ht], w=[ps])
                if cnt % 2 == 0:
                    P.act(sg[:, ch, :], ps[:], AF.Copy, r=[ps], w=[sg], scale=sc)
                else:
                    P.ts(sg[:, ch, :], ps[:], sc, None, ALU.mult, r=[ps], w=[sg])
                cnt += 1
            P.dma(rep(dst, "(c p) s -> p c s", p=128)[:, :, t0:t0 + 512], sg[:], r=[sg], eng="pool")
        vs = vst[t % 2]
        ds = dst_[t % 2]
        for j in range(4):
            for half, c0 in enumerate((1280, 2820)):
                for kc in range(8):
                    P.mm(pv[:, half * 256:(half + 1) * 256], ht[:, kc, j * 128:(j + 1) * 128], WA[:, kc, c0:c0 + 256],
                         start=(kc == 0), stop=(kc == 7), r=[WA, ht], w=[pv])
            P.copy(vs[:, j, :], pv[:], r=[pv], w=[vs], eng=("dve" if j % 2 == 0 else "act"))
            for kc in range(8):
                P.mm(pd[:, j, :], ht[:, kc, j * 128:(j + 1) * 128], WA[:, kc, 2304:2308], start=(kc == 0), stop=(kc == 7),
                     r=[WA, ht], w=[pd])
        P.copy(ds[:], pd[:], r=[pd], w=[ds])
        P.dma(rep(G.vb[t0:t0 + 512, :], "(j p) c -> p j c", p=128), vs[:, :, 0:256], r=[vs], eng="pool")
        P.dma(rep(G.vd[t0:t0 + 512, :], "(j p) c -> p j c", p=128), vs[:, :, 256:512], r=[vs], eng="pool")
        P.dma(rep(G.dt[t0:t0 + 512, :], "(j p) h -> p j h", p=128), ds[:], r=[ds], eng="pool")
    P.finalize()


def ph_conva(nc, G, l):
    S = G.S
    P = Prog(nc, G.pool)
    cw = P.sb("cw", [128, 3, 2], F32)
    for k in range(3):
        P.dma(cw[:, k, :], rep(G.conv_a_w[l, k], "(c p) -> p c", p=128), w=[cw], allow_slow_non_contiguous=True)
    uv = rep(G.uaT, "(c p) s -> p c s", p=128)
    TT = 512
    ins = [P.sb("cin%d" % i, [128, 6, TT + 2], BF16) for i in range(2)]
    for i in range(2):
        P.memset(ins[i][:, :, 0:2], 0.0, w=[ins[i]])
    pt = P.sb("cp", [128, TT + 2], F32)
    acc = P.sb("cacc", [128, TT], F32)
    ys = [P.sb("cy%d" % i, [128, 2, TT], BF16) for i in range(2)]
    for t in range(S // TT):
        t0 = t * TT
        it = ins[t % 2]
        if t == 0:
            P.dma(it[:, :, 2:TT + 2], uv[:, :, 0:TT], w=[it])
        else:
            P.dma(it[:, :, :], uv[:, :, t0 - 2:t0 + TT], w=[it])
        y = ys[t % 2]
        for fc in range(2):
            P.tt(pt[:], it[:, fc, :], it[:, 4 + fc, :], ALU.mult, r=[it], w=[pt])
            P.ts(acc[:], pt[:, 2:TT + 2], cw[:, 2, fc:fc + 1], None, ALU.mult, r=[pt, cw], w=[acc])
            P.stt(acc[:], pt[:, 1:TT + 1], cw[:, 1, fc:fc + 1], acc[:], ALU.mult, ALU.add, r=[pt, cw, acc], w=[acc])
            P.stt(acc[:], pt[:, 0:TT], cw[:, 0, fc:fc + 1], acc[:], ALU.mult, ALU.add, r=[pt, cw, acc], w=[acc])
            P.tt(y[:, fc, :], acc[:], it[:, 2 + fc, 2:TT + 2], ALU.mult, r=[acc, it], w=[y])
        P.dma(rep(G.yT[0], "(c p) s -> p c s", p=128)[:, :, t0:t0 + TT], y[:], r=[y], eng="pool")
    P.finalize()


def ph_ssd(nc, G, l, hook=None):
    S = G.S
    P = Prog(nc, G.pool)
    T_ = 256
    ident, identf = mk_ident(P)
    U32 = P.sb("U32", [128, 128], F32)
    P.affine(U32, [[-1, 128]], 0, 1, ALU.is_gt, 0.0, 1.0)
    ones32 = P.sb("ones32", [128, 128], F32)
    P.memset(ones32[:], 1.0, w=[ones32])
    onesb = P.sb("onesb", [128, 128], BF16)
    P.memset(onesb[:], 1.0, w=[onesb])
    L0 = P.sb("L0", [128, 256], F32)
    P.affine(L0, [[1, 256]], 0, -1, ALU.is_ge, 0.0, 1.0)
    NEG0 = P.sb("NEG0", [128, 256], F32)
    P.affine(NEG0, [[1, 256]], 0, -1, ALU.is_ge, -BIG, 0.0)
    mh = P.sb("mh", [128, 256], F32)
    P.memset(mh[:], -0.5, w=[mh])
    cw = P.sb("cw", [128, 4, 4], F32)
    for k in range(4):
        P.dma(cw[:, k, :], rep(G.ssm_conv_w[l, k], "(c p) -> p c", p=128), w=[cw], allow_slow_non_contiguous=True)
    cb = P.sb("cb", [128, 4], F32)
    P.dma(cb[:], rep(G.ssm_conv_b[l], "(c p) -> p c", p=128), w=[cb], allow_slow_non_contiguous=True)
    dtb = P.sb("dtb", [128, 4], F32)
    P.dma(dtb[:], G.ssm_dt_bias[l].partition_broadcast(128), w=[dtb])
    Arow = P.sb("Arow", [128, 4], F32)
    P.dma(Arow[:], G.ssm_a_log[l].partition_broadcast(128), w=[Arow])
    P.act(Arow[:], Arow[:], AF.Exp, r=[Arow], w=[Arow])
    P.ts(Arow[:], Arow[:], -1.0, None, ALU.mult, r=[Arow], w=[Arow])
    Dcol = P.sb("Dcol", [128, 2], F32)
    for g in range(2):
        for e_ in range(2):
            P.dma(Dcol[e_ * 64:(e_ + 1) * 64, g:g + 1], G.ssm_d[l, 2 * g + e_:2 * g + e_ + 1].partition_broadcast(64), w=[Dcol])
    ng = P.sb("ng", [128, 2], F32)
    P.dma(ng[:], rep(G.ssm_norm_g[l], "(g p) -> p g", p=128), w=[ng], allow_slow_non_contiguous=True)

    XIN = [P.sb("XIN%d" % i, [128, 4, T_ + 3], BF16) for i in range(2)]
    ZIN = [P.sb("ZIN%d" % i, [128, 2, T_], BF16) for i in range(2)]
    DTIN = [P.sb("DTIN%d" % i, [128, 2, 4], F32) for i in range(2)]
    P.memset(XIN[0][:, :, 0:3], 0.0, w=[XIN[0]])
    acc = [P.sb("acc%d" % i, [128, T_], F32) for i in range(4)]
    xcs = [[P.sb("xc%d_%d" % (j, i), [128, T_], BF16) for i in range(4)] for j in range(2)]
    dts = P.sb("dts", [128, 2, 4], F32)
    dte_ = P.sb("dte_", [128, 2, 4], F32)
    dsps = [P.sb("dsp%d" % j, [128, 2, 4], F32) for j in range(2)]
    a_ts = [P.sb("a_t%d" % j, [128, 2, 4], F32) for j in range(2)]
    Xpads = [[P.sb("Xpad%d_%d" % (j, i), [128, 2, 2, 128], BF16) for i in range(2)] for j in range(2)]
    for j in range(2):
        for i in range(2):
            P.memset(Xpads[j][i][:], 0.0, w=[Xpads[j][i]], eng=("dve" if i == 0 else "pool"))
    Btoks = [[P.sb("Btok%d_%d" % (j, i), [128, 128], BF16) for i in range(2)] for j in range(2)]
    Xdte = [P.sb("Xdte%d" % i, [128, 256], BF16) for i in range(2)]
    aL0 = [P.sb("aL0_%d" % i, [128, 256], F32) for i in range(2)]
    aL1 = [P.sb("aL1_%d" % i, [128, 128], F32) for i in range(2)]
    Dt = [P.sb("Dt%d" % i, [128, 384], F32) for i in range(2)]
    Et = [P.sb("Et%d" % i, [128, 256], F32) for i in range(4)]
    Mt = [P.sb("Mt%d" % i, [128, 384], BF16) for i in range(2)]
    Cpt = [P.sb("Cpt%d" % i, [128, 256], BF16) for i in range(2)]
    H32 = P.sb("H32", [128, 2, 64], F32)
    Hpad = P.sb("Hpad", [128, 2, 128], BF16)
    P.memset(H32[:], 0.0, w=[H32])
    P.memset(Hpad[:], 0.0, w=[Hpad])
    yf = P.sb("yf", [128, 256], F32)
    szs = [[P.sb("sz%d_%d" % (j, g), [128, 256], F32) for g in range(2)] for j in range(2)]
    gt = P.sb("gt", [128, 256], F32)
    sq = P.sb("sq", [128, 256], BF16)
    rs = P.sb("rs", [128, 256], F32)
    yst = [P.sb("yst%d" % i, [128, 2, 256], BF16) for i in range(2)]
    segp = [P.ps("segp%d" % i, [128, 384]) for i in range(2)]
    csbp = P.ps("csbp", [128, 256])
    Gp = [P.ps("Gp%d" % i, [128, 384]) for i in range(2)]
    Yg = [P.ps("Yg%d" % i, [128, 256]) for i in range(2)]
    misc = P.ps("misc", [128, 512])
    ptb = T(misc.h[:, 0:256].bitcast(BF16), "ptb")
    HSb = T(misc.h[:, 256:512], "HSb")
    ptb.res = misc.res
    HSb.res = misc.res

    xv = rep(G.xbcT, "(c p) s -> p c s", p=128)
    zv = rep(G.zT, "(g p) s -> p g s", p=128)
    yv = rep(G.yT[2], "(g p) s -> p g s", p=128)
    nchunks = S // T_
    hc = 0

    def front_parts(c):
        t0 = c * T_
        xin = XIN[c % 2]
        zin = ZIN[c % 2]
        dtin = DTIN[c % 2]
        xc = xcs[c % 2]
        dsp = dsps[c % 2]
        a_t = a_ts[c % 2]
        Xpad = Xpads[c % 2]
        Btok = Btoks[c % 2]

        def conv(ct):
            P.ts(acc[ct][:], xin[:, ct, 0:T_], cw[:, 0, ct:ct + 1], None, ALU.mult, r=[xin, cw], w=[acc[ct]])
            for k in range(1, 4):
                P.stt(acc[ct][:], xin[:, ct, k:k + T_], cw[:, k, ct:ct + 1], acc[ct][:], ALU.mult, ALU.add,
                      r=[xin, cw, acc[ct]], w=[acc[ct]])
            P.act(xc[ct][:], acc[ct][:], AF.Silu, r=[acc[ct], cb], w=[xc[ct]], bias=cb[:, ct:ct + 1])

        def xtr(g):
            for t in range(2):
                sl = ptb[:, (2 * t + g) * 128:(2 * t + g + 1) * 128]
                P.tr(sl, xc[g][:, t * 128:(t + 1) * 128], ident[:], r=[xc[g], ident], w=[ptb])
                for e_ in range(2):
                    P.ts(Xpad[t][:, g, e_, e_ * 64:(e_ + 1) * 64],
                         ptb[:, (2 * t + g) * 128 + e_ * 64:(2 * t + g) * 128 + (e_ + 1) * 64],
                         dsp[:, t, 2 * g + e_:2 * g + e_ + 1], None, ALU.mult, r=[ptb, dsp], w=[Xpad[t]])

        def p0():
            if c == 0:
                P.dma(xin[:, :, 3:T_ + 3], xv[:, :, 0:T_], w=[xin])
            else:
                P.dma(xin[:, :, :], xv[:, :, t0 - 3:t0 + T_], w=[xin])
            P.dma(zin[:], zv[:, :, t0:t0 + T_], w=[zin])
            P.dma(dtin[:], rep(G.dt[t0:t0 + T_, :], "(t p) h -> p t h", p=128), w=[dtin])
            for t in range(2):
                P.tt(dts[:, t, :], dtin[:, t, :], dtb[:], ALU.add, r=[dtin, dtb], w=[dts])
            P.act(dte_[:], dts[:], AF.Exp, r=[dts], w=[dte_])
            P.act(dsp[:], dte_[:], AF.Ln, r=[dte_], w=[dsp], bias=1.0)
            for t in range(2):
                P.tt(a_t[:, t, :], dsp[:, t, :], Arow[:], ALU.mult, r=[dsp, Arow], w=[a_t])
            conv(0)

        def p1():
            conv(1)
            xtr(0)

        def p2():
            conv(2)
            xtr(1)

        def p3():
            conv(3)
            for g in range(2):
                P.act(szs[c % 2][g][:], zin[:, g, :], AF.Silu, r=[zin], w=[szs[c % 2][g]])
            for t in range(2):
                sl = ptb[:, t * 128:(t + 1) * 128]
                P.tr(sl, xc[2][:, t * 128:(t + 1) * 128], ident[:], r=[xc[2], ident], w=[ptb])
                P.copy(Btok[t][:], sl, r=[ptb], w=[Btok[t]], eng="act")

        return [p0, p1, p2, p3]

    for f_ in front_parts(0):
        f_()
    for c in range(nchunks):
        t0 = c * T_
        zin = ZIN[c % 2]
        xc = xcs[c % 2]
        dsp = dsps[c % 2]
        a_t = a_ts[c % 2]
        Xpad = Xpads[c % 2]
        Btok = Btoks[c % 2]
        if c > 0:
            for f_ in front_parts(c):
                f_()
        nxt = [None] * 4
        for g in range(2):
            gr = slice(g * 64, (g + 1) * 64)
            P.mm(Gp[g][:, 0:256], xc[2][gr, 0:128], xc[3][gr, 0:256], r=[xc[2], xc[3]], w=[Gp[g]])
            P.mm(Gp[g][:, 256:384], xc[2][gr, 128:256], xc[3][gr, 128:256], r=[xc[2], xc[3]], w=[Gp[g]])
        def h_pre(h):
            b = (hc + h) % 2
            P.ts(aL0[b][:], L0[:], a_t[:, 0, h:h + 1], None, ALU.mult, r=[L0, a_t], w=[aL0[b]])
            P.ts(aL1[b][:], L0[:, 0:128], a_t[:, 1, h:h + 1], None, ALU.mult, r=[L0, a_t], w=[aL1[b]])

        def h_seg(h):
            b = (hc + h) % 2
            sp_ = segp[b]
            P.mm(sp_[:, 0:256], U32[:], aL0[b][:], start=True, stop=False, r=[U32, aL0[b]], w=[sp_])
            P.mm(sp_[:, 128:256], ones32[:], aL1[b][:], start=False, stop=False, r=[ones32, aL1[b]], w=[sp_])
            P.mm(sp_[:, 0:256], identf[:], NEG0[:], start=False, stop=True, r=[identf, NEG0], w=[sp_])
            P.mm(sp_[:, 256:384], U32[:], aL1[b][:], start=True, stop=False, r=[U32, aL1[b]], w=[sp_])
            P.mm(sp_[:, 256:384], identf[:], NEG0[:, 0:128], start=False, stop=True, r=[identf, NEG0], w=[sp_])
            P.mm(csbp[:, 0:256], ones32[:], aL0[b][:], start=True, stop=False, r=[ones32, aL0[b]], w=[csbp])
            P.mm(csbp[:, 128:256], ones32[:], aL1[b][:], start=False, stop=True, r=[ones32, aL1[b]], w=[csbp])

        def h_act(h):
            b = (hc + h) % 2
            P.act(Dt[b][:], segp[b][:], AF.Exp, r=[segp[b]], w=[Dt[b]])
            P.act(Et[h][:], csbp[:], AF.Exp, r=[csbp], w=[Et[h]])

        def h_post(h):
            b = (hc + h) % 2
            g, e_ = h // 2, h % 2
            gr = slice(g * 64, (g + 1) * 64)
            P.tt(Mt[b][:], Dt[b][:], Gp[g][:], ALU.mult, r=[Dt[b], Gp[g]], w=[Mt[b]])
            P.tt(Cpt[b][gr, :], xc[3][gr, :], Et[h][gr, :], ALU.mult, r=[xc[3], Et[h]], w=[Cpt[b]])
            P.mm(Yg[g][:, 0:256], Xpad[0][:, g, e_, :], Mt[b][:, 0:256], start=(e_ == 0), stop=False,
                 r=[Xpad[0], Mt[b]], w=[Yg[g]])
            P.mm(Yg[g][:, 128:256], Xpad[1][:, g, e_, :], Mt[b][:, 256:384], start=False, stop=False,
                 r=[Xpad[1], Mt[b]], w=[Yg[g]])
            P.mm(Yg[g][:, 0:256], Hpad[gr, e_, :], Cpt[b][gr, :], start=False, stop=(e_ == 1),
                 r=[Hpad, Cpt[b]], w=[Yg[g]])
            P.ts(Xdte[0][:, h * 64:(h + 1) * 64], Xpad[0][:, g, e_, e_ * 64:(e_ + 1) * 64], Dt[b][:, 255:256], None,
                 ALU.mult, r=[Xpad[0], Dt[b]], w=[Xdte[0]])
            P.ts(Xdte[1][:, h * 64:(h + 1) * 64], Xpad[1][:, g, e_, e_ * 64:(e_ + 1) * 64], Dt[b][:, 383:384], None,
                 ALU.mult, r=[Xpad[1], Dt[b]], w=[Xdte[1]])

        h_pre(0)
        h_seg(0)
        for h in range(4):
            if h + 1 < 4:
                h_pre(h + 1)
            h_act(h)
            if h + 1 < 4:
                h_seg(h + 1)
            h_post(h)
            if nxt[h] is not None:
                nxt[h]()
        P.mm(HSb[:], Btok[0][:], Xdte[0][:], start=True, stop=False, r=[Btok[0], Xdte[0]], w=[HSb])
        P.mm(HSb[:], Btok[1][:], Xdte[1][:], start=False, stop=True, r=[Btok[1], Xdte[1]], w=[HSb])
        for g in range(2):
            gr = slice(g * 64, (g + 1) * 64)
            for e_ in range(2):
                h = 2 * g + e_
                P.stt(H32[gr, e_, :], H32[gr, e_, :], Et[h][gr, 255:256], HSb[gr, h * 64:(h + 1) * 64], ALU.mult, ALU.add,
                      r=[H32, Et[h], HSb], w=[H32])
                P.copy(Hpad[gr, e_, e_ * 64:(e_ + 1) * 64], H32[gr, e_, :], r=[H32], w=[Hpad])
        ys = yst[c % 2]
        for g in range(2):
            P.stt(yf[:], xc[g][:], Dcol[:, g:g + 1], Yg[g][:], ALU.mult, ALU.add, r=[xc[g], Dcol, Yg[g]], w=[yf])
            sz = szs[c % 2][g]
            P.tt(gt[:], yf[:], sz[:], ALU.mult, r=[yf, sz], w=[gt])
            P.tt(sq[:], gt[:], gt[:], ALU.mult, r=[gt], w=[sq])
            P.mm(csbp[:], onesb[:], sq[:], r=[onesb, sq], w=[csbp])
            P.ts(rs[:], csbp[:], 1.0 / 128, EPS, ALU.mult, ALU.add, r=[csbp], w=[rs])
            P.act(rs[:], rs[:], AF.Ln, r=[rs], w=[rs])
            P.act(rs[:], rs[:], AF.Exp, r=[rs], w=[rs], scale=-0.5)
            P.stt(ys[:, g, :], gt[:], ng[:, g:g + 1], rs[:], ALU.mult, ALU.mult, r=[gt, ng, rs], w=[ys])
        P.dma(yv[:, :, t0:t0 + T_], ys[:], r=[ys], eng="pool")
        if hook is not None:
            hook(P, 2)
    if hook is not None:
        hook(P, 10 ** 6)
    P.finalize()


def ph_sb(nc, G, l):
    S = G.S
    P = Prog(nc, G.pool)
    NT = S // 512
    NK = S // 128
    ident, identf = mk_ident(P)
    IUf = P.sb("IUf", [128, 128], F32)
    P.affine(IUf, [[-1, 128]], 0, 1, ALU.is_ge, 0.0, -1.0)
    IUn = P.sb("IUn", [128, 128], BF16)
    P.copy(IUn[:], IUf[:], r=[IUf], w=[IUn])
    onesb = P.sb("onesb", [128, 2], BF16)
    P.memset(onesb[:], 1.0, w=[onesb])
    CM = []
    f = P.sb("CMf", [128, 512], F32)
    for d in range(4):
        P.affine(f, [[1, 512]], -128 * d, -1, ALU.is_gt, -BIG, 0.0)
        b = P.sb("CM%d" % d, [128, 512], BF16)
        P.copy(b[:], f[:], r=[f], w=[b])
        CM.append(b)
    qz = [[P.sb("qz%d_%d" % (j, i), [128, S], BF16) for i in range(2)] for j in range(2)]
    kz = [[P.sb("kz%d_%d" % (j, i), [128, S], BF16) for i in range(2)] for j in range(2)]
    for j in range(2):
        for i in range(2):
            P.memset(qz[j][i][64:128, :], 0.0, w=[qz[j][i]], eng=("dve" if i == 0 else "pool"))
            P.memset(kz[j][i][64:128, :], 0.0, w=[kz[j][i]], eng=("dve" if i == 0 else "pool"))
    v = P.sb("v", [128, NK, 256], BF16)
    vv = rep(G.vb, "(n p) c -> p n c", p=128)
    step = 2048
    for s0 in range(0, S, step):
        s1 = min(S, s0 + step)
        P.dma(v[:, s0 // 128:s1 // 128, :], vv[:, s0 // 128:s1 // 128, :], w=[v])

    def load_qk(hp):
        for i in range(2):
            h = 2 * hp + i
            for s0 in range(0, S, 4096):
                s1 = min(S, s0 + 4096)
                P.dma(qz[hp % 2][i][0:64, s0:s1], G.qbT[h * 64:(h + 1) * 64, s0:s1], w=[qz[hp % 2][i]])
                P.dma(kz[hp % 2][i][0:64, s0:s1], G.kbT[h * 64:(h + 1) * 64, s0:s1], w=[kz[hp % 2][i]])

    load_qk(0)
    load_qk(1)
    Zb = [[P.ps("Zb%d_%d" % (i, j), [128, 512]) for j in range(3)] for i in range(2)]
    OC = [P.ps("OC%d" % i, [128, 512]) for i in range(2)]
    Op = [T(rep(OC[i].h[:, 0:256], "p (q d) -> p q d", d=64), "Op%d" % i) for i in range(2)]
    csp = [T(OC[0].h[:, 256 + 4 * i:260 + 4 * i], "csp%d" % i) for i in range(2)]
    csp2 = T(OC[0].h[:, 256:264], "csp2")
    csp2.res = OC[0].res
    pTv = [T(OC[i].h[:, 384:512].bitcast(BF16), "pTv%d" % i) for i in range(2)]
    for i in range(2):
        Op[i].res = OC[i].res
        csp[i].res = OC[0].res
        pTv[i].res = OC[i].res
    Et = [[P.sb("Et%d_%d" % (i, j), [128, 512], F32) for j in range(2)] for i in range(2)]
    SPt = [[P.sb("SPt%d_%d" % (i, j), [128, 512], BF16) for j in range(2)] for i in range(2)]
    Pt = [[P.sb("Pt%d_%d" % (i, j), [128, 512], BF16) for j in range(2)] for i in range(2)]
    acc = [[P.sb("acc%d_%d" % (i, j), [128, 4, 64], F32) for j in range(2)] for i in range(2)]
    tmpa = [P.sb("tmpa%d" % i, [128, 4, 64], F32) for i in range(2)]
    accb = [P.sb("accb%d" % i, [128, 4, 64], BF16) for i in range(2)]
    dd2 = [P.sb("dd2_%d" % j, [128, 8], F32) for j in range(2)]
    dd = [[T(dd2[j].h[:, 4 * i:4 * i + 4], "dd%d_%d" % (i, j)) for j in range(2)] for i in range(2)]
    for i in range(2):
        for j in range(2):
            dd[i][j].res = dd2[j].res
    yst = [P.sb("yst%d" % i, [64, 512], BF16) for i in range(4)]
    yc = [0]

    for hp in range(2):
        its = [(c, n) for c in range(NT) for n in range(0, 4 * c + 4)]
        N = len(its)

        def dof(k):
            c, n = its[k]
            return max(0, n - 4 * c)

        def stA(k):
            c, n = its[k]
            co = 128 * dof(k)
            qs = slice(c * 512 + co, (c + 1) * 512)
            ks = slice(n * 128, (n + 1) * 128)
            diag = n >= 4 * c
            for i in range(2):
                Z = Zb[i][k % 3]
                P.mm(Z[:, co:512], kz[hp % 2][i][:, ks], qz[hp % 2][i][:, qs], start=True, stop=(not diag),
                     r=[kz[hp % 2][i], qz[hp % 2][i]], w=[Z])
                if diag:
                    P.mm(Z[:, co:512], ident[:], CM[n - 4 * c][:, co:512], start=False, stop=True,
                         r=[ident, CM[n - 4 * c]], w=[Z])

        def stB(k):
            co = 128 * dof(k)
            for i in range(2):
                P.act(Et[i][k % 2][:, co:512], Zb[i][k % 3][:, co:512], AF.Exp, r=[Zb[i][k % 3]], w=[Et[i][k % 2]])
            for i in range(2):
                P.act(SPt[i][k % 2][:, co:512], Et[i][k % 2][:, co:512], AF.Ln, r=[Et[i][k % 2]], w=[SPt[i][k % 2]], bias=1.0)

        def stC(k):
            co = 128 * dof(k)
            for i in range(2):
                Z = Zb[i][k % 3]
                P.mm(Z[:, co:512], IUn[:], SPt[i][k % 2][:, co:512], start=False, stop=True, r=[IUn, SPt[i][k % 2]], w=[Z], sgc=True)

        def stD(k):
            co = 128 * dof(k)
            for i in range(2):
                P.act(Pt[i][k % 2][:, co:512], Zb[i][k % 3][:, co:512], AF.Exp, r=[Zb[i][k % 3]], w=[Pt[i][k % 2]])

        def stE(k):
            c, n = its[k]
            d0 = dof(k)
            for i in range(2):
                h = 2 * hp + i
                for qi in range(d0, 4):
                    P.mm(Op[i][:, qi, :], Pt[i][k % 2][:, qi * 128:(qi + 1) * 128], v[:, n, h * 64:(h + 1) * 64],
                         r=[Pt[i][k % 2], v], w=[Op[i]])
                for qi in range(d0, 4):
                    P.mm(csp[i][:, qi:qi + 1], SPt[i][k % 2][:, qi * 128:(qi + 1) * 128], onesb[:, 0:1],
                         r=[SPt[i][k % 2], onesb], w=[csp[i]])

        def stFd(k):
            c, n = its[k]
            if n > 0:
                P.act(dd2[k % 2][:], csp2[:], AF.Exp, r=[csp2], w=[dd2[k % 2]], scale=-1.0)

        def stF(k):
            c, n = its[k]
            cb = c % 2
            for i in range(2):
                if n == 0:
                    P.copy(acc[i][cb][:], Op[i][:], r=[Op[i]], w=[acc[i][cb]])
                else:
                    d0 = dof(k)
                    P.tt(tmpa[i][:, d0:4, :], acc[i][cb][:, d0:4, :],
                         dd[i][k % 2][:, d0:4].unsqueeze(2).to_broadcast([128, 4 - d0, 64]), ALU.mult,
                         r=[acc[i][cb], dd[i][k % 2]], w=[tmpa[i]])
                    P.tt(acc[i][cb][:, d0:4, :], tmpa[i][:, d0:4, :], Op[i][:, d0:4, :], ALU.add, r=[tmpa[i], Op[i]], w=[acc[i][cb]])
            if n == 4 * c + 3:
                qs = slice(c * 512, (c + 1) * 512)
                for i in range(2):
                    h = 2 * hp + i
                    P.copy(accb[i][:], acc[i][cb][:], r=[acc[i][cb]], w=[accb[i]])
                    ys = yst[yc[0] % 4]
                    yc[0] += 1
                    for half in range(2):
                        for q2 in range(2):
                            qi = 2 * half + q2
                            P.tr(pTv[i][0:64, q2 * 128:(q2 + 1) * 128], accb[i][:, qi, :], ident[:], r=[accb[i], ident], w=[pTv[i]])
                        P.copy(ys[:, half * 256:(half + 1) * 256], pTv[i][0:64, :], r=[pTv[i]], w=[ys], eng="act")
                    P.dma(G.yT[1, h * 64:(h + 1) * 64, qs], ys[:], r=[ys], eng="pool")

        stA(0)
        for k in range(N + 2):
            if k + 1 < N:
                stA(k + 1)
            if k < N:
                stB(k)
            if 0 <= k - 2 < N:
                stFd(k - 2)
            if 0 <= k - 1 < N:
                stD(k - 1)
            if k < N:
                stC(k)
            if 0 <= k - 2 < N:
                stF(k - 2)
            if 0 <= k - 1 < N:
                stE(k - 1)
    P.finalize()


def ph_moba(nc, G, l, stabilize=True):
    S = G.S
    P = Prog(nc, G.pool)
    NT = S // 512
    NK = S // 128
    NB = S // 256
    assert NB <= 32
    ident, identf = mk_ident(P)
    ones32 = P.sb("ones32", [128, 64], F32)
    P.memset(ones32[:], 1.0, w=[ones32])
    onesb = P.sb("onesb", [128, 128], BF16)
    P.memset(onesb[:], 1.0, w=[onesb])
    CM = []
    f = P.sb("CMf", [128, 512], F32)
    for d in range(4):
        P.affine(f, [[1, 512]], -128 * d, -1, ALU.is_ge, -BIG, 0.0)
        b = P.sb("CM%d" % d, [128, 512], BF16)
        P.copy(b[:], f[:], r=[f], w=[b])
        CM.append(b)
    KE = [P.sb("KE%d" % i, [128, S], BF16) for i in range(2)]
    QN = [P.sb("QN%d" % i, [128, S], BF16) for i in range(2)]
    for i in range(2):
        P.memset(KE[i][64:128, :], 0.0, w=[KE[i]], eng=("dve" if i == 0 else "pool"))
        P.memset(QN[i][64:128, :], 0.0, w=[QN[i]], eng=("dve" if i == 0 else "pool"))
    ohf = P.sb("ohf", [128, 2048], F32)
    for s0 in range(0, S, 2048):
        w_ = min(2048, S - s0)
        ohv = T(rep(ohf.h[64:96, 0:w_], "p (b k) -> p b k", k=256), "ohv")
        ohv.res = ohf.res
        P.affine(ohv, [[-1, w_ // 256], [0, 256]], -(s0 // 256), 1, ALU.is_equal, 0.0, 1.0)
        for i in range(2):
            P.copy(KE[i][64:96, s0:s0 + w_], ohf[64:96, 0:w_], r=[ohf], w=[KE[i]], eng=("dve" if i == 0 else "act"))
    Vaug = P.sb("Vaug", [128, NK, 4, 65], BF16)
    vtmp = [P.sb("vtmp%d" % i, [128, 8, 256], BF16) for i in range(2)]
    vv = rep(G.vd, "(n p) c -> p n c", p=128)
    P.memset(Vaug[:, :, :, 64:65], 1.0, w=[Vaug])
    step = 1024
    for si, s0 in enumerate(range(0, S, step)):
        s1 = min(S, s0 + step)
        n0, n1 = s0 // 128, s1 // 128
        vt = vtmp[si % 2]
        P.dma(vt[:, 0:n1 - n0, :], vv[:, n0:n1, :], w=[vt])
        for h in range(4):
            P.copy(Vaug[:, n0:n1, h, 0:64], vt[:, 0:n1 - n0, h * 64:(h + 1) * 64], r=[vt], w=[Vaug],
                   eng=("act" if h % 2 == 0 else "dve"))
    km32 = [P.sb("km32_%d" % i, [64, 32], F32) for i in range(2)]
    kmhi = [P.sb("kmhi%d" % i, [64, 32], BF16) for i in range(2)]
    kmhf = [P.sb("kmhf%d" % i, [64, 32], F32) for i in range(2)]
    kmlo = [P.sb("kmlo%d" % i, [64, 32], BF16) for i in range(2)]
    for i in range(2):
        P.memset(km32[i][:], 0.0, w=[km32[i]])

    Zp = [[P.ps("Zp%d_%d" % (i, j), [128, 512]) for j in range(2)] for i in range(2)]
    OT = [[P.ps("OT%d_%d" % (i, j), [128, 512]) for j in range(2)] for i in range(2)]
    KM = P.sb("KM", [128, 2], F32)
    ksqa = P.sb("ksqa", [64, S], BF16)
    kmx = P.sb("kmx", [128, 16], F32)
    kqs = [OT[1][0], OT[1][1]]
    NEGVB = P.sb("NEGVB", [128, 32, 2, 32], F32)
    P.affine(NEGVB, [[1, 32], [0, 2], [-1, 32]], 0, 0, ALU.is_gt, -BIG, 0.0)
    OWNB = P.sb("OWNB", [128, 32, 2, 32], F32)
    P.affine(OWNB, [[1, 32], [0, 2], [-1, 32]], 0, 0, ALU.is_equal, 0.0, 1.0)
    NEGV3 = rep(NEGVB[:], "p a b n -> p (a b) n")
    OWN3 = rep(OWNB[:], "p a b n -> p (a b) n")
    QB = 16
    qsq = [P.sb("qsq%d" % i, [64, QB * 128], BF16) for i in range(2)]
    gm = [P.sb("gm%d" % i, [128, QB, 32], F32) for i in range(2)]
    g2 = [P.sb("g2_%d" % i, [128, QB, 32], F32) for i in range(2)]
    eq = [P.sb("eq%d" % i, [128, QB, 32], F32) for i in range(2)]
    mx = [P.sb("mx%d" % i, [128, QB], F32) for i in range(2)]
    mq = [P.sb("mq%d" % i, [128, QB], F32) for i in range(2)]
    nb = [P.sb("nb%d" % i, [128, QB, 32], BF16) for i in range(2)]
    Pt = [[P.sb("Pt%d_%d" % (i, j), [128, 512], BF16) for j in range(2)] for i in range(2)]
    RL = [P.sb("RL%d" % i, [128, 512], F32) for i in range(2)]
    bcs = [P.sb("bcs%d" % i, [64, 512], F32) for i in range(2)]
    yo = [P.sb("yo%d" % i, [64, 512], BF16) for i in range(4)]
    gpb = [T(rep(Zp[i][0].h[:, :], "p (q n) -> p q n", n=32), "gpb%d" % i) for i in range(2)]
    pTb = [T(Zp[i][1].h[:, :].bitcast(BF16), "pTb%d" % i) for i in range(2)]
    qnp = [T(OT[i][0].h[:, 0:QB], "qnp%d" % i) for i in range(2)]
    for i in range(2):
        gpb[i].res = Zp[i][0].res
        pTb[i].res = Zp[i][1].res
        qnp[i].res = OT[i][0].res
    yc = [0]
    zc = [0]
    kc_ = 0
    for hp in range(2):
        for i in range(2):
            h = 2 * hp + i
            for s0 in range(0, S, 4096):
                s1 = min(S, s0 + 4096)
                P.dma(QN[i][0:64, s0:s1], G.qdT[h * 64:(h + 1) * 64, s0:s1], w=[QN[i]])
                P.dma(KE[i][0:64, s0:s1], G.kdT[h * 64:(h + 1) * 64, s0:s1], w=[KE[i]])
        for i in range(2):
            P.op("dve", (lambda i_: (lambda e: e.tensor_reduce(out=km32[i_][:, 0:NB], in_=rep(KE[i_][0:64, :], "p (b k) -> p b k", k=256),
                                                               axis=AX.X, op=ALU.add)))(i), r=[KE[i]], w=[km32[i]])
            P.copy(kmhi[i][:], km32[i][:], r=[km32[i]], w=[kmhi[i]])
            P.copy(kmhf[i][:], kmhi[i][:], r=[kmhi[i]], w=[kmhf[i]])
            P.tt(kmlo[i][:], km32[i][:], kmhf[i][:], ALU.subtract, r=[km32[i], kmhf[i]], w=[kmlo[i]])
            if stabilize:
                nch = S // 512
                for ci in range(nch):
                    s0 = ci * 512
                    P.tt(ksqa[:, s0:s0 + 512], KE[i][0:64, s0:s0 + 512], KE[i][0:64, s0:s0 + 512], ALU.mult, r=[KE[i]], w=[ksqa])
                for ci in range(nch):
                    s0 = ci * 512
                    kqb = kqs[ci % 2]
                    P.mm(kqb[:], onesb[0:64, :], ksqa[:, s0:s0 + 512], r=[onesb, ksqa], w=[kqb])
                    P.op("dve", (lambda o_, i_: (lambda e: e.tensor_reduce(out=o_, in_=i_, axis=AX.X, op=ALU.max)))(
                        kmx[:, ci:ci + 1], kqb[:]), r=[kqb], w=[kmx])
                P.op("dve", (lambda o_, i_: (lambda e: e.tensor_reduce(out=o_, in_=i_, axis=AX.X, op=ALU.max)))(
                    KM[:, i:i + 1], kmx[:, 0:nch]), r=[kmx], w=[KM])
        for q0 in range(0, NK, QB):
            nq = min(QB, NK - q0)
            cs_ = slice(q0 * 128, (q0 + nq) * 128)
            for i in range(2):
                if stabilize:
                    P.tt(qsq[i][:, 0:nq * 128], QN[i][0:64, cs_], QN[i][0:64, cs_], ALU.mult, r=[QN[i]], w=[qsq[i]])
                for j in range(nq):
                    cj = slice((q0 + j) * 128, (q0 + j + 1) * 128)
                    P.mm(gpb[i][:, j, :], QN[i][0:64, cj], kmhi[i][:], start=True, stop=False, r=[QN[i], kmhi[i]], w=[gpb[i]])
                    P.mm(gpb[i][:, j, :], QN[i][0:64, cj], kmlo[i][:], start=False, stop=True, r=[QN[i], kmlo[i]], w=[gpb[i]])
                    if stabilize:
                        P.mm(qnp[i][:, j:j + 1], qsq[i][:, j * 128:(j + 1) * 128], onesb[0:64, 0:1],
                             r=[qsq[i], onesb], w=[qnp[i]])
            for i in range(2):
                G_ = gm[i]
                P.tt(G_[:, 0:nq, :], gpb[i][:, 0:nq, :], NEGV3[:, q0:q0 + nq, :], ALU.add, r=[gpb[i], NEGVB], w=[G_])
                src = G_
                for it in range(3):
                    P.op("dve", (lambda o_, s_: (lambda e: e.tensor_reduce(out=o_, in_=s_, axis=AX.X, op=ALU.max)))(
                        mx[i][:, 0:nq], src[:, 0:nq, :]), r=[src], w=[mx[i]])
                    if it < 2:
                        P.tt(eq[i][:, 0:nq, :], src[:, 0:nq, :], mx[i][:, 0:nq].unsqueeze(2).to_broadcast([128, nq, 32]),
                             ALU.is_equal, r=[src, mx[i]], w=[eq[i]])
                        P.stt(g2[i][:, 0:nq, :], eq[i][:, 0:nq, :], -1e6, src[:, 0:nq, :], ALU.mult, ALU.add,
                              r=[eq[i], src], w=[g2[i]])
                        src = g2[i]
                P.ts(mx[i][:, 0:nq], mx[i][:, 0:nq], -BIG / 2, None, ALU.max, r=[mx[i]], w=[mx[i]])
                P.tt(eq[i][:, 0:nq, :], G_[:, 0:nq, :], mx[i][:, 0:nq].unsqueeze(2).to_broadcast([128, nq, 32]),
                     ALU.is_ge, r=[G_, mx[i]], w=[eq[i]])
                P.tt(eq[i][:, 0:nq, :], eq[i][:, 0:nq, :], OWN3[:, q0:q0 + nq, :], ALU.max, r=[eq[i], OWNB], w=[eq[i]])
                if stabilize:
                    P.act(mq[i][:, 0:nq], qnp[i][:, 0:nq], AF.Sqrt, r=[qnp[i], KM], w=[mq[i]], scale=KM[:, i:i + 1])
                    P.ts(mq[i][:, 0:nq], mq[i][:, 0:nq], -1.0, -BIG, ALU.mult, ALU.add, r=[mq[i]], w=[mq[i]])
                else:
                    P.memset(mq[i][:], -BIG, w=[mq[i]], eng="dve")
                P.stt(nb[i][:, 0:nq, :], eq[i][:, 0:nq, :], BIG, mq[i][:, 0:nq].unsqueeze(2).to_broadcast([128, nq, 32]),
                      ALU.mult, ALU.add, r=[eq[i], mq[i]], w=[nb[i]])
                for j0 in range(0, nq, 8):
                    nj = min(8, nq - j0)
                    for j in range(j0, j0 + nj):
                        P.tr(pTb[i][64:96, (j - j0) * 128:(j - j0 + 1) * 128], nb[i][:, j, :], ident[:], r=[nb[i], ident], w=[pTb[i]])
                    P.copy(QN[i][64:96, (q0 + j0) * 128:(q0 + j0 + nj) * 128], pTb[i][64:96, 0:nj * 128], r=[pTb[i]], w=[QN[i]],
                           eng="act")
        its = [(c, n) for c in range(NT) for n in range(0, 4 * c + 4)]
        N = len(its)
        zslot = {}

        def mA(k):
            c, n = its[k]
            qs = slice(c * 512, (c + 1) * 512)
            ks = slice(n * 128, (n + 1) * 128)
            diag = n >= 4 * c
            zslot[k] = zc[0] % 2
            zc[0] += 1
            for i in range(2):
                Z = Zp[i][zslot[k]]
                P.mm(Z[:], KE[i][:, ks], QN[i][:, qs], start=True, stop=(not diag), r=[KE[i], QN[i]], w=[Z])
                if diag:
                    P.mm(Z[:], ident[:], CM[n - 4 * c][:], start=False, stop=True, r=[ident, CM[n - 4 * c]], w=[Z])

        def mB(k):
            for i in range(2):
                P.act(Pt[i][k % 2][:], Zp[i][zslot[k]][:], AF.Exp, r=[Zp[i][zslot[k]]], w=[Pt[i][k % 2]])

        def mC(k):
            c, n = its[k]
            for i in range(2):
                h = 2 * hp + i
                P.mm(OT[i][c % 2][0:65, :], Vaug[:, n, h, :], Pt[i][k % 2][:], start=(n == 0), stop=(n == 4 * c + 3),
                     r=[Vaug, Pt[i][k % 2]], w=[OT[i][c % 2]])

        def mFin(c):
            qs = slice(c * 512, (c + 1) * 512)
            for i in range(2):
                h = 2 * hp + i
                O_ = OT[i][c % 2]
                P.op("dve", (lambda o_, i_: (lambda e: e.reciprocal(out=o_, in_=i_)))(RL[i][64:65, :], O_[64:65, :]),
                     r=[O_], w=[RL[i]])
                Zf = Zp[i][zfree[0]]
                P.mm(Zf[0:64, :], ones32[64:65, 0:64], RL[i][64:65, :], r=[ones32, RL[i]], w=[Zf])
                P.copy(bcs[i][:], Zf[0:64, :], r=[Zf], w=[bcs[i]], eng="act")
                y = yo[yc[0] % 4]
                yc[0] += 1
                P.tt(y[:], O_[0:64, :], bcs[i][:], ALU.mult, r=[O_, bcs[i]], w=[y])
                P.dma(G.yT[3, h * 64:(h + 1) * 64, qs], y[:], r=[y], eng="pool")

        zfree = [0]
        mA(0)
        pend = None
        for k in range(N):
            if k + 1 < N:
                mA(k + 1)
            mB(k)
            zfree[0] = zslot[k]
            if pend is not None:
                mFin(pend)
                pend = None
            mC(k)
            c, n = its[k]
            if n == 4 * c + 3:
                pend = c
        mFin(pend)
    P.finalize()


def c1_weight_loads(G, l, Wg, Wb, Wo):
    fns = []
    for kc in range(8):
        fns.append((lambda kc_: (lambda P: load_w(P, Wg, Wg[:, kc_, :], G.w_in[l, kc_ * 128:(kc_ + 1) * 128, 3076:3076 + 4096])))(kc))
    for br in range(4):
        for k2 in range(2):
            fns.append((lambda b_, k_: (lambda P: load_w(P, Wb, Wb[:, b_, k_, :], G.w_branch[l, b_, k_ * 128:(k_ + 1) * 128, :])))(br, k2))
    for kc in range(8):
        fns.append((lambda kc_: (lambda P: load_w(P, Wo, Wo[:, kc_, :], G.w_o[l, kc_ * 128:(kc_ + 1) * 128, :])))(kc))
    return fns


def c2_block_loads(G, l, q, Wq):
    SW = 1408
    fns = []
    for kc in range(8):
        fns.append((lambda kc_: (lambda P: load_w(P, Wq, Wq[:, kc_, :],
                                                  G.w_gate_up[l, kc_ * 128:(kc_ + 1) * 128, q * SW:(q + 1) * SW])))(kc))
    return fns


def ph_c1(nc, G, l, W=None):
    S = G.S
    P = Prog(nc, G.pool)
    TT = 256
    J = 2
    ident, _ = mk_ident(P)
    K = norm_alloc(P, J)
    if W is not None:
        Wg, Wb, Wo = W
    else:
        Wg = P.sb("Wg", [128, 8, 4096], BF16)
        Wb = P.sb("Wb", [128, 4, 2, D], BF16)
        Wo = P.sb("Wo", [128, 8, D], BF16)
        for fn in c1_weight_loads(G, l, Wg, Wb, Wo):
            fn(P)
    gb = load_gain(P, "gb", G.norm2_g[l])
    hts = [P.sb("ht%d" % i, [128, 8, TT], BF16) for i in range(2)]
    yts = [P.sb("yt%d" % i, [128, 4, 2, TT], BF16) for i in range(2)]
    xos = [P.sb("xo%d" % i, [128, J, D], F32) for i in range(2)]
    h2s = [P.sb("h2s%d" % i, [128, 8, TT], BF16) for i in range(2)]
    mT = P.sb("mT", [128, 8, TT], BF16)
    sg = [P.sb("sg%d" % i, [128, TT], F32) for i in range(2)]
    tmp = [P.sb("tmp%d" % i, [128, TT], F32) for i in range(2)]
    mg = [P.sb("mg%d" % i, [128, TT], F32) for i in range(2)]
    Gp = [P.ps("Gp%d" % i, [128, TT]) for i in range(2)]
    Pj = [P.ps("Pj%d" % i, [128, TT]) for i in range(2)]
    Op = [P.ps("Op%d" % i, [128, 512]) for i in range(2)]
    xsrc = G.x_in if l == 0 else G.xa
    hTv = rep(G.hT, "(k p) s -> p k s", p=128)
    h2v = rep(G.h2T, "(k p) s -> p k s", p=128)
    cnt = 0
    oc = 0
    pend = None
    for t in range(S // TT):
        t0 = t * TT
        ht = hts[t % 2]
        yt = yts[t % 2]
        xo = xos[t % 2]
        P.dma(ht[:], hTv[:, :, t0:t0 + TT], w=[ht])
        for br in range(4):
            P.dma(yt[:, br, :, :], rep(G.yT[br], "(k p) s -> p k s", p=128)[:, :, t0:t0 + TT], w=[yt])
        P.dma(xo[:], rep(xsrc[t0:t0 + TT, :], "(j p) d -> p j d", p=128), w=[xo])
        for dmc in range(8):
            m = mg[dmc % 2]
            for br in range(4):
                gp = Gp[cnt % 2]
                pj = Pj[cnt % 2]
                s_ = sg[cnt % 2]
                tm = tmp[cnt % 2]
                cnt += 1
                cg = br * 1024 + dmc * 128
                for kc in range(8):
                    P.mm(gp[:], Wg[:, kc, cg:cg + 128], ht[:, kc, :], start=(kc == 0), stop=(kc == 7), r=[Wg, ht], w=[gp])
                for k2 in range(2):
                    P.mm(pj[:], Wb[:, br, k2, dmc * 128:(dmc + 1) * 128], yt[:, br, k2, :], start=(k2 == 0), stop=(k2 == 1),
                         r=[Wb, yt], w=[pj])
                P.act(s_[:], gp[:], AF.Sigmoid, r=[gp], w=[s_])
                if br == 0:
                    P.tt(m[:], s_[:], pj[:], ALU.mult, r=[s_, pj], w=[m])
                else:
                    P.tt(tm[:], s_[:], pj[:], ALU.mult, r=[s_, pj], w=[tm])
                    if br < 3:
                        P.tt(m[:], m[:], tm[:], ALU.add, r=[m, tm], w=[m])
                    else:
                        P.tt(mT[:, dmc, :], m[:], tm[:], ALU.add, r=[m, tm], w=[mT])
        if pend is not None:
            h2 = h2s[pend[0] % 2]
            norm_p2(P, K, ident, h2)
            P.dma(h2v[:, :, pend[1]:pend[1] + TT], h2[:], r=[h2], eng="pool")
            pend = None
        for j in range(J):
            for hf in range(2):
                op_ = Op[oc % 2]
                oc += 1
                for dmc in range(8):
                    P.mm(op_[:], mT[:, dmc, j * 128:(j + 1) * 128], Wo[:, dmc, hf * 512:(hf + 1) * 512], start=(dmc == 0),
                         stop=(dmc == 7), r=[mT, Wo], w=[op_])
                P.tt(xo[:, j, hf * 512:(hf + 1) * 512], xo[:, j, hf * 512:(hf + 1) * 512], op_[:], ALU.add, r=[xo, op_], w=[xo])
        P.dma(rep(G.xm[t0:t0 + TT, :], "(j p) d -> p j d", p=128), xo[:], r=[xo], eng="pool")
        norm_p1(P, K, xo, gb)
        pend = (t, t0)
    if pend is not None:
        h2 = h2s[pend[0] % 2]
        norm_p2(P, K, ident, h2)
        P.dma(h2v[:, :, pend[1]:pend[1] + TT], h2[:], r=[h2], eng="pool")
    P.finalize()


def ph_c2(nc, G, l, pre=None):
    S = G.S
    P = Prog(nc, G.pool)
    TT = 256
    J = 2
    NF = FF // 128
    last = (l == G.L - 1)
    ident, _ = mk_ident(P)
    K = norm_alloc(P, J)
    SW = 1408
    Wgu = [None] * 4
    for q in range(4):
        if pre is not None and q in pre:
            Wgu[q] = pre[q]
        else:
            Wgu[q] = P.sb("Wgu%d" % q, [128, 8, SW], BF16)
    Wd = P.sb("Wd", [128, NF, D], BF16)
    for q in (0, 2, 1, 3):
        if pre is not None and q in pre:
            continue
        for fn in c2_block_loads(G, l, q, Wgu[q]):
            fn(P)
    for fc in range(NF):
        load_w(P, Wd, Wd[:, fc, :], G.w_down[l, fc * 128:(fc + 1) * 128, :])
    if not last:
        gb = load_gain(P, "gb", G.norm1_g[l + 1])
    if last:
        fgb = P.sb("fgb", [128, D], F32)
        P.dma(fgb[:], G.final_g.partition_broadcast(128), w=[fgb])
    h2s = [P.sb("h2s%d" % i, [128, 8, TT], BF16) for i in range(2)]
    xos = [P.sb("xo%d" % i, [128, J, D], F32) for i in range(2)]
    aT = P.sb("aT", [128, NF, TT], BF16)
    hto = P.sb("hto", [128, 8, TT], BF16)
    sg = [P.sb("sg%d" % i, [128, TT], F32) for i in range(2)]
    Gp = [P.ps("Gp%d" % i, [128, TT]) for i in range(2)]
    Up = [P.ps("Up%d" % i, [128, TT]) for i in range(2)]
    Op = [P.ps("Op%d" % i, [128, 512]) for i in range(2)]
    hTv = rep(G.hT, "(k p) s -> p k s", p=128)
    h2v = rep(G.h2T, "(k p) s -> p k s", p=128)
    cnt = 0
    oc = 0
    pend = None
    for t in range(S // TT):
        t0 = t * TT
        h2 = h2s[t % 2]
        xo = xos[t % 2]
        P.dma(h2[:], h2v[:, :, t0:t0 + TT], w=[h2])
        P.dma(xo[:], rep(G.xm[t0:t0 + TT, :], "(j p) d -> p j d", p=128), w=[xo])
        for fc in range(NF):
            gp = Gp[cnt % 2]
            up = Up[cnt % 2]
            s_ = sg[cnt % 2]
            cnt += 1
            wq = Wgu[fc // 11]
            wu = Wgu[2 + fc // 11]
            fo = (fc % 11) * 128
            for kc in range(8):
                P.mm(gp[:], wq[:, kc, fo:fo + 128], h2[:, kc, :], start=(kc == 0), stop=(kc == 7), r=[wq, h2], w=[gp])
            for kc in range(8):
                P.mm(up[:], wu[:, kc, fo:fo + 128], h2[:, kc, :], start=(kc == 0), stop=(kc == 7), r=[wu, h2], w=[up])
            P.act(s_[:], gp[:], AF.Silu, r=[gp], w=[s_])
            P.tt(aT[:, fc, :], s_[:], up[:], ALU.mult, r=[s_, up], w=[aT])
        if pend is not None:
            norm_p2(P, K, ident, hto)
            P.dma(hTv[:, :, pend:pend + TT], hto[:], r=[hto], eng="pool")
            pend = None
        for j in range(J):
            for hf in range(2):
                op_ = Op[oc % 2]
                oc += 1
                for fc in range(NF):
                    P.mm(op_[:], aT[:, fc, j * 128:(j + 1) * 128], Wd[:, fc, hf * 512:(hf + 1) * 512], start=(fc == 0),
                         stop=(fc == NF - 1), r=[aT, Wd], w=[op_])
                P.tt(xo[:, j, hf * 512:(hf + 1) * 512], xo[:, j, hf * 512:(hf + 1) * 512], op_[:], ALU.add, r=[xo, op_], w=[xo])
        if not last:
            P.dma(rep(G.xa[t0:t0 + TT, :], "(j p) d -> p j d", p=128), xo[:], r=[xo], eng="pool")
            norm_p1(P, K, xo, gb)
            pend = t0
        else:
            norm_stats(P, K, xo, J)
            for j in range(J):
                P.stt(xo[:, j, :], xo[:, j, :], K.rstd[:, j:j + 1], fgb[:], ALU.mult, ALU.mult, r=[xo, K.rstd, fgb], w=[xo])
            P.dma(rep(G.out[t0:t0 + TT, :], "(j p) d -> p j d", p=128), xo[:], r=[xo], eng="pool")
    if pend is not None:
        norm_p2(P, K, ident, hto)
        P.dma(hTv[:, :, pend:pend + TT], hto[:], r=[hto], eng="pool")
    P.finalize()


W_SPECS = [
    ("norm1_g", lambda L: [L, D]), ("w_in", lambda L: [L, D, NIN]), ("conv_a_w", lambda L: [L, 3, 256]),
    ("ssm_conv_w", lambda L: [L, 4, 512]), ("ssm_conv_b", lambda L: [L, 512]), ("ssm_dt_bias", lambda L: [L, 4]),
    ("ssm_a_log", lambda L: [L, 4]), ("ssm_d", lambda L: [L, 4]), ("ssm_norm_g", lambda L: [L, 256]),
    ("w_branch", lambda L: [L, 4, 256, D]), ("w_o", lambda L: [L, D, D]), ("norm2_g", lambda L: [L, D]),
    ("w_gate_up", lambda L: [L, D, 2 * FF]), ("w_down", lambda L: [L, FF, D]), ("final_g", lambda L: [D]),
]


def build(S, L, dbg=(), phases=None):
    nc = bass.Bass("TRN2", target_bir_lowering=False)
    G = NS()
    G.S = S
    G.L = L
    G.pool = SemPool(nc)
    G.x_in = nc.dram_tensor("x", [S, D], F32, kind="ExternalInput").ap()
    for name, shp in W_SPECS:
        setattr(G, name, nc.dram_tensor(name, shp(L), F32, kind="ExternalInput").ap())
    G.out = nc.dram_tensor("out", [S, D], F32, kind="ExternalOutput").ap()

    def scr(name, shape, dt):
        kind = "ExternalOutput" if name in dbg else "Internal"
        t = nc.dram_tensor(name, list(shape), dt, kind=kind).ap()
        setattr(G, name, t)
        return t

    scr("hT", [D, S], BF16)
    scr("h2T", [D, S], BF16)
    scr("xa", [S, D], F32)
    scr("xm", [S, D], F32)
    scr("uaT", [768, S], BF16)
    scr("qbT", [256, S], BF16)
    scr("kbT", [256, S], BF16)
    scr("vb", [S, 256], BF16)
    scr("zT", [256, S], BF16)
    scr("xbcT", [512, S], BF16)
    scr("dt", [S, 4], F32)
    scr("qdT", [256, S], BF16)
    scr("kdT", [256, S], BF16)
    scr("vd", [S, 256], BF16)
    scr("yT", [4, 256, S], BF16)
    run = (lambda p: True) if phases is None else (lambda p: p in phases)
    if run("norm0"):
        ph_norm0(nc, G)
    for l in range(L):
        if run("proj"):
            ph_proj(nc, G, l)
        if run("conva"):
            ph_conva(nc, G, l)
        if run("sb"):
            ph_sb(nc, G, l)
        if run("moba"):
            ph_moba(nc, G, l)
        full = run("ssd") and run("c1") and run("c2")
        if not full:
            if run("ssd"):
                ph_ssd(nc, G, l)
            if run("c1"):
                ph_c1(nc, G, l)
            if run("c2"):
                ph_c2(nc, G, l)
            continue
        esB = ExitStack()
        esA = ExitStack()
        tag = "L%d_" % l
        pre = {q: T(esB.enter_context(nc.sbuf_tensor(tag + "Wgu%d" % q, [128, 8, 1408], BF16)), "Wgu%d" % q) for q in (0, 2)}
        Wg = T(esA.enter_context(nc.sbuf_tensor(tag + "Wg", [128, 8, 4096], BF16)), "Wg")
        Wb = T(esA.enter_context(nc.sbuf_tensor(tag + "Wb", [128, 4, 2, D], BF16)), "Wb")
        Wo = T(esA.enter_context(nc.sbuf_tensor(tag + "Wo", [128, 8, D], BF16)), "Wo")
        pending = c1_weight_loads(G, l, Wg, Wb, Wo) + c2_block_loads(G, l, 0, pre[0]) + c2_block_loads(G, l, 2, pre[2])

        def hook(P, n):
            for _ in range(min(n, len(pending))):
                pending.pop(0)(P)

        ph_ssd(nc, G, l, hook=hook)
        ph_c1(nc, G, l, W=(Wg, Wb, Wo))
        esA.close()
        ph_c2(nc, G, l, pre=pre)
        esB.close()
    G.pool.es.close()
    return nc


from concourse.bass_utils import run_bass_kernel_spmd

_W_NAMES = [n for n, _ in W_SPECS]


def kernel(**inputs):
    x = np.ascontiguousarray(np.asarray(inputs["x"], dtype=np.float32))
    B, S, _ = x.shape
    L = int(np.asarray(inputs["w_in"]).shape[0])
    assert B == 8
    nc = build(S, L)
    w = {n: np.ascontiguousarray(np.asarray(inputs[n], dtype=np.float32)) for n in _W_NAMES}
    in_maps = []
    for b in range(B):
        m = {"x": x[b]}
        m.update(w)
        in_maps.append(m)
    res = run_bass_kernel_spmd(nc, in_maps, core_ids=list(range(B)))
    return np.stack([np.asarray(res.results[b]["out"], dtype=np.float32) for b in range(B)], axis=0)
```
